# Optimizing a Trainium2 kernel written in Bass

```python
import math
import jax, jax.numpy as jnp
from jax import lax
import numpy as np

D_MODEL = 2048
BATCH = 1
SEQ = 8192
DEPTH = 1

CHUNK = 64
Q_BLOCK = 128
EPS = 1e-6
SB_HEADS = 8
SB_HEAD_DIM = 128
SB_WIDTH = SB_HEADS * SB_HEAD_DIM
MLA_HEADS = 8
MLA_NOPE_DIM = 128
MLA_ROPE_DIM = 64
MLA_QK_DIM = MLA_NOPE_DIM + MLA_ROPE_DIM
MLA_V_DIM = 128
MLA_WIDTH = MLA_HEADS * MLA_V_DIM
Q_LORA_RANK = 512
KV_LORA_RANK = 256
ROPE_THETA = 10000.0
D_MIX = SB_WIDTH + MLA_WIDTH
IN_SPLITS = (SB_WIDTH, SB_WIDTH, SB_WIDTH, SB_WIDTH,
             Q_LORA_RANK, KV_LORA_RANK, MLA_ROPE_DIM, MLA_WIDTH)
D_IN = sum(IN_SPLITS)

kernel_name = 'hybrid_stickbreak_mla_block'


def rms_norm(x, w):
    xf = x.astype(jnp.float32)
    y = xf * lax.rsqrt(jnp.mean(xf * xf, axis=-1, keepdims=True) + EPS)
    return (y * w.astype(jnp.float32)).astype(x.dtype)


def rope_tables(positions):
    inv_freq = ROPE_THETA ** (-jnp.arange(0, MLA_ROPE_DIM, 2, dtype=jnp.float32) / MLA_ROPE_DIM)
    ang = positions.astype(jnp.float32)[..., None] * inv_freq
    return jnp.cos(ang), jnp.sin(ang)


def apply_rope(x, cos, sin):
    x1, x2 = jnp.split(x.astype(jnp.float32), 2, axis=-1)
    out = jnp.concatenate([x1 * cos - x2 * sin, x2 * cos + x1 * sin], axis=-1)
    return out.astype(x.dtype)


def to_heads(t, n_heads):
    b, s, _ = t.shape
    return t.reshape(b, s, n_heads, -1).transpose(0, 2, 1, 3)


def from_heads(t):
    b, h, s, d = t.shape
    return t.transpose(0, 2, 1, 3).reshape(b, s, h * d)


def stick_breaking_attention(q, k, v):
    seq, d = q.shape[2], q.shape[3]
    scale = 1.0 / math.sqrt(d)
    outs = []
    for b0 in range(0, seq, Q_BLOCK):
        kl = b0 + Q_BLOCK
        z = jnp.einsum('bhqd,bhkd->bhqk', q[:, :, b0:kl], k[:, :, :kl]).astype(jnp.float32) * scale
        t_idx = b0 + jnp.arange(Q_BLOCK)[:, None]
        s_idx = jnp.arange(kl)[None, :]
        before = s_idx < t_idx
        log_keep = jnp.where(before, jax.nn.log_sigmoid(-z), 0.0)
        later = lax.cumsum(log_keep, axis=3, reverse=True) - log_keep
        a = jnp.where(before, jnp.exp(jax.nn.log_sigmoid(z) + later), 0.0)
        outs.append(jnp.einsum('bhqk,bhkd->bhqd', a.astype(v.dtype), v[:, :, :kl]))
    return jnp.concatenate(outs, axis=2)


def chunk_causal_softmax_attention(q, k, v):
    seq, d = q.shape[2], q.shape[3]
    scale = 1.0 / math.sqrt(d)
    outs = []
    for b0 in range(0, seq, Q_BLOCK):
        kl = b0 + Q_BLOCK
        s_ = jnp.einsum('bhqd,bhkd->bhqk', q[:, :, b0:kl], k[:, :, :kl]).astype(jnp.float32) * scale
        t_chunk = (b0 + jnp.arange(Q_BLOCK))[:, None] // CHUNK
        s_chunk = jnp.arange(kl)[None, :] // CHUNK
        s_ = jnp.where(s_chunk <= t_chunk, s_, -jnp.inf)
        p = jax.nn.softmax(s_, axis=-1)
        outs.append(jnp.einsum('bhqk,bhkd->bhqd', p.astype(v.dtype), v[:, :, :kl]))
    return jnp.concatenate(outs, axis=2)


def hybrid_layer(x, cos, sin, pre_norm_w, w_in, q_norm_w, w_q_up, kv_norm_w, w_kv_up, w_out, post_norm_w):
    b, s, _ = x.shape
    h = rms_norm(x, pre_norm_w)
    proj = h @ w_in
    split_pts = tuple(int(i) for i in np.cumsum(IN_SPLITS)[:-1])
    sb_q, sb_k, sb_v, sb_gate, c_q, c_kv, k_rope, mla_gate = jnp.split(proj, split_pts, axis=-1)

    o_a = from_heads(stick_breaking_attention(to_heads(sb_q, SB_HEADS),
                                              to_heads(sb_k, SB_HEADS),
                                              to_heads(sb_v, SB_HEADS)))

    q_full = (rms_norm(c_q, q_norm_w) @ w_q_up).reshape(b, s, MLA_HEADS, MLA_QK_DIM)
    q_nope, q_rot = jnp.split(q_full, [MLA_NOPE_DIM], axis=-1)
    q_rot = apply_rope(q_rot, cos[:, :, None, :], sin[:, :, None, :])
    kv = (rms_norm(c_kv, kv_norm_w) @ w_kv_up).reshape(b, s, MLA_HEADS, MLA_NOPE_DIM + MLA_V_DIM)
    k_nope, v_mla = jnp.split(kv, [MLA_NOPE_DIM], axis=-1)
    k_rot = apply_rope(k_rope, cos, sin)
    k_rot = jnp.broadcast_to(k_rot[:, :, None, :], (b, s, MLA_HEADS, MLA_ROPE_DIM))
    q_mla = jnp.concatenate([q_nope, q_rot], axis=-1).transpose(0, 2, 1, 3)
    k_mla = jnp.concatenate([k_nope, k_rot], axis=-1).transpose(0, 2, 1, 3)
    o_b = from_heads(chunk_causal_softmax_attention(q_mla, k_mla, v_mla.transpose(0, 2, 1, 3)))

    mixed = jnp.concatenate([o_a * jax.nn.silu(sb_gate), o_b * jax.nn.silu(mla_gate)], axis=-1)
    y = mixed @ w_out
    return x + rms_norm(y, post_norm_w)


def setup_inputs(seed: int = 0) -> dict:
    key = jax.random.key(seed)
    ks = jax.random.split(key, 12)
    f32 = jnp.float32
    x = jax.random.normal(ks[0], (BATCH, SEQ, D_MODEL), f32)
    positions = jnp.broadcast_to(jnp.arange(SEQ, dtype=jnp.int32)[None, :], (BATCH, SEQ))
    pre_norm_w = 1.0 + 0.05 * jax.random.normal(ks[1], (DEPTH, D_MODEL), f32)
    w_in = jax.random.normal(ks[2], (DEPTH, D_MODEL, D_IN), f32) * D_MODEL ** -0.5
    q_norm_w = 1.0 + 0.05 * jax.random.normal(ks[3], (DEPTH, Q_LORA_RANK), f32)
    w_q_up = jax.random.normal(ks[4], (DEPTH, Q_LORA_RANK, MLA_HEADS * MLA_QK_DIM), f32) * Q_LORA_RANK ** -0.5
    kv_norm_w = 1.0 + 0.05 * jax.random.normal(ks[5], (DEPTH, KV_LORA_RANK), f32)
    w_kv_up = jax.random.normal(ks[6], (DEPTH, KV_LORA_RANK, MLA_HEADS * (MLA_NOPE_DIM + MLA_V_DIM)), f32) * KV_LORA_RANK ** -0.5
    w_out = jax.random.normal(ks[7], (DEPTH, D_MIX, D_MODEL), f32) * D_MIX ** -0.5
    post_norm_w = 1.0 + 0.05 * jax.random.normal(ks[8], (DEPTH, D_MODEL), f32)
    return {'x': x, 'positions': positions, 'pre_norm_w': pre_norm_w, 'w_in': w_in,
            'q_norm_w': q_norm_w, 'w_q_up': w_q_up, 'kv_norm_w': kv_norm_w, 'w_kv_up': w_kv_up,
            'w_out': w_out, 'post_norm_w': post_norm_w}


def reference(x, positions, pre_norm_w, w_in, q_norm_w, w_q_up, kv_norm_w, w_kv_up, w_out, post_norm_w):
    cos, sin = rope_tables(positions)
    for i in range(DEPTH):
        x = hybrid_layer(x, cos, sin, pre_norm_w[i], w_in[i], q_norm_w[i], w_q_up[i],
                         kv_norm_w[i], w_kv_up[i], w_out[i], post_norm_w[i])
    return x
```

```python
import math
from contextlib import ExitStack

import numpy as np
import concourse.bass as bass
import concourse.mybir as mybir
from concourse.bass_utils import run_bass_kernel_spmd

F32 = mybir.dt.float32
BF16 = mybir.dt.bfloat16
I32 = mybir.dt.int32
AF = mybir.ActivationFunctionType
ALU = mybir.AluOpType

NCORES = 8
D = 2048
SEQ = 8192
TL = 1024
NB = 8
DIN = 5952
EPS = 1e-6
NEG = -30000.0
SC_SB = 1.0 / math.sqrt(128.0)
SC_MLA = 1.0 / math.sqrt(192.0)
SB_ROWS = 2048
MLA_ROWS = 2112

DEBUG = False


class _Stop(Exception):
    pass


class Builder:
    def __init__(self, nc, stack):
        self.nc = nc
        self.stack = stack
        self.q = {k: [] for k in ("pe", "act", "dve", "pool", "sp")}
        self.waited = {k: {} for k in self.q}
        self.prog = {k: self.new_sem("prog_" + k) for k in ("pe", "act", "dve", "pool")}
        self.nsem = 0

    def new_sem(self, name):
        h = self.stack.enter_context(self.nc.semaphore(name))
        return [h, 0]

    def op(self, eng, fn, waits=(), sem=None, amt=1, inc=True):
        ws = []
        mx = {}
        for t in waits:
            if t is None:
                continue
            s, v = t
            if id(s) not in mx or mx[id(s)][1] < v:
                mx[id(s)] = (s, v)
        for key, (s, v) in mx.items():
            if self.waited[eng].get(key, 0) >= v:
                continue
            self.waited[eng][key] = v
            ws.append((s[0], v))
        tok = None
        incspec = None
        if inc:
            s = sem if sem is not None else self.prog[eng]
            s[1] += amt
            tok = (s, s[1])
            incspec = (s[0], amt)
        self.q[eng].append((fn, ws, incspec))
        return tok

    def replay(self, eng_obj, key):
        for fn, ws, incspec in self.q[key]:
            for h, v in ws:
                eng_obj.wait_ge(h, v)
            ins = fn(eng_obj)
            if incspec is not None:
                ins.then_inc(incspec[0], incspec[1])


def I(name, *args, **kw):
    return lambda e: getattr(e, name)(*args, **kw)


class Banks:
    def __init__(self, aps):
        self.aps = aps
        self.free = [None] * len(aps)
        self.i = 0

    def get(self):
        i = self.i
        self.i = (self.i + 1) % len(self.aps)
        return i, self.aps[i], self.free[i]

    def release(self, i, tok):
        self.free[i] = tok


class Arena:
    def __init__(self, t, nbytes):
        self.t = t
        self.nbytes = nbytes
        self.off = 0

    def alloc(self, shape, dt, parts=128):
        esz = 2 if dt == BF16 else 4
        n = 1
        for s in shape:
            n *= s
        nb = n * esz
        self.off = (self.off + 63) // 64 * 64
        assert self.off + nb <= self.nbytes, ("SBUF arena overflow", self.off, nb)
        ap = self.t[0:parts, self.off // 2:(self.off + nb) // 2]
        self.off += nb
        if esz == 4:
            ap = ap.bitcast(dt)
        if len(shape) == 2:
            ap = ap.rearrange("p (a b) -> p a b", a=shape[0])
        elif len(shape) == 3:
            ap = ap.rearrange("p (a b c) -> p a b c", a=shape[0], b=shape[1])
        return ap

    def mark(self):
        return self.off

    def reset(self, m):
        self.off = m


def build_program(debug=False, stop_after=9):
    nc = bass.Bass("TRN2", target_bir_lowering=False)

    def din(name, shape, dt=F32):
        return nc.dram_tensor(name, shape, dt, kind="ExternalInput").ap()

    x_d = din("x", [TL, D])
    posb_d = din("posb", [64, TL], I32)
    win_d = din("w_in", [D, DIN])
    wkrs_d = din("w_krs", [D, 64])
    wqn_d = din("w_qn", [512, 1024])
    wqr_d = din("w_qr", [512, 512])
    wqrs_d = din("w_qrs", [512, 512])
    wkn_d = din("w_kn", [256, 1024])
    wkvv_d = din("w_kvv", [256, 1024])
    wout_d = din("w_out", [D, D])
    pnwb_d = din("pnw_b", [128, D])
    ponwb_d = din("ponw_b", [128, D])
    csts_d = din("cst_s", [128, 8])
    cstbf_d = din("cst_bf", [128, 2560])
    out_d = nc.dram_tensor("out", [TL, D], F32, kind="ExternalOutput").ap()

    xT_d = din("xT", [D, SEQ])
    posa_d = din("posa", [64, SEQ], I32)
    pnwc_d = din("pnw_c", [128, 16])
    kt_sb_d = nc.dram_tensor("kt_sb", [1024, SEQ], BF16).ap()
    v_sb_d = nc.dram_tensor("v_sb", [1024, SEQ], BF16).ap()
    kt_ml_d = nc.dram_tensor("kt_ml", [1024, SEQ], BF16).ap()
    v_ml_d = nc.dram_tensor("v_ml", [1024, SEQ], BF16).ap()
    kr_d = nc.dram_tensor("kr", [64, SEQ], BF16).ap()

    dbg = {}
    if debug:
        def dout(name, shape, dt=F32):
            dbg[name] = nc.dram_tensor(name, shape, dt, kind="ExternalOutput").ap()
        dout("d_hT", [128, 16 * TL], BF16)
        dout("d_qsb", [128, 8 * TL], BF16)
        dout("d_qn", [128, 8 * TL], BF16)
        dout("d_qr", [64, 8 * TL], BF16)
        dout("d_gate", [128, 16 * TL], BF16)
        dout("d_tab", [64, 4 * TL], F32)
        dout("d_k0", [128, 8 * TL], BF16)
        dout("d_v0", [128, 8 * TL], BF16)
        dout("d_kr", [64, 8 * TL], BF16)
        dout("d_cqn", [128, 4 * TL], BF16)
        dout("d_ckvn", [128, 2 * TL], BF16)
        dout("d_kn0", [128, 8 * TL], BF16)
        dout("d_vm0", [128, 8 * TL], BF16)
        dout("d_mixed", [128, 16 * TL], BF16)

    ARENA_BYTES = 200 * 1024
    with ExitStack() as st:
        B = Builder(nc, st)
        big = st.enter_context(nc.sbuf_tensor("arena", [128, ARENA_BYTES // 2], BF16))
        A = Arena(big, ARENA_BYTES)

        cbf = A.alloc([2560], BF16)
        ident = cbf[:, 0:128]
        negtri = cbf[:, 128:256]
        negones = cbf[:, 256:384]
        ones = cbf[:, 384:512]
        msk_sb = cbf[:, 512:1536]
        msk_mla = cbf[:, 1536:2560]
        csts = A.alloc([8], F32)
        small = A.alloc([64], F32)
        small2 = A.alloc([64], F32)
        pscr = A.alloc([8], F32)
        pnwc = A.alloc([16], F32)
        m_const = A.mark()
        qT_sb = A.alloc([8, TL], BF16)
        qN = A.alloc([8, TL], BF16)
        qR = A.alloc([8, TL], BF16, parts=64)
        gate = A.alloc([16, TL], BF16)
        m_persist = A.mark()

        psum = [st.enter_context(nc.psum_tensor("ps%d" % i, [128, 512], F32)) for i in range(8)]
        PS = [p[:, :] for p in psum]

        def now(eng):
            return (B.prog[eng], B.prog[eng][1])

        def all_now():
            return [now(k) for k in ("pe", "act", "dve", "pool") if B.prog[k][1] > 0]


        def rsqrt_chain(dst, src_ap, scale, waits):
            ta = B.op("dve", I("tensor_scalar", out=dst, in0=src_ap, scalar1=scale, scalar2=EPS,
                                                        op0=ALU.mult, op1=ALU.add), waits=waits)
            tb = B.op("act", I("sqrt", out=dst, in_=dst), waits=[ta])
            tc = B.op("dve", I("reciprocal", out=dst, in_=dst), waits=[tb])
            return ta, tc

        S_c = B.new_sem("s_cst")
        B.op("pool", I("dma_start", out=cbf, in_=cstbf_d[:, :]), sem=S_c, amt=16)
        B.op("sp", I("dma_start", out=csts, in_=csts_d[:, :]), sem=S_c, amt=16)
        t_cst = (S_c, 32)
        t_cbf = t_cst
        t_z = B.op("dve", I("memset", small, 0.0))
        t_z = B.op("dve", I("memset", small2, 0.0))

        dbg_toks = []
        S_dbg = B.new_sem("s_dbg")

        def dump(name, src_ap, waits):
            dbg_toks.append(B.op("sp", I("dma_start", out=dbg[name], in_=src_ap), waits=waits,
                                 sem=S_dbg, amt=16))

        def maybe_stop(level):
            if stop_after <= level:
                raise _Stop()

        out_toks = []
        try:

            A.reset(m_const)
            wk = A.alloc([16, 1024], BF16)
            wvv = A.alloc([16, 1024], BF16)
            wc = A.alloc([16, 384], BF16)
            wkn_a = A.alloc([2, 1024], BF16)
            wkvv_a = A.alloc([2, 1024], BF16)
            xa = [A.alloc([16, 512], BF16) for _ in range(2)]
            hTa = [A.alloc([16, 512], BF16) for _ in range(2)]
            sqa = A.alloc([8, 512], BF16)
            rstdb_a = A.alloc([512], F32)
            rcol = A.alloc([8], F32)
            ckvf = A.alloc([2, 512], F32)
            sq2 = [A.alloc([512], BF16) for _ in range(2)]
            rstdkv = A.alloc([512], F32)
            ckvn_a = A.alloc([2, 512], BF16)
            kstA = [A.alloc([512], BF16) for _ in range(2)]
            vstA = [A.alloc([512], BF16) for _ in range(2)]
            posi_a = A.alloc([512], I32, parts=64)
            posf_a = A.alloc([512], F32, parts=64)
            ang_a = A.alloc([512], F32, parts=64)
            ry_a = A.alloc([512], F32, parts=64)
            rk_a = A.alloc([512], F32, parts=64)
            cos_a = A.alloc([512], F32, parts=64)
            sin_a = A.alloc([512], F32, parts=64)
            rt1a = A.alloc([512], F32, parts=64)
            rt2a = A.alloc([512], F32, parts=64)
            krot_a = A.alloc([512], BF16, parts=64)

            def wsrc(ap_):
                return ap_.rearrange("(c p) n -> p c n", p=128)

            S_wa = B.new_sem("s_wa")
            B.op("sp", I("dma_start", out=pnwc, in_=pnwc_d[:, :]), sem=S_wa, amt=16)
            B.op("pool", I("dma_start", out=wk, in_=wsrc(win_d[:, 1024:2048])), sem=S_wa, amt=16)
            B.op("pool", I("dma_start", out=wc[:, :, 0:320], in_=wsrc(win_d[:, 4608:4928])), sem=S_wa, amt=16)
            B.op("pool", I("dma_start", out=wc[:, :, 320:384], in_=wsrc(wkrs_d[:, :])), sem=S_wa, amt=16)
            B.op("pool", I("dma_start", out=wvv, in_=wsrc(win_d[:, 2048:3072])), sem=S_wa, amt=16)
            B.op("pool", I("dma_start", out=wkn_a, in_=wsrc(wkn_d[:, :])), sem=S_wa, amt=16)
            B.op("pool", I("dma_start", out=wkvv_a, in_=wsrc(wkvv_d[:, :])), sem=S_wa, amt=16)
            t_wa = (S_wa, 7 * 16)

            banksA = Banks(PS)
            S_xa = [B.new_sem("s_xa%d" % i) for i in range(2)]
            S_pa = B.new_sem("s_posa")
            S_ka = [B.new_sem("s_ka%d" % i) for i in range(2)]
            S_va = [B.new_sem("s_va%d" % i) for i in range(2)]
            S_kra = B.new_sem("s_kra")
            xa_free = [[], []]
            hTa_free = [None, None]
            sqa_free = [None]
            posi_free = [None]
            kstA_free = [None, None]
            vstA_free = [None, None]
            krotA_free = [None]
            kA = [0]
            vA = [0]
            PI = math.pi
            INV2PI = 1.0 / (2.0 * PI)
            C1 = 6.28125
            C2 = 2.0 * PI - C1
            invf = csts[0:64, 6:7]
            sgn = csts[0:64, 7:8]
            NT = SEQ // 512

            def mmg(out_ap, lhs_fn, rhs_fn, nch, waits):
                tok = None
                for c in range(nch):
                    tok = B.op("pe", I("matmul", out_ap, lhsT=lhs_fn(c), rhs=rhs_fn(c),
                                       start=(c == 0), stop=(c == nch - 1)),
                               waits=waits if c == 0 else [], inc=(c == nch - 1))
                return tok

            def kst_out(bank, t_mm, mul_rstd, dst_ap, t_rb_):
                ks = kA[0] % 2
                kA[0] += 1
                if mul_rstd:
                    t_ev = B.op("dve", I("tensor_tensor", out=kstA[ks], in0=bank, in1=rstdb_a, op=ALU.mult),
                                waits=[t_mm, t_rb_, kstA_free[ks]])
                else:
                    t_ev = B.op("act", I("copy", out=kstA[ks], in_=bank), waits=[t_mm, kstA_free[ks]])
                t_d = B.op("sp", I("dma_start", out=dst_ap, in_=kstA[ks]), waits=[t_ev], sem=S_ka[ks], amt=16)
                kstA_free[ks] = t_d
                return t_ev

            def vst_out(bank, t_mm, scale_ap, dst_ap, t_rc_):
                vs = vA[0] % 2
                vA[0] += 1
                if scale_ap is not None:
                    t_ev = B.op("act", I("activation", out=vstA[vs], in_=bank, func=AF.Copy, scale=scale_ap),
                                waits=[t_mm, t_rc_, vstA_free[vs]])
                else:
                    t_ev = B.op("dve", I("tensor_copy", out=vstA[vs], in_=bank), waits=[t_mm, vstA_free[vs]])
                t_d = B.op("sp", I("dma_start", out=dst_ap, in_=vstA[vs].rearrange("s (h d) -> s h d", h=4)),
                           waits=[t_ev], sem=S_va[vs], amt=16)
                vstA_free[vs] = t_d
                return t_ev

            for T in range(NT):
                sl = T % 2
                t0 = T * 512
                t_xa = B.op("pool", I("dma_start", out=xa[sl], in_=xT_d[:, t0:t0 + 512].rearrange("(c p) t -> p c t", p=128)),
                            waits=xa_free[sl], sem=S_xa[sl], amt=16)
                t_pos = B.op("sp", I("dma_start", out=posi_a, in_=posa_d[:, t0:t0 + 512]), waits=[posi_free[0]],
                             sem=S_pa, amt=16)
                bss, SSb, fss = banksA.get()
                bcs, CSb, fcs = banksA.get()
                t_sq = None
                t_on = None
                for half in range(2):
                    t_sq = B.op("act", I("activation", out=sqa, in_=xa[sl][:, 8 * half:8 * half + 8, :], func=AF.Square),
                                waits=[t_xa, sqa_free[0]])
                    for c in range(8):
                        t_on = B.op("pe", I("matmul", SSb, lhsT=ones, rhs=sqa[:, c, :],
                                            start=(half == 0 and c == 0), stop=(half == 1 and c == 7)),
                                    waits=[t_sq, fss, t_cbf] if c == 0 else [], inc=(c == 7))
                    for tb in range(4):
                        for c in range(8):
                            t_cs = B.op("pe", I("matmul", CSb[:, 2 * tb + half:2 * tb + half + 1],
                                                lhsT=sqa[:, c, tb * 128:(tb + 1) * 128], rhs=ones[:, 0:1],
                                                start=(c == 0), stop=(c == 7)),
                                        waits=[fcs] if (c == 0 and tb == 0 and half == 0) else [], inc=(c == 7))
                    sqa_free[0] = t_cs
                t_ra, t_rb = rsqrt_chain(rstdb_a, SSb, 1.0 / D, [t_on, now("pe"), now("dve")])
                banksA.release(bss, t_ra)
                csv = CSb[:, 0:8].rearrange("p (t h) -> p t h", h=2)
                t_c1 = B.op("dve", I("tensor_reduce", out=rcol[:, 0:4], in_=csv, axis=mybir.AxisListType.X, op=ALU.add),
                            waits=[t_cs, now("act")])
                banksA.release(bcs, t_c1)
                t_c2, t_rc = rsqrt_chain(rcol[:, 4:8], rcol[:, 0:4], 1.0 / D, [t_c1])
                t_h = B.op("dve", I("tensor_tensor", out=hTa[sl], in0=xa[sl],
                                    in1=pnwc.unsqueeze(2).to_broadcast([128, 16, 512]), op=ALU.mult),
                           waits=[t_xa, hTa_free[sl], t_wa])
                xa_free[sl] = [t_h, t_sq]
                tq = B.op("dve", I("tensor_copy", out=posf_a, in_=posi_a), waits=[t_pos, now("act")])
                posi_free[0] = tq
                tq = B.op("dve", I("tensor_scalar", out=ang_a, in0=posf_a, scalar1=invf, scalar2=None, op0=ALU.mult),
                          waits=[tq, t_cst])
                ty = B.op("dve", I("tensor_scalar", out=ry_a, in0=ang_a, scalar1=INV2PI, scalar2=0.5,
                                   op0=ALU.mult, op1=ALU.add), waits=[tq])
                tk = B.op("dve", I("tensor_copy", out=posi_a, in_=ry_a), waits=[ty])
                tkf = B.op("dve", I("tensor_copy", out=rk_a, in_=posi_a), waits=[tk])
                posi_free[0] = tkf
                tg = B.op("dve", I("tensor_tensor", out=ry_a, in0=rk_a, in1=ry_a, op=ALU.is_gt), waits=[tkf])
                tm = B.op("dve", I("tensor_tensor", out=rk_a, in0=rk_a, in1=ry_a, op=ALU.subtract), waits=[tg])
                tr1 = B.op("dve", I("scalar_tensor_tensor", out=ang_a, in0=rk_a, scalar=-C1, in1=ang_a,
                                    op0=ALU.mult, op1=ALU.add), waits=[tm])
                tr2 = B.op("dve", I("scalar_tensor_tensor", out=ang_a, in0=rk_a, scalar=-C2, in1=ang_a,
                                    op0=ALU.mult, op1=ALU.add), waits=[tr1])
                tc1 = B.op("dve", I("tensor_scalar", out=ang_a, in0=ang_a, scalar1=PI, scalar2=-PI,
                                    op0=ALU.min, op1=ALU.max), waits=[tr2])
                ts3 = B.op("act", I("activation", out=sin_a, in_=ang_a, func=AF.Sin), waits=[tc1, now("dve")])
                ts4 = B.op("dve", I("tensor_scalar", out=sin_a, in0=sin_a, scalar1=sgn, scalar2=None, op0=ALU.mult),
                           waits=[ts3])
                t5 = B.op("dve", I("tensor_scalar", out=rk_a, in0=ang_a, scalar1=0.5 * PI, scalar2=None, op0=ALU.add),
                          waits=[ts3, tm])
                t5b = B.op("dve", I("tensor_single_scalar", out=ry_a, in_=rk_a, scalar=PI, op=ALU.is_gt), waits=[t5])
                t6 = B.op("dve", I("scalar_tensor_tensor", out=rk_a, in0=ry_a, scalar=-2.0 * PI, in1=rk_a,
                                   op0=ALU.mult, op1=ALU.add), waits=[t5b])
                t6b = B.op("dve", I("tensor_scalar", out=rk_a, in0=rk_a, scalar1=PI, scalar2=-PI,
                                    op0=ALU.min, op1=ALU.max), waits=[t6])
                ts7 = B.op("act", I("activation", out=cos_a, in_=rk_a, func=AF.Sin), waits=[t6b])
                t_mm = None
                for h in range(8):
                    bi, bank, bfree = banksA.get()
                    t_mm = mmg(bank, lambda c, h=h: wk[:, c, h * 128:(h + 1) * 128], lambda c: hTa[sl][:, c, :], 16,
                               [t_h, t_wa, bfree])
                    t_ev = kst_out(bank, t_mm, True, kt_sb_d[h * 128:(h + 1) * 128, t0:t0 + 512], t_rb)
                    banksA.release(bi, t_ev)
                lat = []
                for j in range(2):
                    bi, bank, bfree = banksA.get()
                    t_mm = mmg(bank, lambda c, j=j: wc[:, c, j * 128:(j + 1) * 128], lambda c: hTa[sl][:, c, :], 16,
                               [t_h, t_wa, bfree])
                    lat.append((bi, bank, t_mm))
                b1, bk1, f1 = banksA.get()
                t_m1 = mmg(bk1[0:64, :], lambda c: wc[:, c, 256:320], lambda c: hTa[sl][:, c, :], 16, [t_h, t_wa, f1])
                b2, bk2, f2 = banksA.get()
                t_m2 = mmg(bk2[0:64, :], lambda c: wc[:, c, 320:384], lambda c: hTa[sl][:, c, :], 16, [t_h, t_wa, f2])
                bs2, SS2, fs2 = banksA.get()
                t_o2 = None
                for j in range(2):
                    bi, bank, t_mm = lat[j]
                    t_f = B.op("dve", I("tensor_tensor", out=ckvf[:, j, :], in0=bank, in1=rstdb_a, op=ALU.mult),
                               waits=[t_mm, t_rb, now("pe")])
                    banksA.release(bi, t_f)
                    t_s2 = B.op("act", I("activation", out=sq2[j], in_=ckvf[:, j, :], func=AF.Square), waits=[t_f, now("pe")])
                    t_o2 = B.op("pe", I("matmul", SS2, lhsT=ones, rhs=sq2[j], start=(j == 0), stop=(j == 1)),
                                waits=[t_s2, fs2 if j == 0 else None])
                t_ka, t_kb = rsqrt_chain(rstdkv, SS2, 1.0 / 256.0, [t_o2, now("dve")])
                banksA.release(bs2, t_ka)
                t_n = None
                for j in range(2):
                    t_n = B.op("dve", I("scalar_tensor_tensor", out=ckvn_a[:, j, :], in0=ckvf[:, j, :],
                                        scalar=csts[:, 4 + j:5 + j], in1=rstdkv, op0=ALU.mult, op1=ALU.mult),
                               waits=[t_kb, now("pe"), t_cst])
                tr_1 = B.op("dve", I("tensor_tensor", out=rt1a, in0=bk1[0:64, :], in1=cos_a, op=ALU.mult),
                            waits=[t_m1, ts7])
                banksA.release(b1, tr_1)
                tr_2 = B.op("dve", I("tensor_tensor", out=rt2a, in0=bk2[0:64, :], in1=sin_a, op=ALU.mult),
                            waits=[t_m2, ts4])
                banksA.release(b2, tr_2)
                tr_3 = B.op("dve", I("tensor_tensor", out=rt1a, in0=rt1a, in1=rt2a, op=ALU.add), waits=[tr_1, tr_2])
                tr_4 = B.op("dve", I("tensor_tensor", out=krot_a, in0=rt1a, in1=rstdb_a[0:64, :], op=ALU.mult),
                            waits=[tr_3, t_rb, krotA_free[0]])
                krotA_free[0] = B.op("sp", I("dma_start", out=kr_d[:, t0:t0 + 512], in_=krot_a), waits=[tr_4],
                                     sem=S_kra, amt=16)
                for tb in range(4):
                    gb = 4 * T + tb
                    for half in range(2):
                        bi, bank, bfree = banksA.get()
                        t_mm = mmg(bank, lambda c, tb=tb: hTa[sl][:, c, tb * 128:(tb + 1) * 128],
                                   lambda c, half=half: wvv[:, c, half * 512:(half + 1) * 512], 16, [t_h, t_wa, bfree])
                        dview = v_sb_d[half * 512:(half + 1) * 512, gb * 128:(gb + 1) * 128].rearrange(
                            "(h s) d -> s h d", s=128)
                        t_ev = vst_out(bank, t_mm, rcol[:, 4 + tb:5 + tb], dview, t_rc)
                        banksA.release(bi, t_ev)
                hTa_free[sl] = t_mm
                for h in range(8):
                    bi, bank, bfree = banksA.get()
                    t_mm = mmg(bank, lambda c, h=h: wkn_a[:, c, h * 128:(h + 1) * 128], lambda c: ckvn_a[:, c, :], 2,
                               [t_n, t_wa, bfree])
                    t_ev = kst_out(bank, t_mm, False, kt_ml_d[h * 128:(h + 1) * 128, t0:t0 + 512], None)
                    banksA.release(bi, t_ev)
                for tb in range(4):
                    gb = 4 * T + tb
                    for half in range(2):
                        bi, bank, bfree = banksA.get()
                        t_mm = mmg(bank, lambda c, tb=tb: ckvn_a[:, c, tb * 128:(tb + 1) * 128],
                                   lambda c, half=half: wkvv_a[:, c, half * 512:(half + 1) * 512], 2, [t_n, t_wa, bfree])
                        dview = v_ml_d[half * 512:(half + 1) * 512, gb * 128:(gb + 1) * 128].rearrange(
                            "(h s) d -> s h d", s=128)
                        t_ev = vst_out(bank, t_mm, None, dview, None)
                        banksA.release(bi, t_ev)
            phA_all = all_now() + [(S_ka[0], S_ka[0][1]), (S_ka[1], S_ka[1][1]), (S_va[0], S_va[0][1]),
                                   (S_va[1], S_va[1][1]), (S_kra, S_kra[1])]
            t_ag_sb = None
            t_ag_mla = None
            A.reset(m_persist)

            hT = A.alloc([16, TL], BF16)
            cosT = A.alloc([TL], F32, parts=64)
            sinS = A.alloc([TL], F32, parts=64)
            cosTq = A.alloc([TL], F32, parts=64)
            sinSq = A.alloc([TL], F32, parts=64)
            m_ph1 = A.mark()
            wg = [A.alloc([8192], BF16) for i in range(2)]
            cqn = A.alloc([4, TL], BF16)
            ckvn = A.alloc([2, TL], BF16)
            kst = [A.alloc([TL], BF16) for i in range(2)]
            vst = [A.alloc([512], BF16) for i in range(2)]
            sq = [A.alloc([512], BF16) for i in range(2)]
            rstdb = A.alloc([512], F32)
            rt1 = A.alloc([512], F32, parts=64)
            rt2 = A.alloc([512], F32, parts=64)
            krot = A.alloc([TL], BF16, parts=64)
            A.reset(m_ph1)

            posi = A.alloc([TL], I32, parts=64)
            posf = A.alloc([TL], F32, parts=64)
            ang = A.alloc([TL], F32, parts=64)
            rr_y = A.alloc([TL], F32, parts=64)
            rr_k = A.alloc([TL], F32, parts=64)
            xt = [A.alloc([D], F32) for i in range(2)]
            xn = [A.alloc([D], BF16) for i in range(2)]
            junk = A.alloc([D], BF16)
            pnwb = A.alloc([D], F32)

            S_p = B.new_sem("s_pos")
            B.op("sp", I("dma_start", out=posi, in_=posb_d[:, :]), waits=phA_all, sem=S_p, amt=16)
            B.op("sp", I("dma_start", out=pnwb, in_=pnwb_d[:, :]), waits=phA_all, sem=S_p, amt=16)
            t_pnw = (S_p, 32)
            t = B.op("dve", I("tensor_copy", out=posf, in_=posi), waits=[t_pnw])
            invf = csts[0:64, 6:7]
            sgn = csts[0:64, 7:8]
            PI = math.pi
            INV2PI = 1.0 / (2.0 * PI)
            C1 = 6.28125
            C2 = 2.0 * PI - C1
            t1 = B.op("dve", I("tensor_scalar", out=ang, in0=posf, scalar1=invf, scalar2=None, op0=ALU.mult),
                      waits=[t, t_cst])
            ty = B.op("dve", I("tensor_scalar", out=rr_y, in0=ang, scalar1=INV2PI, scalar2=0.5,
                                                        op0=ALU.mult, op1=ALU.add), waits=[t1])
            tk = B.op("dve", I("tensor_copy", out=posi, in_=rr_y), waits=[ty])
            tkf = B.op("dve", I("tensor_copy", out=rr_k, in_=posi), waits=[tk])
            tg = B.op("dve", I("tensor_tensor", out=rr_y, in0=rr_k, in1=rr_y, op=ALU.is_gt), waits=[tkf])
            tm = B.op("dve", I("tensor_tensor", out=rr_k, in0=rr_k, in1=rr_y, op=ALU.subtract), waits=[tg])
            tr1 = B.op("dve", I("scalar_tensor_tensor", out=ang, in0=rr_k, scalar=-C1, in1=ang,
                                                                op0=ALU.mult, op1=ALU.add), waits=[tm])
            tr2 = B.op("dve", I("scalar_tensor_tensor", out=ang, in0=rr_k, scalar=-C2, in1=ang,
                                                                op0=ALU.mult, op1=ALU.add), waits=[tr1])
            tc1 = B.op("dve", I("tensor_scalar", out=ang, in0=ang, scalar1=PI, scalar2=-PI,
                                                         op0=ALU.min, op1=ALU.max), waits=[tr2])
            t3 = B.op("act", I("activation", out=sinS, in_=ang, func=AF.Sin), waits=[tc1])
            t4 = B.op("dve", I("tensor_scalar", out=sinS, in0=sinS, scalar1=sgn, scalar2=None, op0=ALU.mult),
                      waits=[t3])
            t_sinq = B.op("dve", I("tensor_scalar", out=sinSq, in0=sinS, scalar1=SC_MLA, scalar2=None, op0=ALU.mult),
                          waits=[t4])
            t5 = B.op("dve", I("tensor_scalar", out=rr_k, in0=ang, scalar1=0.5 * PI, scalar2=None, op0=ALU.add),
                      waits=[t3, tm])
            t5b = B.op("dve", I("tensor_single_scalar", out=rr_y, in_=rr_k, scalar=PI, op=ALU.is_gt), waits=[t5])
            t6 = B.op("dve", I("scalar_tensor_tensor", out=rr_k, in0=rr_y, scalar=-2.0 * PI, in1=rr_k,
                                                               op0=ALU.mult, op1=ALU.add), waits=[t5b])
            t6b = B.op("dve", I("tensor_scalar", out=rr_k, in0=rr_k, scalar1=PI, scalar2=-PI,
                                                         op0=ALU.min, op1=ALU.max), waits=[t6])
            t7 = B.op("act", I("activation", out=cosT, in_=rr_k, func=AF.Sin), waits=[t6b])
            t_cosq = B.op("dve", I("tensor_scalar", out=cosTq, in0=cosT, scalar1=SC_MLA, scalar2=None, op0=ALU.mult),
                          waits=[t7])
            t_tabs = [t4, t_sinq, t7, t_cosq]

            S_x = [B.new_sem("s_x%d" % i) for i in range(2)]
            xn_free = [None, None]
            xt_free = [None, None]
            tpb = [PS[0].bitcast(BF16), PS[1].bitcast(BF16)]
            tp_free = [None, None]
            tpi = 0
            hT_toks = []
            for b in range(NB):
                s = b % 2
                t_x = B.op("sp", I("dma_start", out=xt[s], in_=x_d[b * 128:(b + 1) * 128, :]),
                           waits=[xt_free[s]] + phA_all, sem=S_x[s], amt=16)
                t_sq = B.op("act", I("activation", out=junk, in_=xt[s], func=AF.Square,
                                                                   accum_out=small[:, b:b + 1]), waits=[t_x, t_z])
                t_r1, t_r2 = rsqrt_chain(small[:, 16 + b:17 + b], small[:, b:b + 1], 1.0 / D, [t_sq])
                t_xn = B.op("dve", I("scalar_tensor_tensor",
                    out=xn[s], in0=xt[s], scalar=small[:, 16 + b:17 + b], in1=pnwb,
                    op0=ALU.mult, op1=ALU.mult), waits=[t_r2, t_x, t_pnw, xn_free[s]])
                xt_free[s] = t_xn
                for g in range(4):
                    ti = tpi % 2
                    tpi += 1
                    for j in range(4):
                        c = 4 * g + j
                        t_tp = B.op("pe", I("transpose",
                            out=tpb[ti][:, j * 128:(j + 1) * 128], in_=xn[s][:, c * 128:(c + 1) * 128], identity=ident),
                            waits=[t_xn, tp_free[ti], t_cbf] if j == 0 else [], inc=(j == 3))
                    src = tpb[ti][:, 0:512].rearrange("p (j t) -> p j t", j=4)
                    dst = hT[:, 4 * g:4 * g + 4, b * 128:(b + 1) * 128]
                    if g % 2 == 0:
                        t_ev = B.op("act", I("copy", out=dst, in_=src), waits=[t_tp])
                    else:
                        t_ev = B.op("dve", I("tensor_copy", out=dst, in_=src), waits=[t_tp])
                    tp_free[ti] = t_ev
                    hT_toks.append(t_ev)
                xn_free[s] = t_tp
            ph0_done = hT_toks[-8:] + t_tabs
            if stop_after <= 0:
                dump("d_hT", hT.rearrange("p c t -> p (c t)"), ph0_done)
                for i_, tb_ in enumerate((cosT, sinS, cosTq, sinSq)):
                    dump("d_tab", tb_, ph0_done) if False else dbg_toks.append(B.op(
                        "sp", I("dma_start", out=dbg["d_tab"][:, i_ * TL:(i_ + 1) * TL], in_=tb_),
                        waits=ph0_done, sem=S_dbg, amt=16))
            maybe_stop(0)

            banks = Banks(PS)
            banks.free[0] = tp_free[0]
            banks.free[1] = tp_free[1]
            S_w = [B.new_sem("s_w%d" % i) for i in range(2)]
            wg_free = [None, None]
            wslot = [0]

            def load_w(parts):
                s = wslot[0] % 2
                wslot[0] += 1
                tok = None
                for (src, dst) in parts:
                    tok = B.op("pool", I("dma_start", out=dst, in_=src),
                               waits=[wg_free[s]] + ph0_done, sem=S_w[s], amt=16)
                return s, tok

            def wview(s, off, nch, width):
                return wg[s][:, off:off + nch * width].rearrange("p (c n) -> p c n", c=nch)

            def mm_group(out_ap, lhs_fn, rhs_fn, nch, waits):
                tok = None
                for c in range(nch):
                    tok = B.op("pe", I("matmul", out_ap, lhsT=lhs_fn(c), rhs=rhs_fn(c),
                                                              start=(c == 0), stop=(c == nch - 1)),
                               waits=waits if c == 0 else [], inc=(c == nch - 1))
                return tok

            S_k = [B.new_sem("s_kst%d" % i) for i in range(2)]
            S_v = [B.new_sem("s_vst%d" % i) for i in range(2)]
            kst_free = [None, None]
            vst_free = [None, None]
            kcnt = [0]
            vcnt = [0]
            snd_sb_toks = []
            snd_mla_toks = []

            def win_cols(c0, n):
                return win_d[:, c0:c0 + n].rearrange("(c p) n -> p c n", p=128)

            def proj_k_heads(wv, t_w, s_w, nheads, head0, dst, toks, nch, rhs_tile, rdy):
                for hh in range(nheads):
                    h = head0 + hh
                    ks = kcnt[0] % 2
                    kcnt[0] += 1
                    evs = []
                    for tt in range(2):
                        bi, bank, bfree = banks.get()
                        t_mm = mm_group(bank, lambda c, hh=hh: wv[:, c, hh * 128:(hh + 1) * 128],
                                        lambda c, tt=tt: rhs_tile[:, c, tt * 512:(tt + 1) * 512], nch,
                                        [t_w, bfree] + rdy)
                        t_ev = B.op("act", I("copy",
                            out=kst[ks][:, tt * 512:(tt + 1) * 512], in_=bank), waits=[t_mm, kst_free[ks]])
                        banks.release(bi, t_ev)
                        evs.append(t_ev)
                    wg_free[s_w] = t_mm
                    r0 = h * 128
                    t_d = B.op("sp", I("dma_start", out=dst[r0:r0 + 128, :], in_=kst[ks]),
                               waits=evs, sem=S_k[ks], amt=16)
                    kst_free[ks] = t_d
                    toks.append(t_d)

            def proj_v_tok(wv, t_w, s_w, ch, nch, lhs_tile, dst, row0, toks, rdy):
                for b in range(NB):
                    bi, bank, bfree = banks.get()
                    t_mm = mm_group(bank, lambda c, b=b: lhs_tile[:, c, b * 128:(b + 1) * 128],
                                    lambda c: wv[:, c, :], nch, [t_w, bfree] + rdy)
                    vs = vcnt[0] % 2
                    vcnt[0] += 1
                    t_ev = B.op("dve", I("tensor_copy", out=vst[vs], in_=bank),
                                waits=[t_mm, vst_free[vs]])
                    banks.release(bi, t_ev)
                    r0 = row0 + ch * 512
                    dview = dst[r0:r0 + 512, b * 128:(b + 1) * 128].rearrange("(h s) d -> s h d", s=128)
                    t_d = B.op("sp", I("dma_start",
                        out=dview, in_=vst[vs].rearrange("s (h d) -> s h d", h=4)),
                        waits=[t_ev], sem=S_v[vs], amt=16)
                    vst_free[vs] = t_d
                    toks.append(t_d)
                wg_free[s_w] = t_mm

            sqfree = [None, None]
            rstd_free = [None]
            rope_free = [None]

            def latent_norm(wv, t_w, ncb, nfeat, nw_col0, dst, extra_fn=None):
                last = None
                for tt in range(2):
                    lb = []
                    for j in range(ncb):
                        bi, bank, bfree = banks.get()
                        t_mm = mm_group(bank, lambda c, j=j: wv[:, c, j * 128:(j + 1) * 128],
                                        lambda c, tt=tt: hT[:, c, tt * 512:(tt + 1) * 512], 16,
                                        [t_w, bfree] + ph0_done)
                        lb.append((bi, bank, t_mm))
                    ex = extra_fn(tt) if extra_fn is not None else None
                    bs, ssb, ssfree = banks.get()
                    t_o = None
                    for j in range(ncb):
                        bi, bank, t_mm = lb[j]
                        t_s = B.op("act", I("activation", out=sq[j % 2], in_=bank, func=AF.Square),
                                   waits=[t_mm, sqfree[j % 2]])
                        t_o = B.op("pe", I("matmul", ssb, lhsT=ones, rhs=sq[j % 2],
                                                                  start=(j == 0), stop=(j == ncb - 1)),
                                   waits=[t_s, ssfree if j == 0 else None, t_cbf])
                        sqfree[j % 2] = t_o
                    t_a, t_b = rsqrt_chain(rstdb, ssb, 1.0 / nfeat, [t_o, rstd_free[0]])
                    banks.release(bs, t_a)
                    for j in range(ncb):
                        bi, bank, t_mm = lb[j]
                        t_n = B.op("dve", I("scalar_tensor_tensor",
                            out=dst[:, j, tt * 512:(tt + 1) * 512], in0=bank, scalar=csts[:, nw_col0 + j:nw_col0 + j + 1],
                            in1=rstdb, op0=ALU.mult, op1=ALU.mult), waits=[t_b, t_mm, t_cst])
                        banks.release(bi, t_n)
                        last = t_n
                    rstd_free[0] = last
                    if ex is not None:
                        ex()
                return last

            def rope(pa, pb, ta, tb, ct, sn, dst_ap):
                t1 = B.op("dve", I("tensor_tensor", out=rt1, in0=pa, in1=ct, op=ALU.mult),
                          waits=[ta, rope_free[0]] + t_tabs)
                t2 = B.op("dve", I("tensor_tensor", out=rt2, in0=pb, in1=sn, op=ALU.mult),
                          waits=[tb] + t_tabs)
                t3 = B.op("dve", I("tensor_tensor", out=dst_ap, in0=rt1, in1=rt2, op=ALU.add),
                          waits=[t1, t2])
                rope_free[0] = t3
                return t1, t2, t3

            def proj_fm_keep(col0, dst, chunk0, func, scale):
                toks = []
                for gi in range(2):
                    s_w = wslot[0] % 2
                    s_w, t_w = load_w([(win_cols(col0 + gi * 512, 512), wview(s_w, 0, 16, 512))])
                    wv = wview(s_w, 0, 16, 512)
                    for hh in range(4):
                        ch = chunk0 + gi * 4 + hh
                        for tt in range(2):
                            bi, bank, bfree = banks.get()
                            t_mm = mm_group(bank, lambda c, hh=hh, wv=wv: wv[:, c, hh * 128:(hh + 1) * 128],
                                            lambda c, tt=tt: hT[:, c, tt * 512:(tt + 1) * 512], 16,
                                            [t_w, bfree] + ph0_done)
                            t_ev = B.op("act", I("activation",
                                out=dst[:, ch, tt * 512:(tt + 1) * 512], in_=bank, func=func, scale=scale), waits=[t_mm])
                            banks.release(bi, t_ev)
                            toks.append(t_ev)
                    wg_free[s_w] = t_mm
                return toks

            proj_fm_keep(0, qT_sb, 0, AF.Copy, SC_SB)
            proj_fm_keep(3072, gate, 0, AF.Silu, 1.0)
            proj_fm_keep(4928, gate, 8, AF.Silu, 1.0)

            s_w = wslot[0] % 2
            s_w, t_w = load_w([(win_cols(4096, 512), wview(s_w, 0, 16, 512))])
            t_cqn = latent_norm(wview(s_w, 0, 16, 512), t_w, 4, 512, 0, cqn, None)
            wg_free[s_w] = now("pe")

            s_w = wslot[0] % 2
            s_w, t_w = load_w([
                (wqn_d[:, :].rearrange("(c p) n -> p c n", p=128), wview(s_w, 0, 4, 1024)),
                (wqr_d[:, :].rearrange("(c p) n -> p c n", p=128), wview(s_w, 4096, 4, 512)),
                (wqrs_d[:, :].rearrange("(c p) n -> p c n", p=128), wview(s_w, 6144, 4, 512)),
            ])
            t_w = (S_w[s_w], S_w[s_w][1])
            wqn_v = wview(s_w, 0, 4, 1024)
            wqr_v = wview(s_w, 4096, 4, 512)
            wqrs_v = wview(s_w, 6144, 4, 512)
            for h in range(8):
                for tt in range(2):
                    bi, bank, bfree = banks.get()
                    t_mm = mm_group(bank, lambda c, h=h: wqn_v[:, c, h * 128:(h + 1) * 128],
                                    lambda c, tt=tt: cqn[:, c, tt * 512:(tt + 1) * 512], 4, [t_w, bfree, t_cqn])
                    t_ev = B.op("act", I("activation",
                        out=qN[:, h, tt * 512:(tt + 1) * 512], in_=bank, func=AF.Copy, scale=SC_MLA), waits=[t_mm])
                    banks.release(bi, t_ev)
                    b1, bk1, f1 = banks.get()
                    t_m1 = mm_group(bk1[0:64, :], lambda c, h=h: wqr_v[:, c, h * 64:(h + 1) * 64],
                                    lambda c, tt=tt: cqn[:, c, tt * 512:(tt + 1) * 512], 4, [t_w, f1, t_cqn])
                    b2, bk2, f2 = banks.get()
                    t_m2 = mm_group(bk2[0:64, :], lambda c, h=h: wqrs_v[:, c, h * 64:(h + 1) * 64],
                                    lambda c, tt=tt: cqn[:, c, tt * 512:(tt + 1) * 512], 4, [t_w, f2, t_cqn])
                    t1, t2, t3 = rope(bk1[0:64, :], bk2[0:64, :], t_m1, t_m2, cosTq[:, tt * 512:(tt + 1) * 512],
                                      sinSq[:, tt * 512:(tt + 1) * 512], qR[:, h, tt * 512:(tt + 1) * 512])
                    banks.release(b1, t1)
                    banks.release(b2, t2)
            ph1_all = all_now()

            if debug:
                dump("d_hT", hT.rearrange("p c t -> p (c t)"), ph1_all)
                dump("d_qsb", qT_sb.rearrange("p c t -> p (c t)"), ph1_all)
                dump("d_qn", qN.rearrange("p c t -> p (c t)"), ph1_all)
                dump("d_qr", qR.rearrange("p c t -> p (c t)"), ph1_all)
                dump("d_gate", gate.rearrange("p c t -> p (c t)"), ph1_all)
                dump("d_cqn", cqn.rearrange("p c t -> p (c t)"), ph1_all)
                ph1_all = ph1_all + [(S_dbg, S_dbg[1])]
            maybe_stop(1)

            if True:
                A.reset(m_persist)
                kbuf = [A.alloc([64, 128], BF16) for _ in range(2)]
                vbuf = [A.alloc([64, 128], BF16) for _ in range(2)]
                krbuf = A.alloc([64, 128], BF16, parts=64)
                e_t = [A.alloc([512], F32) for _ in range(2)]
                sp_t = [A.alloc([512], BF16) for _ in range(4)]
                a_t = [A.alloc([512], BF16) for _ in range(3)]
                Rt = A.alloc([512], BF16)
                rec = A.alloc([512], F32)
                otmp = A.alloc([512], F32)
                m_ph2 = A.mark()

                ZB = PS[0:4]
                OACC = PS[4:6]
                DEN = PS[6:8]
                zfree = [None] * 4
                oacc_free = [None, None]
                den_free = [None, None]
                efree = [None, None]
                spfree = [[None, None] for _ in range(4)]
                afree = [None] * 3
                S_kv = [B.new_sem("s_kv%d" % i) for i in range(2)]
                kv_free = [None, None]
                S_krb = B.new_sem("s_krb")
                t_krb = B.op("sp", I("dma_start", out=krbuf, in_=kr_d.rearrange("f (g s) -> f g s", s=128)),
                             waits=ph1_all + phA_all, sem=S_krb, amt=16)

                def load_head(hi):
                    s = hi % 2
                    h = hi % 8
                    if hi < 8:
                        ksrc = kt_sb_d[h * 128:(h + 1) * 128, :]
                        vsrc = v_sb_d[h * 128:(h + 1) * 128, :]
                    else:
                        ksrc = kt_ml_d[h * 128:(h + 1) * 128, :]
                        vsrc = v_ml_d[h * 128:(h + 1) * 128, :]
                    B.op("sp", I("dma_start", out=kbuf[s], in_=ksrc.rearrange("p (g s) -> p g s", s=128)),
                         waits=[kv_free[s]] + ph1_all + phA_all, sem=S_kv[s], amt=16)
                    B.op("sp", I("dma_start", out=vbuf[s], in_=vsrc.rearrange("p (g s) -> p g s", s=128)),
                         waits=[kv_free[s]] + ph1_all + phA_all, sem=S_kv[s], amt=16)
                    return (S_kv[s], S_kv[s][1])

                cnt = {"z": 0, "e": 0, "sp": 0, "a": 0, "st": 0}
                mixed_toks = []
                kv_tok = {}
                kv_tok[0] = load_head(0)
                for hi in range(16):
                    s = hi % 2
                    h = hi % 8
                    is_sb = hi < 8
                    if hi + 1 < 16:
                        kv_tok[hi + 1] = load_head(hi + 1)
                    t_kv = kv_tok[hi]
                    for u in range(2):
                        blocks = [(g, rp) for g in range(4 * u + 3, -1, -1) for rp in range(7, -1, -1)]
                        n = len(blocks)
                        oi = cnt["st"] % 2
                        cnt["st"] += 1
                        oacc = OACC[oi]
                        den = DEN[oi]
                        info = [None] * n
                        t_Rz = None
                        if is_sb:
                            t_Rz = B.op("dve", I("memset", Rt, 0.0), waits=[now("pe")] + ph1_all)
                        tR_prev = [t_Rz]

                        def geo(k):
                            g, rp = blocks[k]
                            c0 = 128 * max(0, g - 4 * u)
                            return g, rp, c0, (g >= 4 * u)

                        def stA(k):
                            g, rp, c0, diag = geo(k)
                            zi = cnt["z"] % 4
                            cnt["z"] += 1
                            zb = ZB[zi]
                            d = {"zi": zi, "zb": zb}
                            info[k] = d
                            qc = slice(512 * u + c0, 512 * u + 512)
                            kblk = kbuf[s][:, 8 * g + rp, :]
                            w = [zfree[zi], t_kv] + ph1_all
                            if is_sb:
                                t = B.op("pe", I("matmul", zb[:, c0:512], lhsT=kblk, rhs=qT_sb[:, h, qc],
                                                                   start=True, stop=False), waits=w, inc=not diag)
                                if diag:
                                    t = B.op("pe", I("matmul", zb[:, c0:c0 + 128], lhsT=ident,
                                                                       rhs=msk_sb[:, rp * 128:(rp + 1) * 128],
                                                                       start=False, stop=False, skip_group_check=True))
                            else:
                                B.op("pe", I("matmul", zb[:, c0:512], lhsT=kblk, rhs=qN[:, h, qc],
                                                               start=True, stop=False), waits=w, inc=False)
                                t = B.op("pe", I("matmul", zb[:, c0:512], lhsT=krbuf[:, 8 * g + rp, :],
                                                                   rhs=qR[:, h, qc], start=False, stop=not diag),
                                         waits=[t_krb])
                                if diag:
                                    t = B.op("pe", I("matmul", zb[:, c0:c0 + 128], lhsT=ident,
                                                                       rhs=msk_mla[:, rp * 128:(rp + 1) * 128],
                                                                       start=False, stop=True, skip_group_check=True))
                            d["tz"] = t

                        def stB(k):
                            g, rp, c0, diag = geo(k)
                            d = info[k]
                            zb = d["zb"]
                            if is_sb:
                                ei = cnt["e"] % 2
                                cnt["e"] += 1
                                si = cnt["sp"] % 4
                                cnt["sp"] += 1
                                d["si"] = si
                                t_e = B.op("act", I("activation", out=e_t[ei][:, c0:512], in_=zb[:, c0:512], func=AF.Exp),
                                           waits=[d["tz"]])
                                t_s = B.op("act", I("activation", out=sp_t[si][:, c0:512], in_=e_t[ei][:, c0:512],
                                                                         func=AF.Ln, bias=1.0, scale=1.0),
                                           waits=[t_e] + spfree[si])
                                d["tsp"] = t_s
                            else:
                                ai = cnt["a"] % 3
                                cnt["a"] += 1
                                d["ai"] = ai
                                t_a = B.op("act", I("activation", out=a_t[ai][:, c0:512], in_=zb[:, c0:512], func=AF.Exp),
                                           waits=[d["tz"], afree[ai]])
                                d["ta"] = t_a
                                zfree[d["zi"]] = t_a

                        def stC(k):
                            if not is_sb:
                                return
                            g, rp, c0, diag = geo(k)
                            d = info[k]
                            zb = d["zb"]
                            si = d["si"]
                            first = (k == 0)
                            t = B.op("pe", I("matmul", zb[:, c0:512], lhsT=negtri, rhs=sp_t[si][:, c0:512],
                                                               start=False, stop=first, skip_group_check=True),
                                     waits=[d["tsp"]], inc=first)
                            if not first:
                                t = B.op("pe", I("matmul", zb[:, c0:512], lhsT=negones, rhs=Rt[:, c0:512],
                                                                   start=False, stop=True, skip_group_check=True),
                                         waits=[tR_prev[0]])
                            d["tc"] = t
                            tR = B.op("dve", I("tensor_tensor", out=Rt[:, c0:512], in0=Rt[:, c0:512],
                                                                       in1=sp_t[si][:, c0:512], op=ALU.add),
                                      waits=[d["tsp"], t, tR_prev[0]])
                            tR_prev[0] = tR
                            spfree[si] = [t, tR]
                            ai = cnt["a"] % 3
                            cnt["a"] += 1
                            d["ai"] = ai
                            t_a = B.op("act", I("activation", out=a_t[ai][:, c0:512], in_=zb[:, c0:512], func=AF.Exp),
                                       waits=[t, afree[ai]])
                            d["ta"] = t_a
                            zfree[d["zi"]] = t_a

                        def stF(k):
                            g, rp, c0, diag = geo(k)
                            d = info[k]
                            ai = d["ai"]
                            vblk = vbuf[s][:, 8 * g + rp, :]
                            w = [d["ta"]]
                            if k == 0:
                                w += [oacc_free[oi], den_free[oi]]
                            t = B.op("pe", I("matmul", oacc[:, c0:512], lhsT=vblk, rhs=a_t[ai][:, c0:512],
                                                               start=(k == 0), stop=(k == n - 1), skip_group_check=True),
                                     waits=w, inc=is_sb)
                            if not is_sb:
                                t = B.op("pe", I("matmul", den[:, c0:512], lhsT=ones, rhs=a_t[ai][:, c0:512],
                                                                   start=(k == 0), stop=(k == n - 1), skip_group_check=True))
                            afree[ai] = t
                            d["tf"] = t

                        for step in range(n + 3):
                            if step + 1 < n + 1 and step < n:
                                pass
                            if step == 0:
                                stA(0)
                            if step + 1 < n:
                                stA(step + 1)
                            if step < n:
                                stB(step)
                            if 0 <= step - 1 < n:
                                stC(step - 1)
                            if 0 <= step - 2 < n:
                                stF(step - 2)
                        t_last = info[n - 1]["tf"]
                        kv_free[s] = t_last
                        ch = h if is_sb else 8 + h
                        gsl = gate[:, ch, 512 * u:512 * u + 512]
                        if is_sb:
                            t_m = B.op("dve", I("tensor_tensor", out=gsl, in0=oacc, in1=gsl, op=ALU.mult),
                                       waits=[t_last] + ph1_all)
                            oacc_free[oi] = t_m
                        else:
                            t_r = B.op("dve", I("reciprocal", out=rec, in_=den), waits=[t_last, now("dve")])
                            den_free[oi] = t_r
                            t_o = B.op("dve", I("tensor_tensor", out=otmp, in0=oacc, in1=rec, op=ALU.mult),
                                       waits=[t_r])
                            oacc_free[oi] = t_o
                            t_m = B.op("dve", I("tensor_tensor", out=gsl, in0=otmp, in1=gsl, op=ALU.mult),
                                       waits=[t_o] + ph1_all)
                        mixed_toks.append(t_m)
                ph2_all = all_now()
                if debug:
                    dbg_toks.append(B.op("sp", I("dma_start", out=dbg["d_mixed"], in_=gate.rearrange("p c t -> p (c t)")),
                                         waits=ph2_all, sem=S_dbg, amt=16))
                    ph2_all = ph2_all + [(S_dbg, S_dbg[1])]

            maybe_stop(2)
            if True:
                A.reset(m_persist)
                wout = A.alloc([16, D], BF16)
                ponwb = A.alloc([D], F32)
                xres = [A.alloc([D], F32) for _ in range(2)]
                ytile = [A.alloc([D], F32) for _ in range(2)]
                junk3 = A.alloc([512], BF16)
                S_wo = B.new_sem("s_wo")
                wo_v = wout_d.rearrange("(c p) n -> p c n", p=128)
                for nq in range(4):
                    B.op("pool", I("dma_start", out=wout[:, :, nq * 512:(nq + 1) * 512],
                                                               in_=wo_v[:, :, nq * 512:(nq + 1) * 512]),
                         waits=ph2_all, sem=S_wo, amt=16)
                B.op("sp", I("dma_start", out=ponwb, in_=ponwb_d[:, :]), waits=ph2_all, sem=S_wo, amt=16)
                t_wo = (S_wo, 80)
                S_xr = [B.new_sem("s_xr%d" % i) for i in range(2)]
                S_o = [B.new_sem("s_o%d" % i) for i in range(2)]
                xr_free = [None, None]
                yt_free = [None, None]
                yb_free = [None] * 8
                out_toks = []
                for b in range(NB):
                    s = b % 2
                    t_xr = B.op("sp", I("dma_start", out=xres[s], in_=x_d[b * 128:(b + 1) * 128, :]),
                                waits=[xr_free[s]] + ph2_all, sem=S_xr[s], amt=16)
                    t_mms = []
                    for nq in range(4):
                        bi = 4 * s + nq
                        t_mm = mm_group(PS[bi], lambda c, b=b: gate[:, c, b * 128:(b + 1) * 128],
                                        lambda c, nq=nq: wout[:, c, nq * 512:(nq + 1) * 512], 16,
                                        [t_wo, yb_free[bi]] + ph2_all)
                        t_mms.append(t_mm)
                        B.op("act", I("activation",
                            out=junk3, in_=PS[bi], func=AF.Square, accum_out=small2[:, 4 * b + nq:4 * b + nq + 1]),
                            waits=[t_mm, t_z])
                    t_sq = now("act")
                    t_s1 = B.op("dve", I("tensor_reduce", out=small2[:, 32 + b:33 + b], in_=small2[:, 4 * b:4 * b + 4],
                                                                      axis=mybir.AxisListType.X, op=ALU.add), waits=[t_sq])
                    t_s2, t_s3 = rsqrt_chain(small2[:, 48 + b:49 + b], small2[:, 32 + b:33 + b], 1.0 / D, [t_s1])
                    t_y = None
                    for nq in range(4):
                        bi = 4 * s + nq
                        cs = slice(nq * 512, (nq + 1) * 512)
                        t_y1 = B.op("dve", I("scalar_tensor_tensor",
                            out=ytile[s][:, cs], in0=PS[bi], scalar=small2[:, 48 + b:49 + b], in1=ponwb[:, cs],
                            op0=ALU.mult, op1=ALU.mult), waits=[t_s3, t_mms[nq], yt_free[s], t_wo])
                        yb_free[bi] = t_y1
                        t_y = B.op("dve", I("tensor_tensor", out=ytile[s][:, cs], in0=ytile[s][:, cs],
                                                                                in1=xres[s][:, cs], op=ALU.add),
                                   waits=[t_y1, t_xr])
                    xr_free[s] = t_y
                    t_o = B.op("sp", I("dma_start", out=out_d[b * 128:(b + 1) * 128, :], in_=ytile[s]),
                               waits=[t_y], sem=S_o[s], amt=16)
                    yt_free[s] = t_o
                    out_toks.append(t_o)


        except _Stop:
            pass

        final_toks = list(dbg_toks) + out_toks[-2:]
        fw = [(t[0][0], t[1]) for t in final_toks]

        with nc.Block() as block:
            @block.tensor
            def _(e):
                B.replay(e, "pe")

            @block.scalar
            def _(e):
                B.replay(e, "act")

            @block.vector
            def _(e):
                B.replay(e, "dve")

            @block.gpsimd
            def _(e):
                B.replay(e, "pool")

            @block.sync
            def _(e):
                B.replay(e, "sp")
                for h_, v_ in fw:
                    e.wait_ge(h_, v_)
    return nc


def _host_inputs(x, positions, pre_norm_w, w_in, q_norm_w, w_q_up, kv_norm_w, w_kv_up, w_out, post_norm_w):
    f32 = np.float32
    x = np.asarray(x, f32)[0]
    pos = np.asarray(positions)[0].astype(np.int32)
    w_in = np.ascontiguousarray(np.asarray(w_in, f32)[0])
    w_q_up = np.asarray(w_q_up, f32)[0]
    w_kv_up = np.asarray(w_kv_up, f32)[0]
    w_out = np.ascontiguousarray(np.asarray(w_out, f32)[0])
    pnw = np.asarray(pre_norm_w, f32)[0]
    ponw = np.asarray(post_norm_w, f32)[0]
    qnw = np.asarray(q_norm_w, f32)[0]
    kvnw = np.asarray(kv_norm_w, f32)[0]

    w_krs = np.ascontiguousarray(np.concatenate([w_in[:, 4896:4928], w_in[:, 4864:4896]], axis=1))
    wq = w_q_up.reshape(512, 8, 192)
    w_qn = np.ascontiguousarray(wq[:, :, 0:128].reshape(512, 1024))
    w_qr = np.ascontiguousarray(wq[:, :, 128:192].reshape(512, 512))
    w_qrs = np.ascontiguousarray(np.concatenate([wq[:, :, 160:192], wq[:, :, 128:160]], axis=2).reshape(512, 512))
    wkv = w_kv_up.reshape(256, 8, 256)
    w_kn = np.ascontiguousarray(wkv[:, :, 0:128].reshape(256, 1024))
    w_kvv = np.ascontiguousarray(wkv[:, :, 128:256].reshape(256, 1024))
    pnw_b = np.ascontiguousarray(np.broadcast_to(pnw[None, :], (128, D)))
    ponw_b = np.ascontiguousarray(np.broadcast_to(ponw[None, :], (128, D)))

    inv_freq = (10000.0 ** (-np.arange(0, 64, 2, dtype=np.float32) / np.float32(64))).astype(f32)
    cst_s = np.zeros((128, 8), f32)
    cst_s[:, 0:4] = qnw.reshape(4, 128).T
    cst_s[:, 4:6] = kvnw.reshape(2, 128).T
    cst_s[0:64, 6] = np.concatenate([inv_freq, inv_freq])
    cst_s[0:64, 7] = np.concatenate([-np.ones(32, f32), np.ones(32, f32)])
    idx = np.arange(128)
    ident = np.eye(128, dtype=f32)
    negtri = -(idx[:, None] >= idx[None, :]).astype(f32)
    negones = -np.ones((128, 128), f32)
    ones = np.ones((128, 128), f32)

    xb = x.reshape(64, 128, D)
    pb = pos.reshape(64, 128)
    xT = np.ascontiguousarray(x.T)
    posa = np.ascontiguousarray(np.broadcast_to(pos[None, :], (64, SEQ))).astype(np.int32)
    pnw_c = np.ascontiguousarray(pnw.reshape(16, 128).T)
    in_maps = []
    for r in range(NCORES):
        msb = np.zeros((128, 8, 128), f32)
        mml = np.zeros((128, 8, 128), f32)
        for rp in range(8):
            if rp > r:
                msb[:, rp, :] = NEG
                mml[:, rp, :] = NEG
            elif rp == r:
                msb[:, rp, :] = np.where(idx[:, None] < idx[None, :], 0.0, NEG)
                mml[:, rp, :] = np.where((idx[:, None] // 64) <= (idx[None, :] // 64), 0.0, NEG)
        cst_bf = np.concatenate([ident, negtri, negones, ones, msb.reshape(128, 1024), mml.reshape(128, 1024)],
                                axis=1).astype(f32)
        in_maps.append({
            "x": np.ascontiguousarray(xb[r::8].reshape(TL, D)),
            "posb": np.ascontiguousarray(np.broadcast_to(pb[r::8].reshape(1, TL), (64, TL))).astype(np.int32),
            "xT": xT, "posa": posa, "pnw_c": pnw_c,
            "w_in": w_in, "w_krs": w_krs, "w_qn": w_qn, "w_qr": w_qr, "w_qrs": w_qrs,
            "w_kn": w_kn, "w_kvv": w_kvv, "w_out": w_out, "pnw_b": pnw_b, "ponw_b": ponw_b,
            "cst_s": cst_s, "cst_bf": np.ascontiguousarray(cst_bf),
        })
    return in_maps


_NC_CACHE = {}


def kernel(x, positions, pre_norm_w, w_in, q_norm_w, w_q_up, kv_norm_w, w_kv_up, w_out, post_norm_w):
    in_maps = _host_inputs(x, positions, pre_norm_w, w_in, q_norm_w, w_q_up, kv_norm_w, w_kv_up, w_out, post_norm_w)
    nc = build_program()
    res = run_bass_kernel_spmd(nc, in_maps, core_ids=list(range(NCORES)))
    out = np.zeros((64, 128, D), np.float32)
    for r in range(NCORES):
        out[r::8] = np.asarray(res.results[r]["out"], np.float32).reshape(8, 128, D)
    return out.reshape(1, SEQ, D)
```

```python
import math
from contextlib import ExitStack

import numpy as np
import concourse.bass as bass
import concourse.mybir as mybir
from concourse.bass_utils import run_bass_kernel_spmd

F32 = mybir.dt.float32
BF16 = mybir.dt.bfloat16
I32 = mybir.dt.int32
AF = mybir.ActivationFunctionType
ALU = mybir.AluOpType

NCORES = 8
D = 2048
SEQ = 8192
TL = 1024
NB = 8
DIN = 5952
EPS = 1e-6
NEG = -30000.0
SC_SB = 1.0 / math.sqrt(128.0)
SC_MLA = 1.0 / math.sqrt(192.0)
SB_ROWS = 2048
MLA_ROWS = 2112

DEBUG = False


class _Stop(Exception):
    pass


class Builder:
    def __init__(self, nc, stack):
        self.nc = nc
        self.stack = stack
        self.q = {k: [] for k in ("pe", "act", "dve", "pool", "sp")}
        self.waited = {k: {} for k in self.q}
        self.prog = {k: self.new_sem("prog_" + k) for k in ("pe", "act", "dve", "pool")}
        self.nsem = 0

    def new_sem(self, name):
        h = self.stack.enter_context(self.nc.semaphore(name))
        return [h, 0]

    def op(self, eng, fn, waits=(), sem=None, amt=1, inc=True):
        ws = []
        mx = {}
        for t in waits:
            if t is None:
                continue
            s, v = t
            if id(s) not in mx or mx[id(s)][1] < v:
                mx[id(s)] = (s, v)
        for key, (s, v) in mx.items():
            if self.waited[eng].get(key, 0) >= v:
                continue
            self.waited[eng][key] = v
            ws.append((s[0], v))
        tok = None
        incspec = None
        if inc:
            s = sem if sem is not None else self.prog[eng]
            s[1] += amt
            tok = (s, s[1])
            incspec = (s[0], amt)
        self.q[eng].append((fn, ws, incspec))
        return tok

    def replay(self, eng_obj, key):
        for fn, ws, incspec in self.q[key]:
            for h, v in ws:
                eng_obj.wait_ge(h, v)
            ins = fn(eng_obj)
            if incspec is not None:
                ins.then_inc(incspec[0], incspec[1])


def I(name, *args, **kw):
    return lambda e: getattr(e, name)(*args, **kw)


class Banks:
    def __init__(self, aps):
        self.aps = aps
        self.free = [None] * len(aps)
        self.i = 0

    def get(self):
        i = self.i
        self.i = (self.i + 1) % len(self.aps)
        return i, self.aps[i], self.free[i]

    def release(self, i, tok):
        self.free[i] = tok


class Arena:
    def __init__(self, t, nbytes):
        self.t = t
        self.nbytes = nbytes
        self.off = 0

    def alloc(self, shape, dt, parts=128):
        esz = 2 if dt == BF16 else 4
        n = 1
        for s in shape:
            n *= s
        nb = n * esz
        self.off = (self.off + 63) // 64 * 64
        assert self.off + nb <= self.nbytes, ("SBUF arena overflow", self.off, nb)
        ap = self.t[0:parts, self.off // 2:(self.off + nb) // 2]
        self.off += nb
        if esz == 4:
            ap = ap.bitcast(dt)
        if len(shape) == 2:
            ap = ap.rearrange("p (a b) -> p a b", a=shape[0])
        elif len(shape) == 3:
            ap = ap.rearrange("p (a b c) -> p a b c", a=shape[0], b=shape[1])
        return ap

    def mark(self):
        return self.off

    def reset(self, m):
        self.off = m


def build_program(debug=False, stop_after=9):
    nc = bass.Bass("TRN2", target_bir_lowering=False)

    def din(name, shape, dt=F32):
        return nc.dram_tensor(name, shape, dt, kind="ExternalInput").ap()

    x_d = din("x", [TL, D])
    posb_d = din("posb", [64, TL], I32)
    win_d = din("w_in", [D, DIN])
    wkrs_d = din("w_krs", [D, 64])
    wqn_d = din("w_qn", [512, 1024])
    wqr_d = din("w_qr", [512, 512])
    wqrs_d = din("w_qrs", [512, 512])
    wkn_d = din("w_kn", [256, 1024])
    wkvv_d = din("w_kvv", [256, 1024])
    wout_d = din("w_out", [D, D])
    pnwb_d = din("pnw_b", [128, D])
    ponwb_d = din("ponw_b", [128, D])
    csts_d = din("cst_s", [128, 8])
    cstbf_d = din("cst_bf", [128, 2560])
    out_d = nc.dram_tensor("out", [TL, D], F32, kind="ExternalOutput").ap()

    xT_d = din("xT", [D, SEQ])
    posa_d = din("posa", [64, SEQ], I32)
    pnwc_d = din("pnw_c", [128, 16])
    kt_sb_d = nc.dram_tensor("kt_sb", [1024, SEQ], BF16).ap()
    v_sb_d = nc.dram_tensor("v_sb", [1024, SEQ], BF16).ap()
    kt_ml_d = nc.dram_tensor("kt_ml", [1024, SEQ], BF16).ap()
    v_ml_d = nc.dram_tensor("v_ml", [1024, SEQ], BF16).ap()
    kr_d = nc.dram_tensor("kr", [64, SEQ], BF16).ap()

    dbg = {}
    if debug:
        def dout(name, shape, dt=F32):
            dbg[name] = nc.dram_tensor(name, shape, dt, kind="ExternalOutput").ap()
        dout("d_hT", [128, 16 * TL], BF16)
        dout("d_qsb", [128, 8 * TL], BF16)
        dout("d_qn", [128, 8 * TL], BF16)
        dout("d_qr", [64, 8 * TL], BF16)
        dout("d_gate", [128, 16 * TL], BF16)
        dout("d_tab", [64, 4 * TL], F32)
        dout("d_k0", [128, 8 * TL], BF16)
        dout("d_v0", [128, 8 * TL], BF16)
        dout("d_kr", [64, 8 * TL], BF16)
        dout("d_cqn", [128, 4 * TL], BF16)
        dout("d_ckvn", [128, 2 * TL], BF16)
        dout("d_kn0", [128, 8 * TL], BF16)
        dout("d_vm0", [128, 8 * TL], BF16)
        dout("d_mixed", [128, 16 * TL], BF16)

    ARENA_BYTES = 200 * 1024
    with ExitStack() as st:
        B = Builder(nc, st)
        big = st.enter_context(nc.sbuf_tensor("arena", [128, ARENA_BYTES // 2], BF16))
        A = Arena(big, ARENA_BYTES)

        cbf = A.alloc([2560], BF16)
        ident = cbf[:, 0:128]
        negtri = cbf[:, 128:256]
        negones = cbf[:, 256:384]
        ones = cbf[:, 384:512]
        msk_sb = cbf[:, 512:1536]
        msk_mla = cbf[:, 1536:2560]
        csts = A.alloc([8], F32)
        small = A.alloc([64], F32)
        small2 = A.alloc([64], F32)
        pscr = A.alloc([8], F32)
        pnwc = A.alloc([16], F32)
        m_const = A.mark()
        qT_sb = A.alloc([8, TL], BF16)
        qN = A.alloc([8, TL], BF16)
        qR = A.alloc([8, TL], BF16, parts=64)
        gate = A.alloc([16, TL], BF16)
        m_persist = A.mark()

        psA = st.enter_context(nc.psum_tensor("psA", [128, 4096], F32))
        PS = [psA[:, 512 * i:512 * (i + 1)] for i in range(8)]

        def now(eng):
            return (B.prog[eng], B.prog[eng][1])

        def all_now():
            return [now(k) for k in ("pe", "act", "dve", "pool") if B.prog[k][1] > 0]


        def rsqrt_chain(dst, src_ap, scale, waits):
            ta = B.op("dve", I("tensor_scalar", out=dst, in0=src_ap, scalar1=scale, scalar2=EPS,
                                                        op0=ALU.mult, op1=ALU.add), waits=waits)
            tb = B.op("act", I("sqrt", out=dst, in_=dst), waits=[ta])
            tc = B.op("dve", I("reciprocal", out=dst, in_=dst), waits=[tb])
            return ta, tc

        S_c = B.new_sem("s_cst")
        B.op("pool", I("dma_start", out=cbf, in_=cstbf_d[:, :]), sem=S_c, amt=16)
        B.op("sp", I("dma_start", out=csts, in_=csts_d[:, :]), sem=S_c, amt=16)
        t_cst = (S_c, 32)
        t_cbf = t_cst
        t_z = B.op("dve", I("memset", small, 0.0))
        t_z = B.op("dve", I("memset", small2, 0.0))

        dbg_toks = []
        S_dbg = B.new_sem("s_dbg")

        def dump(name, src_ap, waits):
            dbg_toks.append(B.op("sp", I("dma_start", out=dbg[name], in_=src_ap), waits=waits,
                                 sem=S_dbg, amt=16))

        def maybe_stop(level):
            if stop_after <= level:
                raise _Stop()

        out_toks = []
        try:

            A.reset(m_const)
            wk = A.alloc([16, 1024], BF16)
            wvv = A.alloc([16, 1024], BF16)
            wc = A.alloc([16, 384], BF16)
            wkn_a = A.alloc([2, 1024], BF16)
            wkvv_a = A.alloc([2, 1024], BF16)
            xa = [A.alloc([16, 512], BF16) for _ in range(2)]
            hTa = [A.alloc([16, 512], BF16) for _ in range(2)]
            sqa = A.alloc([8, 512], BF16)
            rstdb_a = A.alloc([512], F32)
            rcol = A.alloc([8], F32)
            ckvf = A.alloc([2, 512], F32)
            sq2 = [A.alloc([512], BF16) for _ in range(2)]
            rstdkv = A.alloc([512], F32)
            ckvn_a = A.alloc([2, 512], BF16)
            kstA = [A.alloc([512], BF16) for _ in range(2)]
            vstA = [A.alloc([512], BF16) for _ in range(2)]
            posi_a = A.alloc([512], I32, parts=64)
            posf_a = A.alloc([512], F32, parts=64)
            ang_a = A.alloc([512], F32, parts=64)
            ry_a = A.alloc([512], F32, parts=64)
            rk_a = A.alloc([512], F32, parts=64)
            cos_a = A.alloc([512], F32, parts=64)
            sin_a = A.alloc([512], F32, parts=64)
            rt1a = A.alloc([512], F32, parts=64)
            rt2a = A.alloc([512], F32, parts=64)
            krot_a = A.alloc([512], BF16, parts=64)

            def wsrc(ap_):
                return ap_.rearrange("(c p) n -> p c n", p=128)

            S_wa = B.new_sem("s_wa")
            B.op("sp", I("dma_start", out=pnwc, in_=pnwc_d[:, :]), sem=S_wa, amt=16)
            B.op("pool", I("dma_start", out=wk, in_=wsrc(win_d[:, 1024:2048])), sem=S_wa, amt=16)
            B.op("pool", I("dma_start", out=wc[:, :, 0:320], in_=wsrc(win_d[:, 4608:4928])), sem=S_wa, amt=16)
            B.op("pool", I("dma_start", out=wc[:, :, 320:384], in_=wsrc(wkrs_d[:, :])), sem=S_wa, amt=16)
            B.op("pool", I("dma_start", out=wvv, in_=wsrc(win_d[:, 2048:3072])), sem=S_wa, amt=16)
            B.op("pool", I("dma_start", out=wkn_a, in_=wsrc(wkn_d[:, :])), sem=S_wa, amt=16)
            B.op("pool", I("dma_start", out=wkvv_a, in_=wsrc(wkvv_d[:, :])), sem=S_wa, amt=16)
            t_wa = (S_wa, 7 * 16)

            banksA = Banks(PS[0:6])
            SSb = PS[6]
            CSb = PS[7]
            ss_free = [None]
            cs_free = [None]
            pro = {}
            S_xa = [B.new_sem("s_xa%d" % i) for i in range(2)]
            S_pa = B.new_sem("s_posa")
            S_ka = [B.new_sem("s_ka%d" % i) for i in range(2)]
            S_va = [B.new_sem("s_va%d" % i) for i in range(2)]
            S_kra = B.new_sem("s_kra")
            xa_free = [[], []]
            hTa_free = [None, None]
            sqa_free = [None]
            posi_free = [None]
            kstA_free = [None, None]
            vstA_free = [None, None]
            krotA_free = [None]
            kA = [0]
            vA = [0]
            PI = math.pi
            INV2PI = 1.0 / (2.0 * PI)
            C1 = 6.28125
            C2 = 2.0 * PI - C1
            invf = csts[0:64, 6:7]
            sgn = csts[0:64, 7:8]
            NT = SEQ // 512

            def mmg(out_ap, lhs_fn, rhs_fn, nch, waits):
                tok = None
                for c in range(nch):
                    tok = B.op("pe", I("matmul", out_ap, lhsT=lhs_fn(c), rhs=rhs_fn(c),
                                       start=(c == 0), stop=(c == nch - 1)),
                               waits=waits if c == 0 else [], inc=(c == nch - 1))
                return tok

            def kst_out(bank, t_mm, mul_rstd, dst_ap, t_rb_):
                ks = kA[0] % 2
                kA[0] += 1
                if mul_rstd:
                    t_ev = B.op("dve", I("tensor_tensor", out=kstA[ks], in0=bank, in1=rstdb_a, op=ALU.mult),
                                waits=[t_mm, t_rb_, kstA_free[ks]])
                else:
                    t_ev = B.op("act", I("copy", out=kstA[ks], in_=bank), waits=[t_mm, kstA_free[ks]])
                t_d = B.op("sp", I("dma_start", out=dst_ap, in_=kstA[ks]), waits=[t_ev], sem=S_ka[ks], amt=16)
                kstA_free[ks] = t_d
                return t_ev

            def vst_out(bank, t_mm, scale_ap, dst_ap, t_rc_):
                vs = vA[0] % 2
                vA[0] += 1
                if scale_ap is not None:
                    t_ev = B.op("act", I("activation", out=vstA[vs], in_=bank, func=AF.Copy, scale=scale_ap),
                                waits=[t_mm, t_rc_, vstA_free[vs]])
                else:
                    t_ev = B.op("dve", I("tensor_copy", out=vstA[vs], in_=bank), waits=[t_mm, vstA_free[vs]])
                t_d = B.op("sp", I("dma_start", out=dst_ap, in_=vstA[vs].rearrange("s (h d) -> s h d", h=4)),
                           waits=[t_ev], sem=S_va[vs], amt=16)
                vstA_free[vs] = t_d
                return t_ev

            def prologue(T):
                sl = T % 2
                t0 = T * 512
                t_xa = B.op("pool", I("dma_start", out=xa[sl], in_=xT_d[:, t0:t0 + 512].rearrange("(c p) t -> p c t", p=128)),
                            waits=xa_free[sl], sem=S_xa[sl], amt=16)
                t_pos = B.op("sp", I("dma_start", out=posi_a, in_=posa_d[:, t0:t0 + 512]), waits=[posi_free[0]],
                             sem=S_pa, amt=16)
                t_sq = None
                t_on = None
                t_cs = None
                for half in range(2):
                    t_sq = B.op("act", I("activation", out=sqa, in_=xa[sl][:, 8 * half:8 * half + 8, :], func=AF.Square),
                                waits=[t_xa, sqa_free[0]])
                    for c in range(8):
                        t_on = B.op("pe", I("matmul", SSb, lhsT=ones, rhs=sqa[:, c, :],
                                            start=(half == 0 and c == 0), stop=(half == 1 and c == 7)),
                                    waits=[t_sq, ss_free[0], t_cbf] if c == 0 else [], inc=(c == 7))
                    for tb in range(4):
                        for c in range(8):
                            t_cs = B.op("pe", I("matmul", CSb[:, 2 * tb + half:2 * tb + half + 1],
                                                lhsT=sqa[:, c, tb * 128:(tb + 1) * 128], rhs=ones[:, 0:1],
                                                start=(c == 0), stop=(c == 7)),
                                        waits=[cs_free[0]] if (c == 0 and tb == 0 and half == 0) else [], inc=(c == 7))
                    sqa_free[0] = t_cs
                t_h = B.op("dve", I("tensor_tensor", out=hTa[sl], in0=xa[sl],
                                    in1=pnwc.unsqueeze(2).to_broadcast([128, 16, 512]), op=ALU.mult),
                           waits=[t_xa, hTa_free[sl], t_wa])
                xa_free[sl] = [t_h, t_sq]
                pro[T] = (t_pos, t_on, t_cs, t_h)

            prologue(0)
            for T in range(NT):
                sl = T % 2
                t0 = T * 512
                t_pos, t_on, t_cs, t_h = pro[T]
                t_ra, t_rb = rsqrt_chain(rstdb_a, SSb, 1.0 / D, [t_on, now("pe"), now("dve")])
                ss_free[0] = t_ra
                csv = CSb[:, 0:8].rearrange("p (t h) -> p t h", h=2)
                t_c1 = B.op("dve", I("tensor_reduce", out=rcol[:, 0:4], in_=csv, axis=mybir.AxisListType.X, op=ALU.add),
                            waits=[t_cs, now("act")])
                cs_free[0] = t_c1
                t_c2, t_rc = rsqrt_chain(rcol[:, 4:8], rcol[:, 0:4], 1.0 / D, [t_c1])
                tq = B.op("dve", I("tensor_copy", out=posf_a, in_=posi_a), waits=[t_pos, now("act")])
                posi_free[0] = tq
                tq = B.op("dve", I("tensor_scalar", out=ang_a, in0=posf_a, scalar1=invf, scalar2=None, op0=ALU.mult),
                          waits=[tq, t_cst])
                ty = B.op("dve", I("tensor_scalar", out=ry_a, in0=ang_a, scalar1=INV2PI, scalar2=0.5,
                                   op0=ALU.mult, op1=ALU.add), waits=[tq])
                tk = B.op("dve", I("tensor_copy", out=posi_a, in_=ry_a), waits=[ty])
                tkf = B.op("dve", I("tensor_copy", out=rk_a, in_=posi_a), waits=[tk])
                posi_free[0] = tkf
                tg = B.op("dve", I("tensor_tensor", out=ry_a, in0=rk_a, in1=ry_a, op=ALU.is_gt), waits=[tkf])
                tm = B.op("dve", I("tensor_tensor", out=rk_a, in0=rk_a, in1=ry_a, op=ALU.subtract), waits=[tg])
                tr1 = B.op("dve", I("scalar_tensor_tensor", out=ang_a, in0=rk_a, scalar=-C1, in1=ang_a,
                                    op0=ALU.mult, op1=ALU.add), waits=[tm])
                tr2 = B.op("dve", I("scalar_tensor_tensor", out=ang_a, in0=rk_a, scalar=-C2, in1=ang_a,
                                    op0=ALU.mult, op1=ALU.add), waits=[tr1])
                tc1 = B.op("dve", I("tensor_scalar", out=ang_a, in0=ang_a, scalar1=PI, scalar2=-PI,
                                    op0=ALU.min, op1=ALU.max), waits=[tr2])
                ts3 = B.op("act", I("activation", out=sin_a, in_=ang_a, func=AF.Sin), waits=[tc1, now("dve")])
                ts4 = B.op("dve", I("tensor_scalar", out=sin_a, in0=sin_a, scalar1=sgn, scalar2=None, op0=ALU.mult),
                           waits=[ts3])
                t5 = B.op("dve", I("tensor_scalar", out=rk_a, in0=ang_a, scalar1=0.5 * PI, scalar2=None, op0=ALU.add),
                          waits=[ts3, tm])
                t5b = B.op("dve", I("tensor_single_scalar", out=ry_a, in_=rk_a, scalar=PI, op=ALU.is_gt), waits=[t5])
                t6 = B.op("dve", I("scalar_tensor_tensor", out=rk_a, in0=ry_a, scalar=-2.0 * PI, in1=rk_a,
                                   op0=ALU.mult, op1=ALU.add), waits=[t5b])
                t6b = B.op("dve", I("tensor_scalar", out=rk_a, in0=rk_a, scalar1=PI, scalar2=-PI,
                                    op0=ALU.min, op1=ALU.max), waits=[t6])
                ts7 = B.op("act", I("activation", out=cos_a, in_=rk_a, func=AF.Sin), waits=[t6b])
                t_mm = None
                for h in range(8):
                    bi, bank, bfree = banksA.get()
                    t_mm = mmg(bank, lambda c, h=h: wk[:, c, h * 128:(h + 1) * 128], lambda c: hTa[sl][:, c, :], 16,
                               [t_h, t_wa, bfree])
                    t_ev = kst_out(bank, t_mm, True, kt_sb_d[h * 128:(h + 1) * 128, t0:t0 + 512], t_rb)
                    banksA.release(bi, t_ev)
                if T + 1 < NT:
                    prologue(T + 1)
                lat = []
                for j in range(2):
                    bi, bank, bfree = banksA.get()
                    t_mm = mmg(bank, lambda c, j=j: wc[:, c, j * 128:(j + 1) * 128], lambda c: hTa[sl][:, c, :], 16,
                               [t_h, t_wa, bfree])
                    lat.append((bi, bank, t_mm))
                b1, bk1, f1 = banksA.get()
                t_m1 = mmg(bk1[0:64, :], lambda c: wc[:, c, 256:320], lambda c: hTa[sl][:, c, :], 16, [t_h, t_wa, f1])
                b2, bk2, f2 = banksA.get()
                t_m2 = mmg(bk2[0:64, :], lambda c: wc[:, c, 320:384], lambda c: hTa[sl][:, c, :], 16, [t_h, t_wa, f2])
                bs2, SS2, fs2 = banksA.get()
                t_o2 = None
                for j in range(2):
                    bi, bank, t_mm = lat[j]
                    t_f = B.op("dve", I("tensor_tensor", out=ckvf[:, j, :], in0=bank, in1=rstdb_a, op=ALU.mult),
                               waits=[t_mm, t_rb, now("pe")])
                    banksA.release(bi, t_f)
                    t_s2 = B.op("act", I("activation", out=sq2[j], in_=ckvf[:, j, :], func=AF.Square), waits=[t_f, now("pe")])
                    t_o2 = B.op("pe", I("matmul", SS2, lhsT=ones, rhs=sq2[j], start=(j == 0), stop=(j == 1)),
                                waits=[t_s2, fs2 if j == 0 else None])
                t_ka, t_kb = rsqrt_chain(rstdkv, SS2, 1.0 / 256.0, [t_o2, now("dve")])
                banksA.release(bs2, t_ka)
                t_n = None
                for j in range(2):
                    t_n = B.op("dve", I("scalar_tensor_tensor", out=ckvn_a[:, j, :], in0=ckvf[:, j, :],
                                        scalar=csts[:, 4 + j:5 + j], in1=rstdkv, op0=ALU.mult, op1=ALU.mult),
                               waits=[t_kb, now("pe"), t_cst])
                tr_1 = B.op("dve", I("tensor_tensor", out=rt1a, in0=bk1[0:64, :], in1=cos_a, op=ALU.mult),
                            waits=[t_m1, ts7])
                banksA.release(b1, tr_1)
                tr_2 = B.op("dve", I("tensor_tensor", out=rt2a, in0=bk2[0:64, :], in1=sin_a, op=ALU.mult),
                            waits=[t_m2, ts4])
                banksA.release(b2, tr_2)
                tr_3 = B.op("dve", I("tensor_tensor", out=rt1a, in0=rt1a, in1=rt2a, op=ALU.add), waits=[tr_1, tr_2])
                tr_4 = B.op("dve", I("tensor_tensor", out=krot_a, in0=rt1a, in1=rstdb_a[0:64, :], op=ALU.mult),
                            waits=[tr_3, t_rb, krotA_free[0]])
                krotA_free[0] = B.op("sp", I("dma_start", out=kr_d[:, t0:t0 + 512], in_=krot_a), waits=[tr_4],
                                     sem=S_kra, amt=16)
                for tb in range(4):
                    gb = 4 * T + tb
                    for half in range(2):
                        bi, bank, bfree = banksA.get()
                        t_mm = mmg(bank, lambda c, tb=tb: hTa[sl][:, c, tb * 128:(tb + 1) * 128],
                                   lambda c, half=half: wvv[:, c, half * 512:(half + 1) * 512], 16, [t_h, t_wa, bfree])
                        dview = v_sb_d[half * 512:(half + 1) * 512, gb * 128:(gb + 1) * 128].rearrange(
                            "(h s) d -> s h d", s=128)
                        t_ev = vst_out(bank, t_mm, rcol[:, 4 + tb:5 + tb], dview, t_rc)
                        banksA.release(bi, t_ev)
                hTa_free[sl] = t_mm
                for h in range(8):
                    bi, bank, bfree = banksA.get()
                    t_mm = mmg(bank, lambda c, h=h: wkn_a[:, c, h * 128:(h + 1) * 128], lambda c: ckvn_a[:, c, :], 2,
                               [t_n, t_wa, bfree])
                    t_ev = kst_out(bank, t_mm, False, kt_ml_d[h * 128:(h + 1) * 128, t0:t0 + 512], None)
                    banksA.release(bi, t_ev)
                for tb in range(4):
                    gb = 4 * T + tb
                    for half in range(2):
                        bi, bank, bfree = banksA.get()
                        t_mm = mmg(bank, lambda c, tb=tb: ckvn_a[:, c, tb * 128:(tb + 1) * 128],
                                   lambda c, half=half: wkvv_a[:, c, half * 512:(half + 1) * 512], 2, [t_n, t_wa, bfree])
                        dview = v_ml_d[half * 512:(half + 1) * 512, gb * 128:(gb + 1) * 128].rearrange(
                            "(h s) d -> s h d", s=128)
                        t_ev = vst_out(bank, t_mm, None, dview, None)
                        banksA.release(bi, t_ev)
            phA_all = all_now() + [(S_ka[0], S_ka[0][1]), (S_ka[1], S_ka[1][1]), (S_va[0], S_va[0][1]),
                                   (S_va[1], S_va[1][1]), (S_kra, S_kra[1])]
            t_ag_sb = None
            t_ag_mla = None
            A.reset(m_persist)
            maybe_stop(-1)

            hT = A.alloc([16, TL], BF16)
            cosT = A.alloc([TL], F32, parts=64)
            sinS = A.alloc([TL], F32, parts=64)
            cosTq = A.alloc([TL], F32, parts=64)
            sinSq = A.alloc([TL], F32, parts=64)
            m_ph1 = A.mark()
            wg = [A.alloc([8192], BF16) for i in range(2)]
            cqn = A.alloc([4, TL], BF16)
            ckvn = A.alloc([2, TL], BF16)
            kst = [A.alloc([TL], BF16) for i in range(2)]
            vst = [A.alloc([512], BF16) for i in range(2)]
            sq = [A.alloc([512], BF16) for i in range(2)]
            rstdb = A.alloc([512], F32)
            rt1 = A.alloc([512], F32, parts=64)
            rt2 = A.alloc([512], F32, parts=64)
            krot = A.alloc([TL], BF16, parts=64)
            A.reset(m_ph1)

            posi = A.alloc([TL], I32, parts=64)
            posf = A.alloc([TL], F32, parts=64)
            ang = A.alloc([TL], F32, parts=64)
            rr_y = A.alloc([TL], F32, parts=64)
            rr_k = A.alloc([TL], F32, parts=64)
            xt = [A.alloc([D], F32) for i in range(2)]
            xn = [A.alloc([D], BF16) for i in range(2)]
            junk = A.alloc([D], BF16)
            pnwb = A.alloc([D], F32)

            S_p = B.new_sem("s_pos")
            B.op("sp", I("dma_start", out=posi, in_=posb_d[:, :]), waits=phA_all, sem=S_p, amt=16)
            B.op("sp", I("dma_start", out=pnwb, in_=pnwb_d[:, :]), waits=phA_all, sem=S_p, amt=16)
            t_pnw = (S_p, 32)
            t = B.op("dve", I("tensor_copy", out=posf, in_=posi), waits=[t_pnw])
            invf = csts[0:64, 6:7]
            sgn = csts[0:64, 7:8]
            PI = math.pi
            INV2PI = 1.0 / (2.0 * PI)
            C1 = 6.28125
            C2 = 2.0 * PI - C1
            t1 = B.op("dve", I("tensor_scalar", out=ang, in0=posf, scalar1=invf, scalar2=None, op0=ALU.mult),
                      waits=[t, t_cst])
            ty = B.op("dve", I("tensor_scalar", out=rr_y, in0=ang, scalar1=INV2PI, scalar2=0.5,
                                                        op0=ALU.mult, op1=ALU.add), waits=[t1])
            tk = B.op("dve", I("tensor_copy", out=posi, in_=rr_y), waits=[ty])
            tkf = B.op("dve", I("tensor_copy", out=rr_k, in_=posi), waits=[tk])
            tg = B.op("dve", I("tensor_tensor", out=rr_y, in0=rr_k, in1=rr_y, op=ALU.is_gt), waits=[tkf])
            tm = B.op("dve", I("tensor_tensor", out=rr_k, in0=rr_k, in1=rr_y, op=ALU.subtract), waits=[tg])
            tr1 = B.op("dve", I("scalar_tensor_tensor", out=ang, in0=rr_k, scalar=-C1, in1=ang,
                                                                op0=ALU.mult, op1=ALU.add), waits=[tm])
            tr2 = B.op("dve", I("scalar_tensor_tensor", out=ang, in0=rr_k, scalar=-C2, in1=ang,
                                                                op0=ALU.mult, op1=ALU.add), waits=[tr1])
            tc1 = B.op("dve", I("tensor_scalar", out=ang, in0=ang, scalar1=PI, scalar2=-PI,
                                                         op0=ALU.min, op1=ALU.max), waits=[tr2])
            t3 = B.op("act", I("activation", out=sinS, in_=ang, func=AF.Sin), waits=[tc1])
            t4 = B.op("dve", I("tensor_scalar", out=sinS, in0=sinS, scalar1=sgn, scalar2=None, op0=ALU.mult),
                      waits=[t3])
            t_sinq = B.op("dve", I("tensor_scalar", out=sinSq, in0=sinS, scalar1=SC_MLA, scalar2=None, op0=ALU.mult),
                          waits=[t4])
            t5 = B.op("dve", I("tensor_scalar", out=rr_k, in0=ang, scalar1=0.5 * PI, scalar2=None, op0=ALU.add),
                      waits=[t3, tm])
            t5b = B.op("dve", I("tensor_single_scalar", out=rr_y, in_=rr_k, scalar=PI, op=ALU.is_gt), waits=[t5])
            t6 = B.op("dve", I("scalar_tensor_tensor", out=rr_k, in0=rr_y, scalar=-2.0 * PI, in1=rr_k,
                                                               op0=ALU.mult, op1=ALU.add), waits=[t5b])
            t6b = B.op("dve", I("tensor_scalar", out=rr_k, in0=rr_k, scalar1=PI, scalar2=-PI,
                                                         op0=ALU.min, op1=ALU.max), waits=[t6])
            t7 = B.op("act", I("activation", out=cosT, in_=rr_k, func=AF.Sin), waits=[t6b])
            t_cosq = B.op("dve", I("tensor_scalar", out=cosTq, in0=cosT, scalar1=SC_MLA, scalar2=None, op0=ALU.mult),
                          waits=[t7])
            t_tabs = [t4, t_sinq, t7, t_cosq]

            S_x = [B.new_sem("s_x%d" % i) for i in range(2)]
            xn_free = [None, None]
            xt_free = [None, None]
            tpb = [PS[0].bitcast(BF16), PS[1].bitcast(BF16)]
            tp_free = [None, None]
            tpi = 0
            hT_toks = []
            for b in range(NB):
                s = b % 2
                t_x = B.op("sp", I("dma_start", out=xt[s], in_=x_d[b * 128:(b + 1) * 128, :]),
                           waits=[xt_free[s]] + phA_all, sem=S_x[s], amt=16)
                t_sq = B.op("act", I("activation", out=junk, in_=xt[s], func=AF.Square,
                                                                   accum_out=small[:, b:b + 1]), waits=[t_x, t_z])
                t_r1, t_r2 = rsqrt_chain(small[:, 16 + b:17 + b], small[:, b:b + 1], 1.0 / D, [t_sq])
                t_xn = B.op("dve", I("scalar_tensor_tensor",
                    out=xn[s], in0=xt[s], scalar=small[:, 16 + b:17 + b], in1=pnwb,
                    op0=ALU.mult, op1=ALU.mult), waits=[t_r2, t_x, t_pnw, xn_free[s]])
                xt_free[s] = t_xn
                for g in range(4):
                    ti = tpi % 2
                    tpi += 1
                    for j in range(4):
                        c = 4 * g + j
                        t_tp = B.op("pe", I("transpose",
                            out=tpb[ti][:, j * 128:(j + 1) * 128], in_=xn[s][:, c * 128:(c + 1) * 128], identity=ident),
                            waits=[t_xn, tp_free[ti], t_cbf] if j == 0 else [], inc=(j == 3))
                    src = tpb[ti][:, 0:512].rearrange("p (j t) -> p j t", j=4)
                    dst = hT[:, 4 * g:4 * g + 4, b * 128:(b + 1) * 128]
                    if g % 2 == 0:
                        t_ev = B.op("act", I("copy", out=dst, in_=src), waits=[t_tp])
                    else:
                        t_ev = B.op("dve", I("tensor_copy", out=dst, in_=src), waits=[t_tp])
                    tp_free[ti] = t_ev
                    hT_toks.append(t_ev)
                xn_free[s] = t_tp
            ph0_done = hT_toks[-8:] + t_tabs
            if stop_after <= 0:
                dump("d_hT", hT.rearrange("p c t -> p (c t)"), ph0_done)
                for i_, tb_ in enumerate((cosT, sinS, cosTq, sinSq)):
                    dump("d_tab", tb_, ph0_done) if False else dbg_toks.append(B.op(
                        "sp", I("dma_start", out=dbg["d_tab"][:, i_ * TL:(i_ + 1) * TL], in_=tb_),
                        waits=ph0_done, sem=S_dbg, amt=16))
            maybe_stop(0)

            banks = Banks(PS)
            banks.free[0] = tp_free[0]
            banks.free[1] = tp_free[1]
            S_w = [B.new_sem("s_w%d" % i) for i in range(2)]
            wg_free = [None, None]
            wslot = [0]

            def load_w(parts):
                s = wslot[0] % 2
                wslot[0] += 1
                tok = None
                for (src, dst) in parts:
                    tok = B.op("pool", I("dma_start", out=dst, in_=src),
                               waits=[wg_free[s]] + ph0_done, sem=S_w[s], amt=16)
                return s, tok

            def wview(s, off, nch, width):
                return wg[s][:, off:off + nch * width].rearrange("p (c n) -> p c n", c=nch)

            def mm_group(out_ap, lhs_fn, rhs_fn, nch, waits):
                tok = None
                for c in range(nch):
                    tok = B.op("pe", I("matmul", out_ap, lhsT=lhs_fn(c), rhs=rhs_fn(c),
                                                              start=(c == 0), stop=(c == nch - 1)),
                               waits=waits if c == 0 else [], inc=(c == nch - 1))
                return tok

            S_k = [B.new_sem("s_kst%d" % i) for i in range(2)]
            S_v = [B.new_sem("s_vst%d" % i) for i in range(2)]
            kst_free = [None, None]
            vst_free = [None, None]
            kcnt = [0]
            vcnt = [0]
            snd_sb_toks = []
            snd_mla_toks = []

            def win_cols(c0, n):
                return win_d[:, c0:c0 + n].rearrange("(c p) n -> p c n", p=128)

            def proj_k_heads(wv, t_w, s_w, nheads, head0, dst, toks, nch, rhs_tile, rdy):
                for hh in range(nheads):
                    h = head0 + hh
                    ks = kcnt[0] % 2
                    kcnt[0] += 1
                    evs = []
                    for tt in range(2):
                        bi, bank, bfree = banks.get()
                        t_mm = mm_group(bank, lambda c, hh=hh: wv[:, c, hh * 128:(hh + 1) * 128],
                                        lambda c, tt=tt: rhs_tile[:, c, tt * 512:(tt + 1) * 512], nch,
                                        [t_w, bfree] + rdy)
                        t_ev = B.op("act", I("copy",
                            out=kst[ks][:, tt * 512:(tt + 1) * 512], in_=bank), waits=[t_mm, kst_free[ks]])
                        banks.release(bi, t_ev)
                        evs.append(t_ev)
                    wg_free[s_w] = t_mm
                    r0 = h * 128
                    t_d = B.op("sp", I("dma_start", out=dst[r0:r0 + 128, :], in_=kst[ks]),
                               waits=evs, sem=S_k[ks], amt=16)
                    kst_free[ks] = t_d
                    toks.append(t_d)

            def proj_v_tok(wv, t_w, s_w, ch, nch, lhs_tile, dst, row0, toks, rdy):
                for b in range(NB):
                    bi, bank, bfree = banks.get()
                    t_mm = mm_group(bank, lambda c, b=b: lhs_tile[:, c, b * 128:(b + 1) * 128],
                                    lambda c: wv[:, c, :], nch, [t_w, bfree] + rdy)
                    vs = vcnt[0] % 2
                    vcnt[0] += 1
                    t_ev = B.op("dve", I("tensor_copy", out=vst[vs], in_=bank),
                                waits=[t_mm, vst_free[vs]])
                    banks.release(bi, t_ev)
                    r0 = row0 + ch * 512
                    dview = dst[r0:r0 + 512, b * 128:(b + 1) * 128].rearrange("(h s) d -> s h d", s=128)
                    t_d = B.op("sp", I("dma_start",
                        out=dview, in_=vst[vs].rearrange("s (h d) -> s h d", h=4)),
                        waits=[t_ev], sem=S_v[vs], amt=16)
                    vst_free[vs] = t_d
                    toks.append(t_d)
                wg_free[s_w] = t_mm

            sqfree = [None, None]
            rstd_free = [None]
            rope_free = [None]

            def latent_norm(wv, t_w, ncb, nfeat, nw_col0, dst, extra_fn=None):
                last = None
                for tt in range(2):
                    lb = []
                    for j in range(ncb):
                        bi, bank, bfree = banks.get()
                        t_mm = mm_group(bank, lambda c, j=j: wv[:, c, j * 128:(j + 1) * 128],
                                        lambda c, tt=tt: hT[:, c, tt * 512:(tt + 1) * 512], 16,
                                        [t_w, bfree] + ph0_done)
                        lb.append((bi, bank, t_mm))
                    ex = extra_fn(tt) if extra_fn is not None else None
                    bs, ssb, ssfree = banks.get()
                    t_o = None
                    for j in range(ncb):
                        bi, bank, t_mm = lb[j]
                        t_s = B.op("act", I("activation", out=sq[j % 2], in_=bank, func=AF.Square),
                                   waits=[t_mm, sqfree[j % 2]])
                        t_o = B.op("pe", I("matmul", ssb, lhsT=ones, rhs=sq[j % 2],
                                                                  start=(j == 0), stop=(j == ncb - 1)),
                                   waits=[t_s, ssfree if j == 0 else None, t_cbf])
                        sqfree[j % 2] = t_o
                    t_a, t_b = rsqrt_chain(rstdb, ssb, 1.0 / nfeat, [t_o, rstd_free[0]])
                    banks.release(bs, t_a)
                    for j in range(ncb):
                        bi, bank, t_mm = lb[j]
                        t_n = B.op("dve", I("scalar_tensor_tensor",
                            out=dst[:, j, tt * 512:(tt + 1) * 512], in0=bank, scalar=csts[:, nw_col0 + j:nw_col0 + j + 1],
                            in1=rstdb, op0=ALU.mult, op1=ALU.mult), waits=[t_b, t_mm, t_cst])
                        banks.release(bi, t_n)
                        last = t_n
                    rstd_free[0] = last
                    if ex is not None:
                        ex()
                return last

            def rope(pa, pb, ta, tb, ct, sn, dst_ap):
                t1 = B.op("dve", I("tensor_tensor", out=rt1, in0=pa, in1=ct, op=ALU.mult),
                          waits=[ta, rope_free[0]] + t_tabs)
                t2 = B.op("dve", I("tensor_tensor", out=rt2, in0=pb, in1=sn, op=ALU.mult),
                          waits=[tb] + t_tabs)
                t3 = B.op("dve", I("tensor_tensor", out=dst_ap, in0=rt1, in1=rt2, op=ALU.add),
                          waits=[t1, t2])
                rope_free[0] = t3
                return t1, t2, t3

            def proj_fm_keep(col0, dst, chunk0, func, scale):
                toks = []
                for gi in range(2):
                    s_w = wslot[0] % 2
                    s_w, t_w = load_w([(win_cols(col0 + gi * 512, 512), wview(s_w, 0, 16, 512))])
                    wv = wview(s_w, 0, 16, 512)
                    for hh in range(4):
                        ch = chunk0 + gi * 4 + hh
                        for tt in range(2):
                            bi, bank, bfree = banks.get()
                            t_mm = mm_group(bank, lambda c, hh=hh, wv=wv: wv[:, c, hh * 128:(hh + 1) * 128],
                                            lambda c, tt=tt: hT[:, c, tt * 512:(tt + 1) * 512], 16,
                                            [t_w, bfree] + ph0_done)
                            t_ev = B.op("act", I("activation",
                                out=dst[:, ch, tt * 512:(tt + 1) * 512], in_=bank, func=func, scale=scale), waits=[t_mm])
                            banks.release(bi, t_ev)
                            toks.append(t_ev)
                    wg_free[s_w] = t_mm
                return toks

            proj_fm_keep(0, qT_sb, 0, AF.Copy, SC_SB)
            proj_fm_keep(3072, gate, 0, AF.Silu, 1.0)
            proj_fm_keep(4928, gate, 8, AF.Silu, 1.0)

            s_w = wslot[0] % 2
            s_w, t_w = load_w([(win_cols(4096, 512), wview(s_w, 0, 16, 512))])
            t_cqn = latent_norm(wview(s_w, 0, 16, 512), t_w, 4, 512, 0, cqn, None)
            wg_free[s_w] = now("pe")

            s_w = wslot[0] % 2
            s_w, t_w = load_w([
                (wqn_d[:, :].rearrange("(c p) n -> p c n", p=128), wview(s_w, 0, 4, 1024)),
                (wqr_d[:, :].rearrange("(c p) n -> p c n", p=128), wview(s_w, 4096, 4, 512)),
                (wqrs_d[:, :].rearrange("(c p) n -> p c n", p=128), wview(s_w, 6144, 4, 512)),
            ])
            t_w = (S_w[s_w], S_w[s_w][1])
            wqn_v = wview(s_w, 0, 4, 1024)
            wqr_v = wview(s_w, 4096, 4, 512)
            wqrs_v = wview(s_w, 6144, 4, 512)
            for h in range(8):
                for tt in range(2):
                    bi, bank, bfree = banks.get()
                    t_mm = mm_group(bank, lambda c, h=h: wqn_v[:, c, h * 128:(h + 1) * 128],
                                    lambda c, tt=tt: cqn[:, c, tt * 512:(tt + 1) * 512], 4, [t_w, bfree, t_cqn])
                    t_ev = B.op("act", I("activation",
                        out=qN[:, h, tt * 512:(tt + 1) * 512], in_=bank, func=AF.Copy, scale=SC_MLA), waits=[t_mm])
                    banks.release(bi, t_ev)
                    b1, bk1, f1 = banks.get()
                    t_m1 = mm_group(bk1[0:64, :], lambda c, h=h: wqr_v[:, c, h * 64:(h + 1) * 64],
                                    lambda c, tt=tt: cqn[:, c, tt * 512:(tt + 1) * 512], 4, [t_w, f1, t_cqn])
                    b2, bk2, f2 = banks.get()
                    t_m2 = mm_group(bk2[0:64, :], lambda c, h=h: wqrs_v[:, c, h * 64:(h + 1) * 64],
                                    lambda c, tt=tt: cqn[:, c, tt * 512:(tt + 1) * 512], 4, [t_w, f2, t_cqn])
                    t1, t2, t3 = rope(bk1[0:64, :], bk2[0:64, :], t_m1, t_m2, cosTq[:, tt * 512:(tt + 1) * 512],
                                      sinSq[:, tt * 512:(tt + 1) * 512], qR[:, h, tt * 512:(tt + 1) * 512])
                    banks.release(b1, t1)
                    banks.release(b2, t2)
            ph1_all = all_now()

            if debug:
                dump("d_hT", hT.rearrange("p c t -> p (c t)"), ph1_all)
                dump("d_qsb", qT_sb.rearrange("p c t -> p (c t)"), ph1_all)
                dump("d_qn", qN.rearrange("p c t -> p (c t)"), ph1_all)
                dump("d_qr", qR.rearrange("p c t -> p (c t)"), ph1_all)
                dump("d_gate", gate.rearrange("p c t -> p (c t)"), ph1_all)
                dump("d_cqn", cqn.rearrange("p c t -> p (c t)"), ph1_all)
                ph1_all = ph1_all + [(S_dbg, S_dbg[1])]
            maybe_stop(1)

            if True:
                A.reset(m_persist)
                kbuf = [A.alloc([64, 128], BF16) for _ in range(2)]
                vbuf = [A.alloc([64, 128], BF16) for _ in range(2)]
                krbuf = A.alloc([64, 128], BF16, parts=64)
                e_t = [A.alloc([2, 512], F32) for _ in range(2)]
                sp_t = [A.alloc([2, 512], BF16) for _ in range(3)]
                a_t = [A.alloc([2, 512], BF16) for _ in range(3)]
                Rt = A.alloc([512], BF16)
                rec = A.alloc([512], F32)
                otmp = A.alloc([512], F32)
                m_ph2 = A.mark()

                ZP = [psA[:, 1024 * p_:1024 * (p_ + 1)].rearrange("p (b n) -> p b n", b=2) for p_ in range(3)]
                OACC = [PS[6], PS[6]]
                DEN = [PS[7], PS[7]]
                zfree = [None] * 3
                oacc_free = [None, None]
                den_free = [None, None]
                spfree = [[None, None] for _ in range(3)]
                afree = [None] * 3
                S_kv = [B.new_sem("s_kv%d" % i) for i in range(2)]
                kv_free = [None, None]
                S_krb = B.new_sem("s_krb")
                t_krb = B.op("sp", I("dma_start", out=krbuf, in_=kr_d.rearrange("f (g s) -> f g s", s=128)),
                             waits=ph1_all + phA_all, sem=S_krb, amt=16)

                def load_head(hi):
                    s = hi % 2
                    h = hi % 8
                    if hi < 8:
                        ksrc = kt_sb_d[h * 128:(h + 1) * 128, :]
                        vsrc = v_sb_d[h * 128:(h + 1) * 128, :]
                    else:
                        ksrc = kt_ml_d[h * 128:(h + 1) * 128, :]
                        vsrc = v_ml_d[h * 128:(h + 1) * 128, :]
                    B.op("sp", I("dma_start", out=kbuf[s], in_=ksrc.rearrange("p (g s) -> p g s", s=128)),
                         waits=[kv_free[s]] + ph1_all + phA_all, sem=S_kv[s], amt=16)
                    B.op("sp", I("dma_start", out=vbuf[s], in_=vsrc.rearrange("p (g s) -> p g s", s=128)),
                         waits=[kv_free[s]] + ph1_all + phA_all, sem=S_kv[s], amt=16)
                    return (S_kv[s], S_kv[s][1])

                cnt = {"z": 0, "e": 0, "sp": 0, "a": 0, "st": 0}
                mixed_toks = []
                kv_tok = {}
                kv_tok[0] = load_head(0)
                for hi in range(16):
                    s = hi % 2
                    h = hi % 8
                    is_sb = hi < 8
                    if hi + 1 < 16:
                        kv_tok[hi + 1] = load_head(hi + 1)
                    t_kv = kv_tok[hi]
                    for u in range(2):
                        blocks = [(g, rp) for g in range(4 * u + 3, -1, -1) for rp in range(7, -1, -1)]
                        n = len(blocks)
                        oi = 0
                        cnt["st"] += 1
                        oacc = OACC[oi]
                        den = DEN[oi]
                        info = [None] * n
                        t_Rz = None
                        if is_sb:
                            t_Rz = B.op("dve", I("memset", Rt, 0.0), waits=[now("pe")] + ph1_all)
                        tR_prev = [t_Rz]

                        npair = n // 2

                        def geo(P):
                            g, rp = blocks[2 * P]
                            c0 = 128 * max(0, g - 4 * u)
                            return g, c0, (g >= 4 * u)

                        def stA(P):
                            g, c0, diag = geo(P)
                            zi = cnt["z"] % 3
                            cnt["z"] += 1
                            d = {"zi": zi}
                            info[P] = d
                            qc = slice(512 * u + c0, 512 * u + 512)
                            t = None
                            for b_ in range(2):
                                g_, rp = blocks[2 * P + b_]
                                zb = ZP[zi][:, b_, :]
                                kblk = kbuf[s][:, 8 * g + rp, :]
                                w = ([zfree[zi], t_kv] + ph1_all) if b_ == 0 else []
                                if is_sb:
                                    t = B.op("pe", I("matmul", zb[:, c0:512], lhsT=kblk, rhs=qT_sb[:, h, qc],
                                                     start=True, stop=False), waits=w, inc=not diag)
                                    if diag:
                                        t = B.op("pe", I("matmul", zb[:, c0:c0 + 128], lhsT=ident,
                                                         rhs=msk_sb[:, rp * 128:(rp + 1) * 128],
                                                         start=False, stop=False, skip_group_check=True))
                                else:
                                    B.op("pe", I("matmul", zb[:, c0:512], lhsT=kblk, rhs=qN[:, h, qc],
                                                 start=True, stop=False), waits=w, inc=False)
                                    t = B.op("pe", I("matmul", zb[:, c0:512], lhsT=krbuf[:, 8 * g + rp, :],
                                                     rhs=qR[:, h, qc], start=False, stop=not diag),
                                             waits=[t_krb])
                                    if diag:
                                        t = B.op("pe", I("matmul", zb[:, c0:c0 + 128], lhsT=ident,
                                                         rhs=msk_mla[:, rp * 128:(rp + 1) * 128],
                                                         start=False, stop=True, skip_group_check=True))
                            d["tz"] = t

                        def stB(P):
                            g, c0, diag = geo(P)
                            d = info[P]
                            zp = ZP[d["zi"]][:, :, c0:512]
                            if is_sb:
                                ei = cnt["e"] % 2
                                cnt["e"] += 1
                                si = cnt["sp"] % 3
                                cnt["sp"] += 1
                                d["si"] = si
                                t_e = B.op("act", I("activation", out=e_t[ei][:, :, c0:512], in_=zp, func=AF.Exp),
                                           waits=[d["tz"]])
                                t_s = B.op("act", I("activation", out=sp_t[si][:, :, c0:512], in_=e_t[ei][:, :, c0:512],
                                                    func=AF.Ln, bias=1.0, scale=1.0),
                                           waits=[t_e] + spfree[si])
                                d["tsp"] = t_s
                            else:
                                ai = cnt["a"] % 3
                                cnt["a"] += 1
                                d["ai"] = ai
                                t_a = B.op("act", I("activation", out=a_t[ai][:, :, c0:512], in_=zp, func=AF.Exp),
                                           waits=[d["tz"], afree[ai]])
                                d["ta"] = t_a
                                zfree[d["zi"]] = t_a

                        def stC(P):
                            if not is_sb:
                                return
                            g, c0, diag = geo(P)
                            d = info[P]
                            zi = d["zi"]
                            si = d["si"]
                            first = (P == 0)
                            zb0 = ZP[zi][:, 0, c0:512]
                            zb1 = ZP[zi][:, 1, c0:512]
                            sp0 = sp_t[si][:, 0, c0:512]
                            sp1 = sp_t[si][:, 1, c0:512]
                            t = B.op("pe", I("matmul", zb0, lhsT=negtri, rhs=sp0, start=False, stop=first,
                                             skip_group_check=True), waits=[d["tsp"]], inc=False)
                            if not first:
                                B.op("pe", I("matmul", zb0, lhsT=negones, rhs=Rt[:, c0:512], start=False, stop=True,
                                             skip_group_check=True), waits=[tR_prev[0]], inc=False)
                            B.op("pe", I("matmul", zb1, lhsT=negtri, rhs=sp1, start=False, stop=False,
                                         skip_group_check=True), inc=False)
                            t = B.op("pe", I("matmul", zb1, lhsT=negones, rhs=sp0, start=False, stop=first,
                                             skip_group_check=True), inc=first)
                            if not first:
                                t = B.op("pe", I("matmul", zb1, lhsT=negones, rhs=Rt[:, c0:512], start=False, stop=True,
                                                 skip_group_check=True))
                            d["tc"] = t
                            tR1 = B.op("dve", I("tensor_tensor", out=Rt[:, c0:512], in0=Rt[:, c0:512], in1=sp0, op=ALU.add),
                                       waits=[d["tsp"], t, tR_prev[0]])
                            tR2 = B.op("dve", I("tensor_tensor", out=Rt[:, c0:512], in0=Rt[:, c0:512], in1=sp1, op=ALU.add),
                                       waits=[tR1])
                            tR_prev[0] = tR2
                            spfree[si] = [t, tR2]
                            ai = cnt["a"] % 3
                            cnt["a"] += 1
                            d["ai"] = ai
                            t_a = B.op("act", I("activation", out=a_t[ai][:, :, c0:512], in_=ZP[zi][:, :, c0:512], func=AF.Exp),
                                       waits=[t, afree[ai]])
                            d["ta"] = t_a
                            zfree[zi] = t_a

                        def stF(P):
                            g, c0, diag = geo(P)
                            d = info[P]
                            ai = d["ai"]
                            t = None
                            for b_ in range(2):
                                k = 2 * P + b_
                                g_, rp = blocks[k]
                                vblk = vbuf[s][:, 8 * g + rp, :]
                                w = [d["ta"]] if b_ == 0 else []
                                if k == 0:
                                    w += [oacc_free[oi], den_free[oi]]
                                t = B.op("pe", I("matmul", oacc[:, c0:512], lhsT=vblk, rhs=a_t[ai][:, b_, c0:512],
                                                 start=(k == 0), stop=(k == n - 1), skip_group_check=True),
                                         waits=w, inc=(is_sb and b_ == 1))
                                if not is_sb:
                                    t = B.op("pe", I("matmul", den[:, c0:512], lhsT=ones, rhs=a_t[ai][:, b_, c0:512],
                                                     start=(k == 0), stop=(k == n - 1), skip_group_check=True),
                                             inc=(b_ == 1))
                            afree[ai] = t
                            d["tf"] = t

                        for step in range(npair + 3):
                            if step == 0:
                                stA(0)
                            if step + 1 < npair:
                                stA(step + 1)
                            if step < npair:
                                stB(step)
                            if 0 <= step - 1 < npair:
                                stC(step - 1)
                            if 0 <= step - 2 < npair:
                                stF(step - 2)
                        t_last = info[npair - 1]["tf"]
                        kv_free[s] = t_last
                        ch = h if is_sb else 8 + h
                        gsl = gate[:, ch, 512 * u:512 * u + 512]
                        if is_sb:
                            t_m = B.op("dve", I("tensor_tensor", out=gsl, in0=oacc, in1=gsl, op=ALU.mult),
                                       waits=[t_last] + ph1_all)
                            oacc_free[oi] = t_m
                        else:
                            t_r = B.op("dve", I("reciprocal", out=rec, in_=den), waits=[t_last, now("dve")])
                            den_free[oi] = t_r
                            t_o = B.op("dve", I("tensor_tensor", out=otmp, in0=oacc, in1=rec, op=ALU.mult),
                                       waits=[t_r])
                            oacc_free[oi] = t_o
                            t_m = B.op("dve", I("tensor_tensor", out=gsl, in0=otmp, in1=gsl, op=ALU.mult),
                                       waits=[t_o] + ph1_all)
                        mixed_toks.append(t_m)
                ph2_all = all_now()
                if debug:
                    dbg_toks.append(B.op("sp", I("dma_start", out=dbg["d_mixed"], in_=gate.rearrange("p c t -> p (c t)")),
                                         waits=ph2_all, sem=S_dbg, amt=16))
                    ph2_all = ph2_all + [(S_dbg, S_dbg[1])]

            maybe_stop(2)
            if True:
                A.reset(m_persist)
                wout = A.alloc([16, D], BF16)
                ponwb = A.alloc([D], F32)
                xres = [A.alloc([D], F32) for _ in range(2)]
                ytile = [A.alloc([D], F32) for _ in range(2)]
                junk3 = A.alloc([512], BF16)
                S_wo = B.new_sem("s_wo")
                wo_v = wout_d.rearrange("(c p) n -> p c n", p=128)
                for nq in range(4):
                    B.op("pool", I("dma_start", out=wout[:, :, nq * 512:(nq + 1) * 512],
                                                               in_=wo_v[:, :, nq * 512:(nq + 1) * 512]),
                         waits=ph2_all, sem=S_wo, amt=16)
                B.op("sp", I("dma_start", out=ponwb, in_=ponwb_d[:, :]), waits=ph2_all, sem=S_wo, amt=16)
                t_wo = (S_wo, 80)
                S_xr = [B.new_sem("s_xr%d" % i) for i in range(2)]
                S_o = [B.new_sem("s_o%d" % i) for i in range(2)]
                xr_free = [None, None]
                yt_free = [None, None]
                yb_free = [None] * 8
                out_toks = []
                for b in range(NB):
                    s = b % 2
                    t_xr = B.op("sp", I("dma_start", out=xres[s], in_=x_d[b * 128:(b + 1) * 128, :]),
                                waits=[xr_free[s]] + ph2_all, sem=S_xr[s], amt=16)
                    t_mms = []
                    for nq in range(4):
                        bi = 4 * s + nq
                        t_mm = mm_group(PS[bi], lambda c, b=b: gate[:, c, b * 128:(b + 1) * 128],
                                        lambda c, nq=nq: wout[:, c, nq * 512:(nq + 1) * 512], 16,
                                        [t_wo, yb_free[bi]] + ph2_all)
                        t_mms.append(t_mm)
                        B.op("act", I("activation",
                            out=junk3, in_=PS[bi], func=AF.Square, accum_out=small2[:, 4 * b + nq:4 * b + nq + 1]),
                            waits=[t_mm, t_z])
                    t_sq = now("act")
                    t_s1 = B.op("dve", I("tensor_reduce", out=small2[:, 32 + b:33 + b], in_=small2[:, 4 * b:4 * b + 4],
                                                                      axis=mybir.AxisListType.X, op=ALU.add), waits=[t_sq])
                    t_s2, t_s3 = rsqrt_chain(small2[:, 48 + b:49 + b], small2[:, 32 + b:33 + b], 1.0 / D, [t_s1])
                    t_y = None
                    for nq in range(4):
                        bi = 4 * s + nq
                        cs = slice(nq * 512, (nq + 1) * 512)
                        t_y1 = B.op("dve", I("scalar_tensor_tensor",
                            out=ytile[s][:, cs], in0=PS[bi], scalar=small2[:, 48 + b:49 + b], in1=ponwb[:, cs],
                            op0=ALU.mult, op1=ALU.mult), waits=[t_s3, t_mms[nq], yt_free[s], t_wo])
                        yb_free[bi] = t_y1
                        t_y = B.op("dve", I("tensor_tensor", out=ytile[s][:, cs], in0=ytile[s][:, cs],
                                                                                in1=xres[s][:, cs], op=ALU.add),
                                   waits=[t_y1, t_xr])
                    xr_free[s] = t_y
                    t_o = B.op("sp", I("dma_start", out=out_d[b * 128:(b + 1) * 128, :], in_=ytile[s]),
                               waits=[t_y], sem=S_o[s], amt=16)
                    yt_free[s] = t_o
                    out_toks.append(t_o)


        except _Stop:
            pass

        final_toks = list(dbg_toks) + out_toks[-2:]
        fw = [(t[0][0], t[1]) for t in final_toks]

        with nc.Block() as block:
            @block.tensor
            def _(e):
                B.replay(e, "pe")

            @block.scalar
            def _(e):
                B.replay(e, "act")

            @block.vector
            def _(e):
                B.replay(e, "dve")

            @block.gpsimd
            def _(e):
                B.replay(e, "pool")

            @block.sync
            def _(e):
                B.replay(e, "sp")
                for h_, v_ in fw:
                    e.wait_ge(h_, v_)
    return nc


def _host_inputs(x, positions, pre_norm_w, w_in, q_norm_w, w_q_up, kv_norm_w, w_kv_up, w_out, post_norm_w):
    f32 = np.float32
    x = np.asarray(x, f32)[0]
    pos = np.asarray(positions)[0].astype(np.int32)
    w_in = np.ascontiguousarray(np.asarray(w_in, f32)[0])
    w_q_up = np.asarray(w_q_up, f32)[0]
    w_kv_up = np.asarray(w_kv_up, f32)[0]
    w_out = np.ascontiguousarray(np.asarray(w_out, f32)[0])
    pnw = np.asarray(pre_norm_w, f32)[0]
    ponw = np.asarray(post_norm_w, f32)[0]
    qnw = np.asarray(q_norm_w, f32)[0]
    kvnw = np.asarray(kv_norm_w, f32)[0]

    w_krs = np.ascontiguousarray(np.concatenate([w_in[:, 4896:4928], w_in[:, 4864:4896]], axis=1))
    wq = w_q_up.reshape(512, 8, 192)
    w_qn = np.ascontiguousarray(wq[:, :, 0:128].reshape(512, 1024))
    w_qr = np.ascontiguousarray(wq[:, :, 128:192].reshape(512, 512))
    w_qrs = np.ascontiguousarray(np.concatenate([wq[:, :, 160:192], wq[:, :, 128:160]], axis=2).reshape(512, 512))
    wkv = w_kv_up.reshape(256, 8, 256)
    w_kn = np.ascontiguousarray(wkv[:, :, 0:128].reshape(256, 1024))
    w_kvv = np.ascontiguousarray(wkv[:, :, 128:256].reshape(256, 1024))
    pnw_b = np.ascontiguousarray(np.broadcast_to(pnw[None, :], (128, D)))
    ponw_b = np.ascontiguousarray(np.broadcast_to(ponw[None, :], (128, D)))

    inv_freq = (10000.0 ** (-np.arange(0, 64, 2, dtype=np.float32) / np.float32(64))).astype(f32)
    cst_s = np.zeros((128, 8), f32)
    cst_s[:, 0:4] = qnw.reshape(4, 128).T
    cst_s[:, 4:6] = kvnw.reshape(2, 128).T
    cst_s[0:64, 6] = np.concatenate([inv_freq, inv_freq])
    cst_s[0:64, 7] = np.concatenate([-np.ones(32, f32), np.ones(32, f32)])
    idx = np.arange(128)
    ident = np.eye(128, dtype=f32)
    negtri = -(idx[:, None] >= idx[None, :]).astype(f32)
    negones = -np.ones((128, 128), f32)
    ones = np.ones((128, 128), f32)

    xb = x.reshape(64, 128, D)
    pb = pos.reshape(64, 128)
    xT = np.ascontiguousarray(x.T)
    posa = np.ascontiguousarray(np.broadcast_to(pos[None, :], (64, SEQ))).astype(np.int32)
    pnw_c = np.ascontiguousarray(pnw.reshape(16, 128).T)
    in_maps = []
    for r in range(NCORES):
        msb = np.zeros((128, 8, 128), f32)
        mml = np.zeros((128, 8, 128), f32)
        for rp in range(8):
            if rp > r:
                msb[:, rp, :] = NEG
                mml[:, rp, :] = NEG
            elif rp == r:
                msb[:, rp, :] = np.where(idx[:, None] < idx[None, :], 0.0, NEG)
                mml[:, rp, :] = np.where((idx[:, None] // 64) <= (idx[None, :] // 64), 0.0, NEG)
        cst_bf = np.concatenate([ident, negtri, negones, ones, msb.reshape(128, 1024), mml.reshape(128, 1024)],
                                axis=1).astype(f32)
        in_maps.append({
            "x": np.ascontiguousarray(xb[r::8].reshape(TL, D)),
            "posb": np.ascontiguousarray(np.broadcast_to(pb[r::8].reshape(1, TL), (64, TL))).astype(np.int32),
            "xT": xT, "posa": posa, "pnw_c": pnw_c,
            "w_in": w_in, "w_krs": w_krs, "w_qn": w_qn, "w_qr": w_qr, "w_qrs": w_qrs,
            "w_kn": w_kn, "w_kvv": w_kvv, "w_out": w_out, "pnw_b": pnw_b, "ponw_b": ponw_b,
            "cst_s": cst_s, "cst_bf": np.ascontiguousarray(cst_bf),
        })
    return in_maps


_NC_CACHE = {}


def kernel(x, positions, pre_norm_w, w_in, q_norm_w, w_q_up, kv_norm_w, w_kv_up, w_out, post_norm_w):
    in_maps = _host_inputs(x, positions, pre_norm_w, w_in, q_norm_w, w_q_up, kv_norm_w, w_kv_up, w_out, post_norm_w)
    nc = build_program()
    res = run_bass_kernel_spmd(nc, in_maps, core_ids=list(range(NCORES)))
    out = np.zeros((64, 128, D), np.float32)
    for r in range(NCORES):
        out[r::8] = np.asarray(res.results[r]["out"], np.float32).reshape(8, 128, D)
    return out.reshape(1, SEQ, D)
```

```python
import math
from contextlib import ExitStack

import numpy as np
import concourse.bass as bass
import concourse.mybir as mybir
from concourse.bass_utils import run_bass_kernel_spmd

F32 = mybir.dt.float32
BF16 = mybir.dt.bfloat16
I32 = mybir.dt.int32
AF = mybir.ActivationFunctionType
ALU = mybir.AluOpType

NCORES = 8
D = 2048
SEQ = 8192
TL = 1024
NB = 8
DIN = 5952
EPS = 1e-6
NEG = -30000.0
SC_SB = 1.0 / math.sqrt(128.0)
SC_MLA = 1.0 / math.sqrt(192.0)
SB_ROWS = 2048
MLA_ROWS = 2112

DEBUG = False


class _Stop(Exception):
    pass


class Builder:
    def __init__(self, nc, stack):
        self.nc = nc
        self.stack = stack
        self.q = {k: [] for k in ("pe", "act", "dve", "pool", "sp")}
        self.waited = {k: {} for k in self.q}
        self.prog = {k: self.new_sem("prog_" + k) for k in ("pe", "act", "dve", "pool")}
        self.nsem = 0

    def new_sem(self, name):
        h = self.stack.enter_context(self.nc.semaphore(name))
        return [h, 0]

    def op(self, eng, fn, waits=(), sem=None, amt=1, inc=True):
        ws = []
        mx = {}
        for t in waits:
            if t is None:
                continue
            s, v = t
            if id(s) not in mx or mx[id(s)][1] < v:
                mx[id(s)] = (s, v)
        for key, (s, v) in mx.items():
            if self.waited[eng].get(key, 0) >= v:
                continue
            self.waited[eng][key] = v
            ws.append((s[0], v))
        tok = None
        incspec = None
        if inc:
            s = sem if sem is not None else self.prog[eng]
            s[1] += amt
            tok = (s, s[1])
            incspec = (s[0], amt)
        self.q[eng].append((fn, ws, incspec))
        return tok

    def replay(self, eng_obj, key):
        for fn, ws, incspec in self.q[key]:
            for h, v in ws[:-1]:
                eng_obj.wait_ge(h, v)
            ins = fn(eng_obj)
            if ws:
                ins._wait_ge(ws[-1][0], ws[-1][1])
            if incspec is not None:
                ins.then_inc(incspec[0], incspec[1])


def I(name, *args, **kw):
    return lambda e: getattr(e, name)(*args, **kw)


class Banks:
    def __init__(self, aps):
        self.aps = aps
        self.free = [None] * len(aps)
        self.i = 0

    def get(self):
        i = self.i
        self.i = (self.i + 1) % len(self.aps)
        return i, self.aps[i], self.free[i]

    def release(self, i, tok):
        self.free[i] = tok


class Arena:
    def __init__(self, t, nbytes):
        self.t = t
        self.nbytes = nbytes
        self.off = 0

    def alloc(self, shape, dt, parts=128):
        esz = 2 if dt == BF16 else 4
        n = 1
        for s in shape:
            n *= s
        nb = n * esz
        self.off = (self.off + 63) // 64 * 64
        assert self.off + nb <= self.nbytes, ("SBUF arena overflow", self.off, nb)
        ap = self.t[0:parts, self.off // 2:(self.off + nb) // 2]
        self.off += nb
        if esz == 4:
            ap = ap.bitcast(dt)
        if len(shape) == 2:
            ap = ap.rearrange("p (a b) -> p a b", a=shape[0])
        elif len(shape) == 3:
            ap = ap.rearrange("p (a b c) -> p a b c", a=shape[0], b=shape[1])
        return ap

    def mark(self):
        return self.off

    def reset(self, m):
        self.off = m


def build_program(debug=False, stop_after=9):
    nc = bass.Bass("TRN2", target_bir_lowering=False)

    def din(name, shape, dt=F32):
        return nc.dram_tensor(name, shape, dt, kind="ExternalInput").ap()

    x_d = din("x", [TL, D])
    posb_d = din("posb", [64, TL], I32)
    win_d = din("w_in", [D, DIN])
    wkrs_d = din("w_krs", [D, 64])
    wqn_d = din("w_qn", [512, 1024])
    wqr_d = din("w_qr", [512, 512])
    wqrs_d = din("w_qrs", [512, 512])
    wkn_d = din("w_kn", [256, 1024])
    wkvv_d = din("w_kvv", [256, 1024])
    wout_d = din("w_out", [D, D])
    pnwb_d = din("pnw_b", [128, D])
    ponwb_d = din("ponw_b", [128, D])
    csts_d = din("cst_s", [128, 8])
    cstbf_d = din("cst_bf", [128, 2560])
    out_d = nc.dram_tensor("out", [TL, D], F32, kind="ExternalOutput").ap()

    xT_d = din("xT", [D, SEQ])
    posa_d = din("posa", [64, SEQ], I32)
    pnwc_d = din("pnw_c", [128, 16])
    kt_sb_d = nc.dram_tensor("kt_sb", [1024, SEQ], BF16).ap()
    v_sb_d = nc.dram_tensor("v_sb", [1024, SEQ], BF16).ap()
    kt_ml_d = nc.dram_tensor("kt_ml", [1024, SEQ], BF16).ap()
    v_ml_d = nc.dram_tensor("v_ml", [1024, SEQ], BF16).ap()
    kr_d = nc.dram_tensor("kr", [64, SEQ], BF16).ap()

    dbg = {}
    if debug:
        def dout(name, shape, dt=F32):
            dbg[name] = nc.dram_tensor(name, shape, dt, kind="ExternalOutput").ap()
        dout("d_hT", [128, 16 * TL], BF16)
        dout("d_qsb", [128, 8 * TL], BF16)
        dout("d_qn", [128, 8 * TL], BF16)
        dout("d_qr", [64, 8 * TL], BF16)
        dout("d_gate", [128, 16 * TL], BF16)
        dout("d_tab", [64, 4 * TL], F32)
        dout("d_k0", [128, 8 * TL], BF16)
        dout("d_v0", [128, 8 * TL], BF16)
        dout("d_kr", [64, 8 * TL], BF16)
        dout("d_cqn", [128, 4 * TL], BF16)
        dout("d_ckvn", [128, 2 * TL], BF16)
        dout("d_kn0", [128, 8 * TL], BF16)
        dout("d_vm0", [128, 8 * TL], BF16)
        dout("d_mixed", [128, 16 * TL], BF16)

    ARENA_BYTES = 200 * 1024
    with ExitStack() as st:
        B = Builder(nc, st)
        big = st.enter_context(nc.sbuf_tensor("arena", [128, ARENA_BYTES // 2], BF16))
        A = Arena(big, ARENA_BYTES)

        cbf = A.alloc([2560], BF16)
        ident = cbf[:, 0:128]
        negtri = cbf[:, 128:256]
        negones = cbf[:, 256:384]
        ones = cbf[:, 384:512]
        msk_sb = cbf[:, 512:1536]
        msk_mla = cbf[:, 1536:2560]
        csts = A.alloc([8], F32)
        small = A.alloc([64], F32)
        small2 = A.alloc([64], F32)
        pscr = A.alloc([8], F32)
        pnwc = A.alloc([16], F32)
        m_const = A.mark()
        qT_sb = A.alloc([8, TL], BF16)
        qN = A.alloc([8, TL], BF16)
        qR = A.alloc([8, TL], BF16, parts=64)
        gate = A.alloc([16, TL], BF16)
        m_persist = A.mark()

        psA = st.enter_context(nc.psum_tensor("psA", [128, 4096], F32))
        PS = [psA[:, 512 * i:512 * (i + 1)] for i in range(8)]

        def now(eng):
            return (B.prog[eng], B.prog[eng][1])

        def all_now():
            return [now(k) for k in ("pe", "act", "dve", "pool") if B.prog[k][1] > 0]


        def rsqrt_chain(dst, src_ap, scale, waits):
            ta = B.op("dve", I("tensor_scalar", out=dst, in0=src_ap, scalar1=scale, scalar2=EPS,
                                                        op0=ALU.mult, op1=ALU.add), waits=waits)
            tb = B.op("act", I("sqrt", out=dst, in_=dst), waits=[ta])
            tc = B.op("dve", I("reciprocal", out=dst, in_=dst), waits=[tb])
            return ta, tc

        S_c = B.new_sem("s_cst")
        S_c2 = B.new_sem("s_cst2")
        B.op("pool", I("dma_start", out=cbf, in_=cstbf_d[:, :]), sem=S_c, amt=16)
        B.op("sp", I("dma_start", out=csts, in_=csts_d[:, :]), sem=S_c2, amt=16)
        t_cst = (S_c2, 16)
        t_cbf = (S_c, 16)
        t_z = B.op("dve", I("memset", small, 0.0))
        t_z = B.op("dve", I("memset", small2, 0.0))

        dbg_toks = []
        S_dbg = B.new_sem("s_dbg")

        def dump(name, src_ap, waits):
            dbg_toks.append(B.op("sp", I("dma_start", out=dbg[name], in_=src_ap), waits=waits,
                                 sem=S_dbg, amt=16))

        def maybe_stop(level):
            if stop_after <= level:
                raise _Stop()

        out_toks = []
        try:

            A.reset(m_const)
            wk = A.alloc([16, 1024], BF16)
            wvv = A.alloc([16, 1024], BF16)
            wc = A.alloc([16, 384], BF16)
            wkn_a = A.alloc([2, 1024], BF16)
            wkvv_a = A.alloc([2, 1024], BF16)
            xa = [A.alloc([16, 512], BF16) for _ in range(2)]
            hTa = [A.alloc([16, 512], BF16) for _ in range(2)]
            sqa = A.alloc([8, 512], BF16)
            rstdb_a = A.alloc([512], F32)
            rcol = A.alloc([8], F32)
            ckvf = A.alloc([2, 512], F32)
            sq2 = [A.alloc([512], BF16) for _ in range(2)]
            rstdkv = A.alloc([512], F32)
            ckvn_a = A.alloc([2, 512], BF16)
            kstA = [A.alloc([512], BF16) for _ in range(2)]
            vstA = [A.alloc([512], BF16) for _ in range(2)]
            posi_a = A.alloc([512], I32, parts=64)
            posf_a = A.alloc([512], F32, parts=64)
            ang_a = A.alloc([512], F32, parts=64)
            ry_a = A.alloc([512], F32, parts=64)
            rk_a = A.alloc([512], F32, parts=64)
            cos_a = A.alloc([512], F32, parts=64)
            sin_a = A.alloc([512], F32, parts=64)
            rt1a = A.alloc([512], F32, parts=64)
            rt2a = A.alloc([512], F32, parts=64)
            krot_a = A.alloc([512], BF16, parts=64)

            def wsrc(ap_):
                return ap_.rearrange("(c p) n -> p c n", p=128)

            S_wa = B.new_sem("s_wa")
            S_wa2 = B.new_sem("s_wa2")
            t_pnwc = B.op("sp", I("dma_start", out=pnwc, in_=pnwc_d[:, :]), sem=S_wa2, amt=16)
            B.op("pool", I("dma_start", out=wk, in_=wsrc(win_d[:, 1024:2048])), sem=S_wa, amt=16)
            B.op("pool", I("dma_start", out=wc[:, :, 0:320], in_=wsrc(win_d[:, 4608:4928])), sem=S_wa, amt=16)
            B.op("pool", I("dma_start", out=wc[:, :, 320:384], in_=wsrc(wkrs_d[:, :])), sem=S_wa, amt=16)
            B.op("pool", I("dma_start", out=wvv, in_=wsrc(win_d[:, 2048:3072])), sem=S_wa, amt=16)
            B.op("pool", I("dma_start", out=wkn_a, in_=wsrc(wkn_d[:, :])), sem=S_wa, amt=16)
            B.op("pool", I("dma_start", out=wkvv_a, in_=wsrc(wkvv_d[:, :])), sem=S_wa, amt=16)
            t_wa = (S_wa, 6 * 16)

            banksA = Banks(PS[0:6])
            SSb = PS[6]
            CSb = PS[7]
            ss_free = [None]
            cs_free = [None]
            pro = {}
            S_xa = [B.new_sem("s_xa%d" % i) for i in range(2)]
            S_pa = B.new_sem("s_posa")
            S_ka = [B.new_sem("s_ka%d" % i) for i in range(2)]
            S_va = [B.new_sem("s_va%d" % i) for i in range(2)]
            S_kra = B.new_sem("s_kra")
            xa_free = [[], []]
            hTa_free = [None, None]
            sqa_free = [None]
            posi_free = [None]
            kstA_free = [None, None]
            vstA_free = [None, None]
            krotA_free = [None]
            kA = [0]
            vA = [0]
            PI = math.pi
            INV2PI = 1.0 / (2.0 * PI)
            C1 = 6.28125
            C2 = 2.0 * PI - C1
            invf = csts[0:64, 6:7]
            sgn = csts[0:64, 7:8]
            NT = SEQ // 512

            def mmg(out_ap, lhs_fn, rhs_fn, nch, waits):
                tok = None
                for c in range(nch):
                    tok = B.op("pe", I("matmul", out_ap, lhsT=lhs_fn(c), rhs=rhs_fn(c),
                                       start=(c == 0), stop=(c == nch - 1)),
                               waits=waits if c == 0 else [], inc=(c == nch - 1))
                return tok

            def kst_out(bank, t_mm, mul_rstd, dst_ap, t_rb_):
                ks = kA[0] % 2
                kA[0] += 1
                if mul_rstd:
                    t_ev = B.op("dve", I("tensor_tensor", out=kstA[ks], in0=bank, in1=rstdb_a, op=ALU.mult),
                                waits=[t_mm, t_rb_, kstA_free[ks]])
                else:
                    t_ev = B.op("act", I("copy", out=kstA[ks], in_=bank), waits=[t_mm, kstA_free[ks]])
                t_d = B.op("sp", I("dma_start", out=dst_ap, in_=kstA[ks]), waits=[t_ev], sem=S_ka[ks], amt=16)
                kstA_free[ks] = t_d
                return t_ev

            def vst_out(bank, t_mm, scale_ap, dst_ap, t_rc_):
                vs = vA[0] % 2
                vA[0] += 1
                if scale_ap is not None:
                    t_ev = B.op("act", I("activation", out=vstA[vs], in_=bank, func=AF.Copy, scale=scale_ap),
                                waits=[t_mm, t_rc_, vstA_free[vs]])
                else:
                    t_ev = B.op("dve", I("tensor_copy", out=vstA[vs], in_=bank), waits=[t_mm, vstA_free[vs]])
                t_d = B.op("sp", I("dma_start", out=dst_ap, in_=vstA[vs].rearrange("s (h d) -> s h d", h=4)),
                           waits=[t_ev], sem=S_va[vs], amt=16)
                vstA_free[vs] = t_d
                return t_ev

            def prologue(T):
                sl = T % 2
                t0 = T * 512
                t_xa = B.op("pool", I("dma_start", out=xa[sl], in_=xT_d[:, t0:t0 + 512].rearrange("(c p) t -> p c t", p=128)),
                            waits=xa_free[sl], sem=S_xa[sl], amt=16)
                t_pos = B.op("sp", I("dma_start", out=posi_a, in_=posa_d[:, t0:t0 + 512]), waits=[posi_free[0]],
                             sem=S_pa, amt=16)
                t_sq = None
                t_on = None
                t_cs = None
                for half in range(2):
                    t_sq = B.op("act", I("activation", out=sqa, in_=xa[sl][:, 8 * half:8 * half + 8, :], func=AF.Square),
                                waits=[t_xa, sqa_free[0]])
                    for c in range(8):
                        t_on = B.op("pe", I("matmul", SSb, lhsT=ones, rhs=sqa[:, c, :],
                                            start=(half == 0 and c == 0), stop=(half == 1 and c == 7)),
                                    waits=[t_sq, ss_free[0], t_cbf] if c == 0 else [], inc=(c == 7))
                    for tb in range(4):
                        for c in range(8):
                            t_cs = B.op("pe", I("matmul", CSb[:, 2 * tb + half:2 * tb + half + 1],
                                                lhsT=sqa[:, c, tb * 128:(tb + 1) * 128], rhs=ones[:, 0:1],
                                                start=(c == 0), stop=(c == 7)),
                                        waits=[cs_free[0]] if (c == 0 and tb == 0 and half == 0) else [], inc=(c == 7))
                    sqa_free[0] = t_cs
                t_h = B.op("dve", I("tensor_tensor", out=hTa[sl], in0=xa[sl],
                                    in1=pnwc.unsqueeze(2).to_broadcast([128, 16, 512]), op=ALU.mult),
                           waits=[t_xa, hTa_free[sl], t_wa, t_pnwc])
                xa_free[sl] = [t_h, t_sq]
                pro[T] = (t_pos, t_on, t_cs, t_h)

            prologue(0)
            for T in range(NT):
                sl = T % 2
                t0 = T * 512
                t_pos, t_on, t_cs, t_h = pro[T]
                t_ra, t_rb = rsqrt_chain(rstdb_a, SSb, 1.0 / D, [t_on, now("pe"), now("dve")])
                ss_free[0] = t_ra
                csv = CSb[:, 0:8].rearrange("p (t h) -> p t h", h=2)
                t_c1 = B.op("dve", I("tensor_reduce", out=rcol[:, 0:4], in_=csv, axis=mybir.AxisListType.X, op=ALU.add),
                            waits=[t_cs, now("act")])
                cs_free[0] = t_c1
                t_c2, t_rc = rsqrt_chain(rcol[:, 4:8], rcol[:, 0:4], 1.0 / D, [t_c1])
                tq = B.op("dve", I("tensor_copy", out=posf_a, in_=posi_a), waits=[t_pos, now("act")])
                posi_free[0] = tq
                tq = B.op("dve", I("tensor_scalar", out=ang_a, in0=posf_a, scalar1=invf, scalar2=None, op0=ALU.mult),
                          waits=[tq, t_cst])
                ty = B.op("dve", I("tensor_scalar", out=ry_a, in0=ang_a, scalar1=INV2PI, scalar2=0.5,
                                   op0=ALU.mult, op1=ALU.add), waits=[tq])
                tk = B.op("dve", I("tensor_copy", out=posi_a, in_=ry_a), waits=[ty])
                tkf = B.op("dve", I("tensor_copy", out=rk_a, in_=posi_a), waits=[tk])
                posi_free[0] = tkf
                tg = B.op("dve", I("tensor_tensor", out=ry_a, in0=rk_a, in1=ry_a, op=ALU.is_gt), waits=[tkf])
                tm = B.op("dve", I("tensor_tensor", out=rk_a, in0=rk_a, in1=ry_a, op=ALU.subtract), waits=[tg])
                tr1 = B.op("dve", I("scalar_tensor_tensor", out=ang_a, in0=rk_a, scalar=-C1, in1=ang_a,
                                    op0=ALU.mult, op1=ALU.add), waits=[tm])
                tr2 = B.op("dve", I("scalar_tensor_tensor", out=ang_a, in0=rk_a, scalar=-C2, in1=ang_a,
                                    op0=ALU.mult, op1=ALU.add), waits=[tr1])
                tc1 = B.op("dve", I("tensor_scalar", out=ang_a, in0=ang_a, scalar1=PI, scalar2=-PI,
                                    op0=ALU.min, op1=ALU.max), waits=[tr2])
                ts3 = B.op("act", I("activation", out=sin_a, in_=ang_a, func=AF.Sin), waits=[tc1, now("dve")])
                ts4 = B.op("dve", I("tensor_scalar", out=sin_a, in0=sin_a, scalar1=sgn, scalar2=None, op0=ALU.mult),
                           waits=[ts3])
                t5 = B.op("dve", I("tensor_scalar", out=rk_a, in0=ang_a, scalar1=0.5 * PI, scalar2=None, op0=ALU.add),
                          waits=[ts3, tm])
                t5b = B.op("dve", I("tensor_single_scalar", out=ry_a, in_=rk_a, scalar=PI, op=ALU.is_gt), waits=[t5])
                t6 = B.op("dve", I("scalar_tensor_tensor", out=rk_a, in0=ry_a, scalar=-2.0 * PI, in1=rk_a,
                                   op0=ALU.mult, op1=ALU.add), waits=[t5b])
                t6b = B.op("dve", I("tensor_scalar", out=rk_a, in0=rk_a, scalar1=PI, scalar2=-PI,
                                    op0=ALU.min, op1=ALU.max), waits=[t6])
                ts7 = B.op("act", I("activation", out=cos_a, in_=rk_a, func=AF.Sin), waits=[t6b])
                t_mm = None
                for h in range(8):
                    bi, bank, bfree = banksA.get()
                    t_mm = mmg(bank, lambda c, h=h: wk[:, c, h * 128:(h + 1) * 128], lambda c: hTa[sl][:, c, :], 16,
                               [t_h, t_wa, bfree])
                    t_ev = kst_out(bank, t_mm, True, kt_sb_d[h * 128:(h + 1) * 128, t0:t0 + 512], t_rb)
                    banksA.release(bi, t_ev)
                if T + 1 < NT:
                    prologue(T + 1)
                lat = []
                for j in range(2):
                    bi, bank, bfree = banksA.get()
                    t_mm = mmg(bank, lambda c, j=j: wc[:, c, j * 128:(j + 1) * 128], lambda c: hTa[sl][:, c, :], 16,
                               [t_h, t_wa, bfree])
                    lat.append((bi, bank, t_mm))
                b1, bk1, f1 = banksA.get()
                t_m1 = mmg(bk1[0:64, :], lambda c: wc[:, c, 256:320], lambda c: hTa[sl][:, c, :], 16, [t_h, t_wa, f1])
                b2, bk2, f2 = banksA.get()
                t_m2 = mmg(bk2[0:64, :], lambda c: wc[:, c, 320:384], lambda c: hTa[sl][:, c, :], 16, [t_h, t_wa, f2])
                bs2, SS2, fs2 = banksA.get()
                t_o2 = None
                for j in range(2):
                    bi, bank, t_mm = lat[j]
                    t_f = B.op("dve", I("tensor_tensor", out=ckvf[:, j, :], in0=bank, in1=rstdb_a, op=ALU.mult),
                               waits=[t_mm, t_rb, now("pe")])
                    banksA.release(bi, t_f)
                    t_s2 = B.op("act", I("activation", out=sq2[j], in_=ckvf[:, j, :], func=AF.Square), waits=[t_f, now("pe")])
                    t_o2 = B.op("pe", I("matmul", SS2, lhsT=ones, rhs=sq2[j], start=(j == 0), stop=(j == 1)),
                                waits=[t_s2, fs2 if j == 0 else None])
                t_ka, t_kb = rsqrt_chain(rstdkv, SS2, 1.0 / 256.0, [t_o2, now("dve")])
                banksA.release(bs2, t_ka)
                t_n = None
                for j in range(2):
                    t_n = B.op("dve", I("scalar_tensor_tensor", out=ckvn_a[:, j, :], in0=ckvf[:, j, :],
                                        scalar=csts[:, 4 + j:5 + j], in1=rstdkv, op0=ALU.mult, op1=ALU.mult),
                               waits=[t_kb, now("pe"), t_cst])
                tr_1 = B.op("dve", I("tensor_tensor", out=rt1a, in0=bk1[0:64, :], in1=cos_a, op=ALU.mult),
                            waits=[t_m1, ts7])
                banksA.release(b1, tr_1)
                tr_2 = B.op("dve", I("tensor_tensor", out=rt2a, in0=bk2[0:64, :], in1=sin_a, op=ALU.mult),
                            waits=[t_m2, ts4])
                banksA.release(b2, tr_2)
                tr_3 = B.op("dve", I("tensor_tensor", out=rt1a, in0=rt1a, in1=rt2a, op=ALU.add), waits=[tr_1, tr_2])
                tr_4 = B.op("dve", I("tensor_tensor", out=krot_a, in0=rt1a, in1=rstdb_a[0:64, :], op=ALU.mult),
                            waits=[tr_3, t_rb, krotA_free[0]])
                krotA_free[0] = B.op("sp", I("dma_start", out=kr_d[:, t0:t0 + 512], in_=krot_a), waits=[tr_4],
                                     sem=S_kra, amt=16)
                for tb in range(4):
                    gb = 4 * T + tb
                    for half in range(2):
                        bi, bank, bfree = banksA.get()
                        t_mm = mmg(bank, lambda c, tb=tb: hTa[sl][:, c, tb * 128:(tb + 1) * 128],
                                   lambda c, half=half: wvv[:, c, half * 512:(half + 1) * 512], 16, [t_h, t_wa, bfree])
                        dview = v_sb_d[half * 512:(half + 1) * 512, gb * 128:(gb + 1) * 128].rearrange(
                            "(h s) d -> s h d", s=128)
                        t_ev = vst_out(bank, t_mm, rcol[:, 4 + tb:5 + tb], dview, t_rc)
                        banksA.release(bi, t_ev)
                hTa_free[sl] = t_mm
                for h in range(8):
                    bi, bank, bfree = banksA.get()
                    t_mm = mmg(bank, lambda c, h=h: wkn_a[:, c, h * 128:(h + 1) * 128], lambda c: ckvn_a[:, c, :], 2,
                               [t_n, t_wa, bfree])
                    t_ev = kst_out(bank, t_mm, False, kt_ml_d[h * 128:(h + 1) * 128, t0:t0 + 512], None)
                    banksA.release(bi, t_ev)
                for tb in range(4):
                    gb = 4 * T + tb
                    for half in range(2):
                        bi, bank, bfree = banksA.get()
                        t_mm = mmg(bank, lambda c, tb=tb: ckvn_a[:, c, tb * 128:(tb + 1) * 128],
                                   lambda c, half=half: wkvv_a[:, c, half * 512:(half + 1) * 512], 2, [t_n, t_wa, bfree])
                        dview = v_ml_d[half * 512:(half + 1) * 512, gb * 128:(gb + 1) * 128].rearrange(
                            "(h s) d -> s h d", s=128)
                        t_ev = vst_out(bank, t_mm, None, dview, None)
                        banksA.release(bi, t_ev)
            phA_all = all_now() + [(S_ka[0], S_ka[0][1]), (S_ka[1], S_ka[1][1]), (S_va[0], S_va[0][1]),
                                   (S_va[1], S_va[1][1]), (S_kra, S_kra[1])]
            t_ag_sb = None
            t_ag_mla = None
            A.reset(m_persist)
            maybe_stop(-1)

            hT = A.alloc([16, TL], BF16)
            cosT = A.alloc([TL], F32, parts=64)
            sinS = A.alloc([TL], F32, parts=64)
            cosTq = A.alloc([TL], F32, parts=64)
            sinSq = A.alloc([TL], F32, parts=64)
            m_ph1 = A.mark()
            wg = [A.alloc([8192], BF16) for i in range(2)]
            cqn = A.alloc([4, TL], BF16)
            ckvn = A.alloc([2, TL], BF16)
            kst = [A.alloc([TL], BF16) for i in range(2)]
            vst = [A.alloc([512], BF16) for i in range(2)]
            sq = [A.alloc([512], BF16) for i in range(2)]
            rstdb = A.alloc([512], F32)
            rt1 = A.alloc([512], F32, parts=64)
            rt2 = A.alloc([512], F32, parts=64)
            krot = A.alloc([TL], BF16, parts=64)
            A.reset(m_ph1)

            posi = A.alloc([TL], I32, parts=64)
            posf = A.alloc([TL], F32, parts=64)
            ang = A.alloc([TL], F32, parts=64)
            rr_y = A.alloc([TL], F32, parts=64)
            rr_k = A.alloc([TL], F32, parts=64)
            xt = [A.alloc([D], F32) for i in range(2)]
            xn = [A.alloc([D], BF16) for i in range(2)]
            junk = A.alloc([D], BF16)
            pnwb = A.alloc([D], F32)

            S_p = B.new_sem("s_pos")
            B.op("sp", I("dma_start", out=posi, in_=posb_d[:, :]), waits=phA_all, sem=S_p, amt=16)
            B.op("sp", I("dma_start", out=pnwb, in_=pnwb_d[:, :]), waits=phA_all, sem=S_p, amt=16)
            t_pnw = (S_p, 32)
            t = B.op("dve", I("tensor_copy", out=posf, in_=posi), waits=[t_pnw])
            invf = csts[0:64, 6:7]
            sgn = csts[0:64, 7:8]
            PI = math.pi
            INV2PI = 1.0 / (2.0 * PI)
            C1 = 6.28125
            C2 = 2.0 * PI - C1
            t1 = B.op("dve", I("tensor_scalar", out=ang, in0=posf, scalar1=invf, scalar2=None, op0=ALU.mult),
                      waits=[t, t_cst])
            ty = B.op("dve", I("tensor_scalar", out=rr_y, in0=ang, scalar1=INV2PI, scalar2=0.5,
                                                        op0=ALU.mult, op1=ALU.add), waits=[t1])
            tk = B.op("dve", I("tensor_copy", out=posi, in_=rr_y), waits=[ty])
            tkf = B.op("dve", I("tensor_copy", out=rr_k, in_=posi), waits=[tk])
            tg = B.op("dve", I("tensor_tensor", out=rr_y, in0=rr_k, in1=rr_y, op=ALU.is_gt), waits=[tkf])
            tm = B.op("dve", I("tensor_tensor", out=rr_k, in0=rr_k, in1=rr_y, op=ALU.subtract), waits=[tg])
            tr1 = B.op("dve", I("scalar_tensor_tensor", out=ang, in0=rr_k, scalar=-C1, in1=ang,
                                                                op0=ALU.mult, op1=ALU.add), waits=[tm])
            tr2 = B.op("dve", I("scalar_tensor_tensor", out=ang, in0=rr_k, scalar=-C2, in1=ang,
                                                                op0=ALU.mult, op1=ALU.add), waits=[tr1])
            tc1 = B.op("dve", I("tensor_scalar", out=ang, in0=ang, scalar1=PI, scalar2=-PI,
                                                         op0=ALU.min, op1=ALU.max), waits=[tr2])
            t3 = B.op("act", I("activation", out=sinS, in_=ang, func=AF.Sin), waits=[tc1])
            t4 = B.op("dve", I("tensor_scalar", out=sinS, in0=sinS, scalar1=sgn, scalar2=None, op0=ALU.mult),
                      waits=[t3])
            t_sinq = B.op("dve", I("tensor_scalar", out=sinSq, in0=sinS, scalar1=SC_MLA, scalar2=None, op0=ALU.mult),
                          waits=[t4])
            t5 = B.op("dve", I("tensor_scalar", out=rr_k, in0=ang, scalar1=0.5 * PI, scalar2=None, op0=ALU.add),
                      waits=[t3, tm])
            t5b = B.op("dve", I("tensor_single_scalar", out=rr_y, in_=rr_k, scalar=PI, op=ALU.is_gt), waits=[t5])
            t6 = B.op("dve", I("scalar_tensor_tensor", out=rr_k, in0=rr_y, scalar=-2.0 * PI, in1=rr_k,
                                                               op0=ALU.mult, op1=ALU.add), waits=[t5b])
            t6b = B.op("dve", I("tensor_scalar", out=rr_k, in0=rr_k, scalar1=PI, scalar2=-PI,
                                                         op0=ALU.min, op1=ALU.max), waits=[t6])
            t7 = B.op("act", I("activation", out=cosT, in_=rr_k, func=AF.Sin), waits=[t6b])
            t_cosq = B.op("dve", I("tensor_scalar", out=cosTq, in0=cosT, scalar1=SC_MLA, scalar2=None, op0=ALU.mult),
                          waits=[t7])
            t_tabs = [t4, t_sinq, t7, t_cosq]

            S_x = [B.new_sem("s_x%d" % i) for i in range(2)]
            xn_free = [None, None]
            xt_free = [None, None]
            tpb = [PS[0].bitcast(BF16), PS[1].bitcast(BF16)]
            tp_free = [None, None]
            tpi = 0
            hT_toks = []
            for b in range(NB):
                s = b % 2
                t_x = B.op("sp", I("dma_start", out=xt[s], in_=x_d[b * 128:(b + 1) * 128, :]),
                           waits=[xt_free[s]] + phA_all, sem=S_x[s], amt=16)
                t_sq = B.op("act", I("activation", out=junk, in_=xt[s], func=AF.Square,
                                                                   accum_out=small[:, b:b + 1]), waits=[t_x, t_z])
                t_r1, t_r2 = rsqrt_chain(small[:, 16 + b:17 + b], small[:, b:b + 1], 1.0 / D, [t_sq])
                t_xn = B.op("dve", I("scalar_tensor_tensor",
                    out=xn[s], in0=xt[s], scalar=small[:, 16 + b:17 + b], in1=pnwb,
                    op0=ALU.mult, op1=ALU.mult), waits=[t_r2, t_x, t_pnw, xn_free[s]])
                xt_free[s] = t_xn
                for g in range(4):
                    ti = tpi % 2
                    tpi += 1
                    for j in range(4):
                        c = 4 * g + j
                        t_tp = B.op("pe", I("transpose",
                            out=tpb[ti][:, j * 128:(j + 1) * 128], in_=xn[s][:, c * 128:(c + 1) * 128], identity=ident),
                            waits=[t_xn, tp_free[ti], t_cbf] if j == 0 else [], inc=(j == 3))
                    src = tpb[ti][:, 0:512].rearrange("p (j t) -> p j t", j=4)
                    dst = hT[:, 4 * g:4 * g + 4, b * 128:(b + 1) * 128]
                    if g % 2 == 0:
                        t_ev = B.op("act", I("copy", out=dst, in_=src), waits=[t_tp])
                    else:
                        t_ev = B.op("dve", I("tensor_copy", out=dst, in_=src), waits=[t_tp])
                    tp_free[ti] = t_ev
                    hT_toks.append(t_ev)
                xn_free[s] = t_tp
            ph0_done = hT_toks[-8:] + t_tabs
            if stop_after <= 0:
                dump("d_hT", hT.rearrange("p c t -> p (c t)"), ph0_done)
                for i_, tb_ in enumerate((cosT, sinS, cosTq, sinSq)):
                    dump("d_tab", tb_, ph0_done) if False else dbg_toks.append(B.op(
                        "sp", I("dma_start", out=dbg["d_tab"][:, i_ * TL:(i_ + 1) * TL], in_=tb_),
                        waits=ph0_done, sem=S_dbg, amt=16))
            maybe_stop(0)

            banks = Banks(PS)
            banks.free[0] = tp_free[0]
            banks.free[1] = tp_free[1]
            S_w = [B.new_sem("s_w%d" % i) for i in range(2)]
            wg_free = [None, None]
            wslot = [0]

            def load_w(parts):
                s = wslot[0] % 2
                wslot[0] += 1
                tok = None
                for (src, dst) in parts:
                    tok = B.op("pool", I("dma_start", out=dst, in_=src),
                               waits=[wg_free[s]] + ph0_done, sem=S_w[s], amt=16)
                return s, tok

            def wview(s, off, nch, width):
                return wg[s][:, off:off + nch * width].rearrange("p (c n) -> p c n", c=nch)

            def mm_group(out_ap, lhs_fn, rhs_fn, nch, waits):
                tok = None
                for c in range(nch):
                    tok = B.op("pe", I("matmul", out_ap, lhsT=lhs_fn(c), rhs=rhs_fn(c),
                                                              start=(c == 0), stop=(c == nch - 1)),
                               waits=waits if c == 0 else [], inc=(c == nch - 1))
                return tok

            S_k = [B.new_sem("s_kst%d" % i) for i in range(2)]
            S_v = [B.new_sem("s_vst%d" % i) for i in range(2)]
            kst_free = [None, None]
            vst_free = [None, None]
            kcnt = [0]
            vcnt = [0]
            snd_sb_toks = []
            snd_mla_toks = []

            def win_cols(c0, n):
                return win_d[:, c0:c0 + n].rearrange("(c p) n -> p c n", p=128)

            def proj_k_heads(wv, t_w, s_w, nheads, head0, dst, toks, nch, rhs_tile, rdy):
                for hh in range(nheads):
                    h = head0 + hh
                    ks = kcnt[0] % 2
                    kcnt[0] += 1
                    evs = []
                    for tt in range(2):
                        bi, bank, bfree = banks.get()
                        t_mm = mm_group(bank, lambda c, hh=hh: wv[:, c, hh * 128:(hh + 1) * 128],
                                        lambda c, tt=tt: rhs_tile[:, c, tt * 512:(tt + 1) * 512], nch,
                                        [t_w, bfree] + rdy)
                        t_ev = B.op("act", I("copy",
                            out=kst[ks][:, tt * 512:(tt + 1) * 512], in_=bank), waits=[t_mm, kst_free[ks]])
                        banks.release(bi, t_ev)
                        evs.append(t_ev)
                    wg_free[s_w] = t_mm
                    r0 = h * 128
                    t_d = B.op("sp", I("dma_start", out=dst[r0:r0 + 128, :], in_=kst[ks]),
                               waits=evs, sem=S_k[ks], amt=16)
                    kst_free[ks] = t_d
                    toks.append(t_d)

            def proj_v_tok(wv, t_w, s_w, ch, nch, lhs_tile, dst, row0, toks, rdy):
                for b in range(NB):
                    bi, bank, bfree = banks.get()
                    t_mm = mm_group(bank, lambda c, b=b: lhs_tile[:, c, b * 128:(b + 1) * 128],
                                    lambda c: wv[:, c, :], nch, [t_w, bfree] + rdy)
                    vs = vcnt[0] % 2
                    vcnt[0] += 1
                    t_ev = B.op("dve", I("tensor_copy", out=vst[vs], in_=bank),
                                waits=[t_mm, vst_free[vs]])
                    banks.release(bi, t_ev)
                    r0 = row0 + ch * 512
                    dview = dst[r0:r0 + 512, b * 128:(b + 1) * 128].rearrange("(h s) d -> s h d", s=128)
                    t_d = B.op("sp", I("dma_start",
                        out=dview, in_=vst[vs].rearrange("s (h d) -> s h d", h=4)),
                        waits=[t_ev], sem=S_v[vs], amt=16)
                    vst_free[vs] = t_d
                    toks.append(t_d)
                wg_free[s_w] = t_mm

            sqfree = [None, None]
            rstd_free = [None]
            rope_free = [None]

            def latent_norm(wv, t_w, ncb, nfeat, nw_col0, dst, extra_fn=None):
                last = None
                for tt in range(2):
                    lb = []
                    for j in range(ncb):
                        bi, bank, bfree = banks.get()
                        t_mm = mm_group(bank, lambda c, j=j: wv[:, c, j * 128:(j + 1) * 128],
                                        lambda c, tt=tt: hT[:, c, tt * 512:(tt + 1) * 512], 16,
                                        [t_w, bfree] + ph0_done)
                        lb.append((bi, bank, t_mm))
                    ex = extra_fn(tt) if extra_fn is not None else None
                    bs, ssb, ssfree = banks.get()
                    t_o = None
                    for j in range(ncb):
                        bi, bank, t_mm = lb[j]
                        t_s = B.op("act", I("activation", out=sq[j % 2], in_=bank, func=AF.Square),
                                   waits=[t_mm, sqfree[j % 2]])
                        t_o = B.op("pe", I("matmul", ssb, lhsT=ones, rhs=sq[j % 2],
                                                                  start=(j == 0), stop=(j == ncb - 1)),
                                   waits=[t_s, ssfree if j == 0 else None, t_cbf])
                        sqfree[j % 2] = t_o
                    t_a, t_b = rsqrt_chain(rstdb, ssb, 1.0 / nfeat, [t_o, rstd_free[0]])
                    banks.release(bs, t_a)
                    for j in range(ncb):
                        bi, bank, t_mm = lb[j]
                        t_n = B.op("dve", I("scalar_tensor_tensor",
                            out=dst[:, j, tt * 512:(tt + 1) * 512], in0=bank, scalar=csts[:, nw_col0 + j:nw_col0 + j + 1],
                            in1=rstdb, op0=ALU.mult, op1=ALU.mult), waits=[t_b, t_mm, t_cst])
                        banks.release(bi, t_n)
                        last = t_n
                    rstd_free[0] = last
                    if ex is not None:
                        ex()
                return last

            def rope(pa, pb, ta, tb, ct, sn, dst_ap):
                t1 = B.op("dve", I("tensor_tensor", out=rt1, in0=pa, in1=ct, op=ALU.mult),
                          waits=[ta, rope_free[0]] + t_tabs)
                t2 = B.op("dve", I("tensor_tensor", out=rt2, in0=pb, in1=sn, op=ALU.mult),
                          waits=[tb] + t_tabs)
                t3 = B.op("dve", I("tensor_tensor", out=dst_ap, in0=rt1, in1=rt2, op=ALU.add),
                          waits=[t1, t2])
                rope_free[0] = t3
                return t1, t2, t3

            def proj_fm_keep(col0, dst, chunk0, func, scale):
                toks = []
                for gi in range(2):
                    s_w = wslot[0] % 2
                    s_w, t_w = load_w([(win_cols(col0 + gi * 512, 512), wview(s_w, 0, 16, 512))])
                    wv = wview(s_w, 0, 16, 512)
                    for hh in range(4):
                        ch = chunk0 + gi * 4 + hh
                        for tt in range(2):
                            bi, bank, bfree = banks.get()
                            t_mm = mm_group(bank, lambda c, hh=hh, wv=wv: wv[:, c, hh * 128:(hh + 1) * 128],
                                            lambda c, tt=tt: hT[:, c, tt * 512:(tt + 1) * 512], 16,
                                            [t_w, bfree] + ph0_done)
                            t_ev = B.op("act", I("activation",
                                out=dst[:, ch, tt * 512:(tt + 1) * 512], in_=bank, func=func, scale=scale), waits=[t_mm])
                            banks.release(bi, t_ev)
                            toks.append(t_ev)
                    wg_free[s_w] = t_mm
                return toks

            proj_fm_keep(0, qT_sb, 0, AF.Copy, SC_SB)
            proj_fm_keep(3072, gate, 0, AF.Silu, 1.0)
            proj_fm_keep(4928, gate, 8, AF.Silu, 1.0)

            s_w = wslot[0] % 2
            s_w, t_w = load_w([(win_cols(4096, 512), wview(s_w, 0, 16, 512))])
            t_cqn = latent_norm(wview(s_w, 0, 16, 512), t_w, 4, 512, 0, cqn, None)
            wg_free[s_w] = now("pe")

            s_w = wslot[0] % 2
            s_w, t_w = load_w([
                (wqn_d[:, :].rearrange("(c p) n -> p c n", p=128), wview(s_w, 0, 4, 1024)),
                (wqr_d[:, :].rearrange("(c p) n -> p c n", p=128), wview(s_w, 4096, 4, 512)),
                (wqrs_d[:, :].rearrange("(c p) n -> p c n", p=128), wview(s_w, 6144, 4, 512)),
            ])
            t_w = (S_w[s_w], S_w[s_w][1])
            wqn_v = wview(s_w, 0, 4, 1024)
            wqr_v = wview(s_w, 4096, 4, 512)
            wqrs_v = wview(s_w, 6144, 4, 512)
            for h in range(8):
                for tt in range(2):
                    bi, bank, bfree = banks.get()
                    t_mm = mm_group(bank, lambda c, h=h: wqn_v[:, c, h * 128:(h + 1) * 128],
                                    lambda c, tt=tt: cqn[:, c, tt * 512:(tt + 1) * 512], 4, [t_w, bfree, t_cqn])
                    t_ev = B.op("act", I("activation",
                        out=qN[:, h, tt * 512:(tt + 1) * 512], in_=bank, func=AF.Copy, scale=SC_MLA), waits=[t_mm])
                    banks.release(bi, t_ev)
                    b1, bk1, f1 = banks.get()
                    t_m1 = mm_group(bk1[0:64, :], lambda c, h=h: wqr_v[:, c, h * 64:(h + 1) * 64],
                                    lambda c, tt=tt: cqn[:, c, tt * 512:(tt + 1) * 512], 4, [t_w, f1, t_cqn])
                    b2, bk2, f2 = banks.get()
                    t_m2 = mm_group(bk2[0:64, :], lambda c, h=h: wqrs_v[:, c, h * 64:(h + 1) * 64],
                                    lambda c, tt=tt: cqn[:, c, tt * 512:(tt + 1) * 512], 4, [t_w, f2, t_cqn])
                    t1, t2, t3 = rope(bk1[0:64, :], bk2[0:64, :], t_m1, t_m2, cosTq[:, tt * 512:(tt + 1) * 512],
                                      sinSq[:, tt * 512:(tt + 1) * 512], qR[:, h, tt * 512:(tt + 1) * 512])
                    banks.release(b1, t1)
                    banks.release(b2, t2)
            ph1_all = all_now()

            if debug:
                dump("d_hT", hT.rearrange("p c t -> p (c t)"), ph1_all)
                dump("d_qsb", qT_sb.rearrange("p c t -> p (c t)"), ph1_all)
                dump("d_qn", qN.rearrange("p c t -> p (c t)"), ph1_all)
                dump("d_qr", qR.rearrange("p c t -> p (c t)"), ph1_all)
                dump("d_gate", gate.rearrange("p c t -> p (c t)"), ph1_all)
                dump("d_cqn", cqn.rearrange("p c t -> p (c t)"), ph1_all)
                ph1_all = ph1_all + [(S_dbg, S_dbg[1])]
            maybe_stop(1)

            if True:
                A.reset(m_persist)
                kbuf = [A.alloc([64, 128], BF16) for _ in range(2)]
                vbuf = [A.alloc([64, 128], BF16) for _ in range(2)]
                krbuf = A.alloc([64, 128], BF16, parts=64)
                e_t = [A.alloc([2, 512], F32) for _ in range(2)]
                sp_t = [A.alloc([2, 512], BF16) for _ in range(3)]
                a_t = [A.alloc([2, 512], BF16) for _ in range(3)]
                Rt = A.alloc([512], BF16)
                rec = A.alloc([512], F32)
                otmp = A.alloc([512], F32)
                pacc = A.alloc([512], F32)
                pbf = A.alloc([512], BF16)
                m_ph2 = A.mark()

                ZP = [psA[:, 1024 * p_:1024 * (p_ + 1)].rearrange("p (b n) -> p b n", b=2) for p_ in range(3)]
                OACC = [PS[6], PS[6]]
                DEN = [PS[7], PS[7]]
                zfree = [None] * 3
                oacc_free = [None, None]
                den_free = [None, None]
                spfree = [[None, None] for _ in range(3)]
                afree = [None] * 3
                adve = [None] * 3
                S_kv = [B.new_sem("s_kv%d" % i) for i in range(2)]
                kv_free = [None, None]
                S_krb = B.new_sem("s_krb")
                t_krb = B.op("sp", I("dma_start", out=krbuf, in_=kr_d.rearrange("f (g s) -> f g s", s=128)),
                             waits=ph1_all + phA_all, sem=S_krb, amt=16)

                def load_head(hi):
                    s = hi % 2
                    h = hi % 8
                    if hi < 8:
                        ksrc = kt_sb_d[h * 128:(h + 1) * 128, :]
                        vsrc = v_sb_d[h * 128:(h + 1) * 128, :]
                    else:
                        ksrc = kt_ml_d[h * 128:(h + 1) * 128, :]
                        vsrc = v_ml_d[h * 128:(h + 1) * 128, :]
                    B.op("sp", I("dma_start", out=kbuf[s], in_=ksrc.rearrange("p (g s) -> p g s", s=128)),
                         waits=[kv_free[s]] + ph1_all + phA_all, sem=S_kv[s], amt=16)
                    B.op("sp", I("dma_start", out=vbuf[s], in_=vsrc.rearrange("p (g s) -> p g s", s=128)),
                         waits=[kv_free[s]] + ph1_all + phA_all, sem=S_kv[s], amt=16)
                    return (S_kv[s], S_kv[s][1])

                cnt = {"z": 0, "e": 0, "sp": 0, "a": 0, "st": 0}
                mixed_toks = []
                kv_tok = {}
                kv_tok[0] = load_head(0)
                for hi in range(16):
                    s = hi % 2
                    h = hi % 8
                    is_sb = hi < 8
                    if hi + 1 < 16:
                        kv_tok[hi + 1] = load_head(hi + 1)
                    t_kv = kv_tok[hi]
                    for u in range(2):
                        blocks = [(g, rp) for g in range(4 * u + 3, -1, -1) for rp in range(7, -1, -1)]
                        n = len(blocks)
                        oi = 0
                        cnt["st"] += 1
                        oacc = OACC[oi]
                        den = DEN[oi]
                        info = [None] * n
                        t_Rz = None
                        if is_sb:
                            t_Rz = B.op("dve", I("memset", Rt, 0.0), waits=[now("pe")] + ph1_all)
                        tR_prev = [t_Rz]
                        tP_prev = [None]
                        if not is_sb:
                            tP_prev[0] = B.op("dve", I("memset", pacc, 0.0), waits=[now("pe")] + ph1_all)

                        npair = n // 2

                        def geo(P):
                            g, rp = blocks[2 * P]
                            c0 = 128 * max(0, g - 4 * u)
                            return g, c0, (g >= 4 * u)

                        def stA(P):
                            g, c0, diag = geo(P)
                            zi = cnt["z"] % 3
                            cnt["z"] += 1
                            d = {"zi": zi}
                            info[P] = d
                            qc = slice(512 * u + c0, 512 * u + 512)
                            t = None
                            for b_ in range(2):
                                g_, rp = blocks[2 * P + b_]
                                zb = ZP[zi][:, b_, :]
                                kblk = kbuf[s][:, 8 * g + rp, :]
                                w = ([zfree[zi], t_kv] + ph1_all) if b_ == 0 else []
                                if is_sb:
                                    t = B.op("pe", I("matmul", zb[:, c0:512], lhsT=kblk, rhs=qT_sb[:, h, qc],
                                                     start=True, stop=False), waits=w, inc=not diag)
                                    if diag:
                                        t = B.op("pe", I("matmul", zb[:, c0:c0 + 128], lhsT=ident,
                                                         rhs=msk_sb[:, rp * 128:(rp + 1) * 128],
                                                         start=False, stop=False, skip_group_check=True))
                                else:
                                    B.op("pe", I("matmul", zb[:, c0:512], lhsT=kblk, rhs=qN[:, h, qc],
                                                 start=True, stop=False), waits=w, inc=False)
                                    t = B.op("pe", I("matmul", zb[:, c0:512], lhsT=krbuf[:, 8 * g + rp, :],
                                                     rhs=qR[:, h, qc], start=False, stop=not diag),
                                             waits=[t_krb])
                                    if diag:
                                        t = B.op("pe", I("matmul", zb[:, c0:c0 + 128], lhsT=ident,
                                                         rhs=msk_mla[:, rp * 128:(rp + 1) * 128],
                                                         start=False, stop=True, skip_group_check=True))
                            d["tz"] = t

                        def stB(P):
                            g, c0, diag = geo(P)
                            d = info[P]
                            zp = ZP[d["zi"]][:, :, c0:512]
                            if is_sb:
                                ei = cnt["e"] % 2
                                cnt["e"] += 1
                                si = cnt["sp"] % 3
                                cnt["sp"] += 1
                                d["si"] = si
                                t_e = B.op("act", I("activation", out=e_t[ei][:, :, c0:512], in_=zp, func=AF.Exp),
                                           waits=[d["tz"]])
                                t_s = B.op("act", I("activation", out=sp_t[si][:, :, c0:512], in_=e_t[ei][:, :, c0:512],
                                                    func=AF.Ln, bias=1.0, scale=1.0),
                                           waits=[t_e] + spfree[si])
                                d["tsp"] = t_s
                            else:
                                ai = cnt["a"] % 3
                                cnt["a"] += 1
                                d["ai"] = ai
                                t_a = B.op("act", I("activation", out=a_t[ai][:, :, c0:512], in_=zp, func=AF.Exp),
                                           waits=[d["tz"], afree[ai], adve[ai]])
                                d["ta"] = t_a
                                zfree[d["zi"]] = t_a
                                tp1 = B.op("dve", I("tensor_tensor", out=pacc[:, c0:512], in0=pacc[:, c0:512],
                                                    in1=a_t[ai][:, 0, c0:512], op=ALU.add), waits=[t_a, tP_prev[0]])
                                tp2 = B.op("dve", I("tensor_tensor", out=pacc[:, c0:512], in0=pacc[:, c0:512],
                                                    in1=a_t[ai][:, 1, c0:512], op=ALU.add), waits=[tp1])
                                tP_prev[0] = tp2
                                adve[ai] = tp2

                        def stC(P):
                            if not is_sb:
                                return
                            g, c0, diag = geo(P)
                            d = info[P]
                            zi = d["zi"]
                            si = d["si"]
                            first = (P == 0)
                            zb0 = ZP[zi][:, 0, c0:512]
                            zb1 = ZP[zi][:, 1, c0:512]
                            sp0 = sp_t[si][:, 0, c0:512]
                            sp1 = sp_t[si][:, 1, c0:512]
                            t = B.op("pe", I("matmul", zb0, lhsT=negtri, rhs=sp0, start=False, stop=first,
                                             skip_group_check=True), waits=[d["tsp"]], inc=False)
                            if not first:
                                B.op("pe", I("matmul", zb0, lhsT=negones, rhs=Rt[:, c0:512], start=False, stop=True,
                                             skip_group_check=True), waits=[tR_prev[0]], inc=False)
                            B.op("pe", I("matmul", zb1, lhsT=negtri, rhs=sp1, start=False, stop=False,
                                         skip_group_check=True), inc=False)
                            t = B.op("pe", I("matmul", zb1, lhsT=negones, rhs=sp0, start=False, stop=first,
                                             skip_group_check=True), inc=first)
                            if not first:
                                t = B.op("pe", I("matmul", zb1, lhsT=negones, rhs=Rt[:, c0:512], start=False, stop=True,
                                                 skip_group_check=True))
                            d["tc"] = t
                            tR1 = B.op("dve", I("tensor_tensor", out=Rt[:, c0:512], in0=Rt[:, c0:512], in1=sp0, op=ALU.add),
                                       waits=[d["tsp"], t, tR_prev[0]])
                            tR2 = B.op("dve", I("tensor_tensor", out=Rt[:, c0:512], in0=Rt[:, c0:512], in1=sp1, op=ALU.add),
                                       waits=[tR1])
                            tR_prev[0] = tR2
                            spfree[si] = [t, tR2]
                            ai = cnt["a"] % 3
                            cnt["a"] += 1
                            d["ai"] = ai
                            t_a = B.op("act", I("activation", out=a_t[ai][:, :, c0:512], in_=ZP[zi][:, :, c0:512], func=AF.Exp),
                                       waits=[t, afree[ai]])
                            d["ta"] = t_a
                            zfree[zi] = t_a

                        def stF(P):
                            g, c0, diag = geo(P)
                            d = info[P]
                            ai = d["ai"]
                            t = None
                            for b_ in range(2):
                                k = 2 * P + b_
                                g_, rp = blocks[k]
                                vblk = vbuf[s][:, 8 * g + rp, :]
                                w = [d["ta"]] if b_ == 0 else []
                                if k == 0:
                                    w += [oacc_free[oi], den_free[oi]]
                                t = B.op("pe", I("matmul", oacc[:, c0:512], lhsT=vblk, rhs=a_t[ai][:, b_, c0:512],
                                                 start=(k == 0), stop=(k == n - 1), skip_group_check=True),
                                         waits=w, inc=(b_ == 1))
                            afree[ai] = t
                            d["tf"] = t

                        for step in range(npair + 3):
                            if step == 0:
                                stA(0)
                            if step + 1 < npair:
                                stA(step + 1)
                            if step < npair:
                                stB(step)
                            if 0 <= step - 1 < npair:
                                stC(step - 1)
                            if 0 <= step - 2 < npair:
                                stF(step - 2)
                        t_last = info[npair - 1]["tf"]
                        kv_free[s] = t_last
                        ch = h if is_sb else 8 + h
                        gsl = gate[:, ch, 512 * u:512 * u + 512]
                        if is_sb:
                            t_m = B.op("dve", I("tensor_tensor", out=gsl, in0=oacc, in1=gsl, op=ALU.mult),
                                       waits=[t_last] + ph1_all)
                            oacc_free[oi] = t_m
                        else:
                            t_cv = B.op("dve", I("tensor_copy", out=pbf, in_=pacc), waits=[tP_prev[0]])
                            t_dn = B.op("pe", I("matmul", den, lhsT=ones, rhs=pbf, start=True, stop=True),
                                        waits=[t_cv, den_free[oi]])
                            t_r = B.op("dve", I("reciprocal", out=rec, in_=den), waits=[t_last, t_dn])
                            den_free[oi] = t_r
                            t_o = B.op("dve", I("tensor_tensor", out=otmp, in0=oacc, in1=rec, op=ALU.mult),
                                       waits=[t_r])
                            oacc_free[oi] = t_o
                            t_m = B.op("dve", I("tensor_tensor", out=gsl, in0=otmp, in1=gsl, op=ALU.mult),
                                       waits=[t_o] + ph1_all)
                        mixed_toks.append(t_m)
                ph2_all = all_now()
                if debug:
                    dbg_toks.append(B.op("sp", I("dma_start", out=dbg["d_mixed"], in_=gate.rearrange("p c t -> p (c t)")),
                                         waits=ph2_all, sem=S_dbg, amt=16))
                    ph2_all = ph2_all + [(S_dbg, S_dbg[1])]

            maybe_stop(2)
            if True:
                A.reset(m_persist)
                wout = A.alloc([16, D], BF16)
                ponwb = A.alloc([D], F32)
                xres = [A.alloc([D], F32) for _ in range(2)]
                ytile = [A.alloc([D], F32) for _ in range(2)]
                junk3 = A.alloc([512], BF16)
                S_wo = B.new_sem("s_wo")
                wo_v = wout_d.rearrange("(c p) n -> p c n", p=128)
                for nq in range(4):
                    B.op("pool", I("dma_start", out=wout[:, :, nq * 512:(nq + 1) * 512],
                                                               in_=wo_v[:, :, nq * 512:(nq + 1) * 512]),
                         waits=ph2_all, sem=S_wo, amt=16)
                S_wo2 = B.new_sem("s_wo2")
                t_pon = B.op("sp", I("dma_start", out=ponwb, in_=ponwb_d[:, :]), waits=ph2_all, sem=S_wo2, amt=16)
                t_wo = (S_wo, 64)
                S_xr = [B.new_sem("s_xr%d" % i) for i in range(2)]
                S_o = [B.new_sem("s_o%d" % i) for i in range(2)]
                xr_free = [None, None]
                yt_free = [None, None]
                yb_free = [None] * 8
                out_toks = []
                for b in range(NB):
                    s = b % 2
                    t_xr = B.op("sp", I("dma_start", out=xres[s], in_=x_d[b * 128:(b + 1) * 128, :]),
                                waits=[xr_free[s]] + ph2_all, sem=S_xr[s], amt=16)
                    t_mms = []
                    for nq in range(4):
                        bi = 4 * s + nq
                        t_mm = mm_group(PS[bi], lambda c, b=b: gate[:, c, b * 128:(b + 1) * 128],
                                        lambda c, nq=nq: wout[:, c, nq * 512:(nq + 1) * 512], 16,
                                        [t_wo, yb_free[bi]] + ph2_all)
                        t_mms.append(t_mm)
                        B.op("act", I("activation",
                            out=junk3, in_=PS[bi], func=AF.Square, accum_out=small2[:, 4 * b + nq:4 * b + nq + 1]),
                            waits=[t_mm, t_z])
                    t_sq = now("act")
                    t_s1 = B.op("dve", I("tensor_reduce", out=small2[:, 32 + b:33 + b], in_=small2[:, 4 * b:4 * b + 4],
                                                                      axis=mybir.AxisListType.X, op=ALU.add), waits=[t_sq])
                    t_s2, t_s3 = rsqrt_chain(small2[:, 48 + b:49 + b], small2[:, 32 + b:33 + b], 1.0 / D, [t_s1])
                    t_y = None
                    for nq in range(4):
                        bi = 4 * s + nq
                        cs = slice(nq * 512, (nq + 1) * 512)
                        t_y1 = B.op("dve", I("scalar_tensor_tensor",
                            out=ytile[s][:, cs], in0=PS[bi], scalar=small2[:, 48 + b:49 + b], in1=ponwb[:, cs],
                            op0=ALU.mult, op1=ALU.mult), waits=[t_s3, t_mms[nq], yt_free[s], t_wo, t_pon])
                        yb_free[bi] = t_y1
                        t_y = B.op("dve", I("tensor_tensor", out=ytile[s][:, cs], in0=ytile[s][:, cs],
                                                                                in1=xres[s][:, cs], op=ALU.add),
                                   waits=[t_y1, t_xr])
                    xr_free[s] = t_y
                    t_o = B.op("sp", I("dma_start", out=out_d[b * 128:(b + 1) * 128, :], in_=ytile[s]),
                               waits=[t_y], sem=S_o[s], amt=16)
                    yt_free[s] = t_o
                    out_toks.append(t_o)


        except _Stop:
            pass

        final_toks = list(dbg_toks) + out_toks[-2:]
        fw = [(t[0][0], t[1]) for t in final_toks]

        with nc.Block() as block:
            @block.tensor
            def _(e):
                B.replay(e, "pe")

            @block.scalar
            def _(e):
                B.replay(e, "act")

            @block.vector
            def _(e):
                B.replay(e, "dve")

            @block.gpsimd
            def _(e):
                B.replay(e, "pool")

            @block.sync
            def _(e):
                B.replay(e, "sp")
                for h_, v_ in fw:
                    e.wait_ge(h_, v_)
    return nc


def _host_inputs(x, positions, pre_norm_w, w_in, q_norm_w, w_q_up, kv_norm_w, w_kv_up, w_out, post_norm_w):
    f32 = np.float32
    x = np.asarray(x, f32)[0]
    pos = np.asarray(positions)[0].astype(np.int32)
    w_in = np.ascontiguousarray(np.asarray(w_in, f32)[0])
    w_q_up = np.asarray(w_q_up, f32)[0]
    w_kv_up = np.asarray(w_kv_up, f32)[0]
    w_out = np.ascontiguousarray(np.asarray(w_out, f32)[0])
    pnw = np.asarray(pre_norm_w, f32)[0]
    ponw = np.asarray(post_norm_w, f32)[0]
    qnw = np.asarray(q_norm_w, f32)[0]
    kvnw = np.asarray(kv_norm_w, f32)[0]

    w_krs = np.ascontiguousarray(np.concatenate([w_in[:, 4896:4928], w_in[:, 4864:4896]], axis=1))
    wq = w_q_up.reshape(512, 8, 192)
    w_qn = np.ascontiguousarray(wq[:, :, 0:128].reshape(512, 1024))
    w_qr = np.ascontiguousarray(wq[:, :, 128:192].reshape(512, 512))
    w_qrs = np.ascontiguousarray(np.concatenate([wq[:, :, 160:192], wq[:, :, 128:160]], axis=2).reshape(512, 512))
    wkv = w_kv_up.reshape(256, 8, 256)
    w_kn = np.ascontiguousarray(wkv[:, :, 0:128].reshape(256, 1024))
    w_kvv = np.ascontiguousarray(wkv[:, :, 128:256].reshape(256, 1024))
    pnw_b = np.ascontiguousarray(np.broadcast_to(pnw[None, :], (128, D)))
    ponw_b = np.ascontiguousarray(np.broadcast_to(ponw[None, :], (128, D)))

    inv_freq = (10000.0 ** (-np.arange(0, 64, 2, dtype=np.float32) / np.float32(64))).astype(f32)
    cst_s = np.zeros((128, 8), f32)
    cst_s[:, 0:4] = qnw.reshape(4, 128).T
    cst_s[:, 4:6] = kvnw.reshape(2, 128).T
    cst_s[0:64, 6] = np.concatenate([inv_freq, inv_freq])
    cst_s[0:64, 7] = np.concatenate([-np.ones(32, f32), np.ones(32, f32)])
    idx = np.arange(128)
    ident = np.eye(128, dtype=f32)
    negtri = -(idx[:, None] >= idx[None, :]).astype(f32)
    negones = -np.ones((128, 128), f32)
    ones = np.ones((128, 128), f32)

    xb = x.reshape(64, 128, D)
    pb = pos.reshape(64, 128)
    xT = np.ascontiguousarray(x.T)
    posa = np.ascontiguousarray(np.broadcast_to(pos[None, :], (64, SEQ))).astype(np.int32)
    pnw_c = np.ascontiguousarray(pnw.reshape(16, 128).T)
    in_maps = []
    for r in range(NCORES):
        msb = np.zeros((128, 8, 128), f32)
        mml = np.zeros((128, 8, 128), f32)
        for rp in range(8):
            if rp > r:
                msb[:, rp, :] = NEG
                mml[:, rp, :] = NEG
            elif rp == r:
                msb[:, rp, :] = np.where(idx[:, None] < idx[None, :], 0.0, NEG)
                mml[:, rp, :] = np.where((idx[:, None] // 64) <= (idx[None, :] // 64), 0.0, NEG)
        cst_bf = np.concatenate([ident, negtri, negones, ones, msb.reshape(128, 1024), mml.reshape(128, 1024)],
                                axis=1).astype(f32)
        in_maps.append({
            "x": np.ascontiguousarray(xb[r::8].reshape(TL, D)),
            "posb": np.ascontiguousarray(np.broadcast_to(pb[r::8].reshape(1, TL), (64, TL))).astype(np.int32),
            "xT": xT, "posa": posa, "pnw_c": pnw_c,
            "w_in": w_in, "w_krs": w_krs, "w_qn": w_qn, "w_qr": w_qr, "w_qrs": w_qrs,
            "w_kn": w_kn, "w_kvv": w_kvv, "w_out": w_out, "pnw_b": pnw_b, "ponw_b": ponw_b,
            "cst_s": cst_s, "cst_bf": np.ascontiguousarray(cst_bf),
        })
    return in_maps


_NC_CACHE = {}


def kernel(x, positions, pre_norm_w, w_in, q_norm_w, w_q_up, kv_norm_w, w_kv_up, w_out, post_norm_w):
    in_maps = _host_inputs(x, positions, pre_norm_w, w_in, q_norm_w, w_q_up, kv_norm_w, w_kv_up, w_out, post_norm_w)
    nc = build_program()
    res = run_bass_kernel_spmd(nc, in_maps, core_ids=list(range(NCORES)))
    out = np.zeros((64, 128, D), np.float32)
    for r in range(NCORES):
        out[r::8] = np.asarray(res.results[r]["out"], np.float32).reshape(8, 128, D)
    return out.reshape(1, SEQ, D)
```

```python
import math
from contextlib import ExitStack

import numpy as np
import concourse.bass as bass
import concourse.mybir as mybir
from concourse.bass_utils import run_bass_kernel_spmd

F32 = mybir.dt.float32
BF16 = mybir.dt.bfloat16
I32 = mybir.dt.int32
AF = mybir.ActivationFunctionType
ALU = mybir.AluOpType

NCORES = 8
D = 2048
SEQ = 8192
TL = 1024
NB = 8
DIN = 5952
EPS = 1e-6
NEG = -30000.0
SC_SB = 1.0 / math.sqrt(128.0)
SC_MLA = 1.0 / math.sqrt(192.0)
SB_ROWS = 2048
MLA_ROWS = 2112

DEBUG = False


class _Stop(Exception):
    pass


class Builder:
    def __init__(self, nc, stack):
        self.nc = nc
        self.stack = stack
        self.q = {k: [] for k in ("pe", "act", "dve", "pool", "sp")}
        self.waited = {k: {} for k in self.q}
        self.prog = {k: self.new_sem("prog_" + k) for k in ("pe", "act", "dve", "pool")}
        self.nsem = 0

    def new_sem(self, name):
        h = self.stack.enter_context(self.nc.semaphore(name))
        return [h, 0]

    def op(self, eng, fn, waits=(), sem=None, amt=1, inc=True):
        ws = []
        mx = {}
        for t in waits:
            if t is None:
                continue
            s, v = t
            if id(s) not in mx or mx[id(s)][1] < v:
                mx[id(s)] = (s, v)
        for key, (s, v) in mx.items():
            if self.waited[eng].get(key, 0) >= v:
                continue
            self.waited[eng][key] = v
            ws.append((s[0], v))
        tok = None
        incspec = None
        if inc:
            s = sem if sem is not None else self.prog[eng]
            s[1] += amt
            tok = (s, s[1])
            incspec = (s[0], amt)
        self.q[eng].append((fn, ws, incspec))
        return tok

    def replay(self, eng_obj, key):
        for fn, ws, incspec in self.q[key]:
            for h, v in ws[:-1]:
                eng_obj.wait_ge(h, v)
            ins = fn(eng_obj)
            if ws:
                ins._wait_ge(ws[-1][0], ws[-1][1])
            if incspec is not None:
                ins.then_inc(incspec[0], incspec[1])


def I(name, *args, **kw):
    return lambda e: getattr(e, name)(*args, **kw)


class Banks:
    def __init__(self, aps):
        self.aps = aps
        self.free = [None] * len(aps)
        self.i = 0

    def get(self):
        i = self.i
        self.i = (self.i + 1) % len(self.aps)
        return i, self.aps[i], self.free[i]

    def release(self, i, tok):
        self.free[i] = tok


class Arena:
    def __init__(self, t, nbytes):
        self.t = t
        self.nbytes = nbytes
        self.off = 0

    def alloc(self, shape, dt, parts=128):
        esz = 2 if dt == BF16 else 4
        n = 1
        for s in shape:
            n *= s
        nb = n * esz
        self.off = (self.off + 63) // 64 * 64
        assert self.off + nb <= self.nbytes, ("SBUF arena overflow", self.off, nb)
        ap = self.t[0:parts, self.off // 2:(self.off + nb) // 2]
        self.off += nb
        if esz == 4:
            ap = ap.bitcast(dt)
        if len(shape) == 2:
            ap = ap.rearrange("p (a b) -> p a b", a=shape[0])
        elif len(shape) == 3:
            ap = ap.rearrange("p (a b c) -> p a b c", a=shape[0], b=shape[1])
        return ap

    def mark(self):
        return self.off

    def reset(self, m):
        self.off = m


def build_program(debug=False, stop_after=9):
    nc = bass.Bass("TRN2", target_bir_lowering=False)

    def din(name, shape, dt=F32):
        return nc.dram_tensor(name, shape, dt, kind="ExternalInput").ap()

    x_d = din("x", [TL, D])
    posb_d = din("posb", [64, TL], I32)
    win_d = din("w_in", [D, DIN])
    wkrs_d = din("w_krs", [D, 64])
    wqn_d = din("w_qn", [512, 1024])
    wqr_d = din("w_qr", [512, 512])
    wqrs_d = din("w_qrs", [512, 512])
    wkn_d = din("w_kn", [256, 1024])
    wkvv_d = din("w_kvv", [256, 1024])
    wout_d = din("w_out", [D, D])
    pnwb_d = din("pnw_b", [128, D])
    ponwb_d = din("ponw_b", [128, D])
    csts_d = din("cst_s", [128, 8])
    cstbf_d = din("cst_bf", [128, 2560])
    out_d = nc.dram_tensor("out", [TL, D], F32, kind="ExternalOutput").ap()

    xT_d = din("xT", [D, SEQ])
    posa_d = din("posa", [64, SEQ], I32)
    pnwc_d = din("pnw_c", [128, 16])
    kt_sb_d = nc.dram_tensor("kt_sb", [1024, SEQ], BF16).ap()
    v_sb_d = nc.dram_tensor("v_sb", [1024, SEQ], BF16).ap()
    kt_ml_d = nc.dram_tensor("kt_ml", [1024, SEQ], BF16).ap()
    v_ml_d = nc.dram_tensor("v_ml", [1024, SEQ], BF16).ap()
    kr_d = nc.dram_tensor("kr", [64, SEQ], BF16).ap()

    dbg = {}
    if debug:
        def dout(name, shape, dt=F32):
            dbg[name] = nc.dram_tensor(name, shape, dt, kind="ExternalOutput").ap()
        dout("d_hT", [128, 16 * TL], BF16)
        dout("d_qsb", [128, 8 * TL], BF16)
        dout("d_qn", [128, 8 * TL], BF16)
        dout("d_qr", [64, 8 * TL], BF16)
        dout("d_gate", [128, 16 * TL], BF16)
        dout("d_tab", [64, 4 * TL], F32)
        dout("d_k0", [128, 8 * TL], BF16)
        dout("d_v0", [128, 8 * TL], BF16)
        dout("d_kr", [64, 8 * TL], BF16)
        dout("d_cqn", [128, 4 * TL], BF16)
        dout("d_ckvn", [128, 2 * TL], BF16)
        dout("d_kn0", [128, 8 * TL], BF16)
        dout("d_vm0", [128, 8 * TL], BF16)
        dout("d_mixed", [128, 16 * TL], BF16)

    ARENA_BYTES = 200 * 1024
    with ExitStack() as st:
        B = Builder(nc, st)
        big = st.enter_context(nc.sbuf_tensor("arena", [128, ARENA_BYTES // 2], BF16))
        A = Arena(big, ARENA_BYTES)

        cbf = A.alloc([2560], BF16)
        ident = cbf[:, 0:128]
        negtri = cbf[:, 128:256]
        negones = cbf[:, 256:384]
        ones = cbf[:, 384:512]
        msk_sb = cbf[:, 512:1536]
        msk_mla = cbf[:, 1536:2560]
        csts = A.alloc([8], F32)
        small = A.alloc([64], F32)
        small2 = A.alloc([64], F32)
        pscr = A.alloc([8], F32)
        pnwc = A.alloc([16], F32)
        m_const = A.mark()
        qT_sb = A.alloc([8, TL], BF16)
        qN = A.alloc([8, TL], BF16)
        qR = A.alloc([8, TL], BF16, parts=64)
        gate = A.alloc([16, TL], BF16)
        m_persist = A.mark()

        psA = st.enter_context(nc.psum_tensor("psA", [128, 4096], F32))
        PS = [psA[:, 512 * i:512 * (i + 1)] for i in range(8)]

        def now(eng):
            return (B.prog[eng], B.prog[eng][1])

        def all_now():
            return [now(k) for k in ("pe", "act", "dve", "pool") if B.prog[k][1] > 0]


        def rsqrt_chain(dst, src_ap, scale, waits):
            ta = B.op("dve", I("tensor_scalar", out=dst, in0=src_ap, scalar1=scale, scalar2=EPS,
                                                        op0=ALU.mult, op1=ALU.add), waits=waits)
            tb = B.op("act", I("sqrt", out=dst, in_=dst), waits=[ta])
            tc = B.op("dve", I("reciprocal", out=dst, in_=dst), waits=[tb])
            return ta, tc

        S_c = B.new_sem("s_cst")
        S_c2 = B.new_sem("s_cst2")
        B.op("pool", I("dma_start", out=cbf, in_=cstbf_d[:, :]), sem=S_c, amt=16)
        B.op("sp", I("dma_start", out=csts, in_=csts_d[:, :]), sem=S_c2, amt=16)
        t_cst = (S_c2, 16)
        t_cbf = (S_c, 16)
        t_z = B.op("dve", I("memset", small, 0.0))
        t_z = B.op("dve", I("memset", small2, 0.0))

        dbg_toks = []
        S_dbg = B.new_sem("s_dbg")

        def dump(name, src_ap, waits):
            dbg_toks.append(B.op("sp", I("dma_start", out=dbg[name], in_=src_ap), waits=waits,
                                 sem=S_dbg, amt=16))

        def maybe_stop(level):
            if stop_after <= level:
                raise _Stop()

        out_toks = []
        try:

            A.reset(m_const)
            wk = A.alloc([16, 1024], BF16)
            wvv = A.alloc([16, 1024], BF16)
            wc = A.alloc([16, 384], BF16)
            wkn_a = A.alloc([2, 1024], BF16)
            wkvv_a = A.alloc([2, 1024], BF16)
            xa = [A.alloc([16, 512], BF16) for _ in range(2)]
            hTa = [A.alloc([16, 512], BF16) for _ in range(2)]
            sqa = A.alloc([8, 512], BF16)
            rstdb_a = A.alloc([512], F32)
            rcol = A.alloc([8], F32)
            ckvf = A.alloc([2, 512], F32)
            sq2 = [A.alloc([512], BF16) for _ in range(2)]
            rstdkv = A.alloc([512], F32)
            ckvn_a = A.alloc([2, 512], BF16)
            kstA = [A.alloc([512], BF16) for _ in range(2)]
            vstA = [A.alloc([512], BF16) for _ in range(2)]
            posi_a = A.alloc([512], I32, parts=64)
            posf_a = A.alloc([512], F32, parts=64)
            ang_a = A.alloc([512], F32, parts=64)
            ry_a = A.alloc([512], F32, parts=64)
            rk_a = A.alloc([512], F32, parts=64)
            cos_a = A.alloc([512], F32, parts=64)
            sin_a = A.alloc([512], F32, parts=64)
            rt1a = A.alloc([512], F32, parts=64)
            rt2a = A.alloc([512], F32, parts=64)
            krot_a = A.alloc([512], BF16, parts=64)

            def wsrc(ap_):
                return ap_.rearrange("(c p) n -> p c n", p=128)

            S_wa = B.new_sem("s_wa")
            S_wa2 = B.new_sem("s_wa2")
            t_pnwc = B.op("sp", I("dma_start", out=pnwc, in_=pnwc_d[:, :]), sem=S_wa2, amt=16)
            B.op("pool", I("dma_start", out=wk, in_=wsrc(win_d[:, 1024:2048])), sem=S_wa, amt=16)
            B.op("pool", I("dma_start", out=wc[:, :, 0:320], in_=wsrc(win_d[:, 4608:4928])), sem=S_wa, amt=16)
            B.op("pool", I("dma_start", out=wc[:, :, 320:384], in_=wsrc(wkrs_d[:, :])), sem=S_wa, amt=16)
            B.op("pool", I("dma_start", out=wvv, in_=wsrc(win_d[:, 2048:3072])), sem=S_wa, amt=16)
            B.op("pool", I("dma_start", out=wkn_a, in_=wsrc(wkn_d[:, :])), sem=S_wa, amt=16)
            B.op("pool", I("dma_start", out=wkvv_a, in_=wsrc(wkvv_d[:, :])), sem=S_wa, amt=16)
            t_wa = (S_wa, 6 * 16)

            banksA = Banks(PS[0:6])
            SSb = PS[6]
            CSb = PS[7]
            ss_free = [None]
            cs_free = [None]
            pro = {}
            S_xa = [B.new_sem("s_xa%d" % i) for i in range(2)]
            S_pa = B.new_sem("s_posa")
            S_ka = [B.new_sem("s_ka%d" % i) for i in range(2)]
            S_va = [B.new_sem("s_va%d" % i) for i in range(2)]
            S_kra = B.new_sem("s_kra")
            xa_free = [[], []]
            hTa_free = [None, None]
            sqa_free = [None]
            posi_free = [None]
            kstA_free = [None, None]
            vstA_free = [None, None]
            krotA_free = [None]
            kA = [0]
            vA = [0]
            PI = math.pi
            INV2PI = 1.0 / (2.0 * PI)
            C1 = 6.28125
            C2 = 2.0 * PI - C1
            invf = csts[0:64, 6:7]
            sgn = csts[0:64, 7:8]
            NT = SEQ // 512

            def mmg(out_ap, lhs_fn, rhs_fn, nch, waits):
                tok = None
                for c in range(nch):
                    tok = B.op("pe", I("matmul", out_ap, lhsT=lhs_fn(c), rhs=rhs_fn(c),
                                       start=(c == 0), stop=(c == nch - 1)),
                               waits=waits if c == 0 else [], inc=(c == nch - 1))
                return tok

            def kst_out(bank, t_mm, mul_rstd, dst_ap, t_rb_):
                ks = kA[0] % 2
                kA[0] += 1
                if mul_rstd:
                    t_ev = B.op("dve", I("tensor_tensor", out=kstA[ks], in0=bank, in1=rstdb_a, op=ALU.mult),
                                waits=[t_mm, t_rb_, kstA_free[ks]])
                else:
                    t_ev = B.op("act", I("copy", out=kstA[ks], in_=bank), waits=[t_mm, kstA_free[ks]])
                t_d = B.op("sp", I("dma_start", out=dst_ap, in_=kstA[ks]), waits=[t_ev], sem=S_ka[ks], amt=16)
                kstA_free[ks] = t_d
                return t_ev

            def vst_out(bank, t_mm, scale_ap, dst_ap, t_rc_):
                vs = vA[0] % 2
                vA[0] += 1
                if scale_ap is not None:
                    t_ev = B.op("act", I("activation", out=vstA[vs], in_=bank, func=AF.Copy, scale=scale_ap),
                                waits=[t_mm, t_rc_, vstA_free[vs]])
                else:
                    t_ev = B.op("dve", I("tensor_copy", out=vstA[vs], in_=bank), waits=[t_mm, vstA_free[vs]])
                t_d = B.op("sp", I("dma_start", out=dst_ap, in_=vstA[vs].rearrange("s (h d) -> s h d", h=4)),
                           waits=[t_ev], sem=S_va[vs], amt=16)
                vstA_free[vs] = t_d
                return t_ev

            def prologue(T):
                sl = T % 2
                t0 = T * 512
                t_xa = B.op("pool", I("dma_start", out=xa[sl], in_=xT_d[:, t0:t0 + 512].rearrange("(c p) t -> p c t", p=128)),
                            waits=xa_free[sl], sem=S_xa[sl], amt=16)
                t_pos = B.op("sp", I("dma_start", out=posi_a, in_=posa_d[:, t0:t0 + 512]), waits=[posi_free[0]],
                             sem=S_pa, amt=16)
                t_sq = None
                t_on = None
                t_cs = None
                for half in range(2):
                    t_sq = B.op("act", I("activation", out=sqa, in_=xa[sl][:, 8 * half:8 * half + 8, :], func=AF.Square),
                                waits=[t_xa, sqa_free[0]])
                    for c in range(8):
                        t_on = B.op("pe", I("matmul", SSb, lhsT=ones, rhs=sqa[:, c, :],
                                            start=(half == 0 and c == 0), stop=(half == 1 and c == 7)),
                                    waits=[t_sq, ss_free[0], t_cbf] if c == 0 else [], inc=(c == 7))
                    for tb in range(4):
                        for c in range(8):
                            t_cs = B.op("pe", I("matmul", CSb[:, 2 * tb + half:2 * tb + half + 1],
                                                lhsT=sqa[:, c, tb * 128:(tb + 1) * 128], rhs=ones[:, 0:1],
                                                start=(c == 0), stop=(c == 7)),
                                        waits=[cs_free[0]] if (c == 0 and tb == 0 and half == 0) else [], inc=(c == 7))
                    sqa_free[0] = t_cs
                t_h = B.op("dve", I("tensor_tensor", out=hTa[sl], in0=xa[sl],
                                    in1=pnwc.unsqueeze(2).to_broadcast([128, 16, 512]), op=ALU.mult),
                           waits=[t_xa, hTa_free[sl], t_wa, t_pnwc])
                xa_free[sl] = [t_h, t_sq]
                pro[T] = (t_pos, t_on, t_cs, t_h)

            prologue(0)
            for T in range(NT):
                sl = T % 2
                t0 = T * 512
                t_pos, t_on, t_cs, t_h = pro[T]
                t_ra, t_rb = rsqrt_chain(rstdb_a, SSb, 1.0 / D, [t_on, now("pe"), now("dve")])
                ss_free[0] = t_ra
                csv = CSb[:, 0:8].rearrange("p (t h) -> p t h", h=2)
                t_c1 = B.op("dve", I("tensor_reduce", out=rcol[:, 0:4], in_=csv, axis=mybir.AxisListType.X, op=ALU.add),
                            waits=[t_cs, now("act")])
                cs_free[0] = t_c1
                t_c2, t_rc = rsqrt_chain(rcol[:, 4:8], rcol[:, 0:4], 1.0 / D, [t_c1])
                tq = B.op("dve", I("tensor_copy", out=posf_a, in_=posi_a), waits=[t_pos, now("act")])
                posi_free[0] = tq
                tq = B.op("dve", I("tensor_scalar", out=ang_a, in0=posf_a, scalar1=invf, scalar2=None, op0=ALU.mult),
                          waits=[tq, t_cst])
                ty = B.op("dve", I("tensor_scalar", out=ry_a, in0=ang_a, scalar1=INV2PI, scalar2=0.5,
                                   op0=ALU.mult, op1=ALU.add), waits=[tq])
                tk = B.op("dve", I("tensor_copy", out=posi_a, in_=ry_a), waits=[ty])
                tkf = B.op("dve", I("tensor_copy", out=rk_a, in_=posi_a), waits=[tk])
                posi_free[0] = tkf
                tg = B.op("dve", I("tensor_tensor", out=ry_a, in0=rk_a, in1=ry_a, op=ALU.is_gt), waits=[tkf])
                tm = B.op("dve", I("tensor_tensor", out=rk_a, in0=rk_a, in1=ry_a, op=ALU.subtract), waits=[tg])
                tr1 = B.op("dve", I("scalar_tensor_tensor", out=ang_a, in0=rk_a, scalar=-C1, in1=ang_a,
                                    op0=ALU.mult, op1=ALU.add), waits=[tm])
                tr2 = B.op("dve", I("scalar_tensor_tensor", out=ang_a, in0=rk_a, scalar=-C2, in1=ang_a,
                                    op0=ALU.mult, op1=ALU.add), waits=[tr1])
                tc1 = B.op("dve", I("tensor_scalar", out=ang_a, in0=ang_a, scalar1=PI, scalar2=-PI,
                                    op0=ALU.min, op1=ALU.max), waits=[tr2])
                ts3 = B.op("act", I("activation", out=sin_a, in_=ang_a, func=AF.Sin), waits=[tc1, now("dve")])
                ts4 = B.op("dve", I("tensor_scalar", out=sin_a, in0=sin_a, scalar1=sgn, scalar2=None, op0=ALU.mult),
                           waits=[ts3])
                t5 = B.op("dve", I("tensor_scalar", out=rk_a, in0=ang_a, scalar1=0.5 * PI, scalar2=None, op0=ALU.add),
                          waits=[ts3, tm])
                t5b = B.op("dve", I("tensor_single_scalar", out=ry_a, in_=rk_a, scalar=PI, op=ALU.is_gt), waits=[t5])
                t6 = B.op("dve", I("scalar_tensor_tensor", out=rk_a, in0=ry_a, scalar=-2.0 * PI, in1=rk_a,
                                   op0=ALU.mult, op1=ALU.add), waits=[t5b])
                t6b = B.op("dve", I("tensor_scalar", out=rk_a, in0=rk_a, scalar1=PI, scalar2=-PI,
                                    op0=ALU.min, op1=ALU.max), waits=[t6])
                ts7 = B.op("act", I("activation", out=cos_a, in_=rk_a, func=AF.Sin), waits=[t6b])
                t_mm = None
                for h in range(8):
                    bi, bank, bfree = banksA.get()
                    t_mm = mmg(bank, lambda c, h=h: wk[:, c, h * 128:(h + 1) * 128], lambda c: hTa[sl][:, c, :], 16,
                               [t_h, t_wa, bfree])
                    t_ev = kst_out(bank, t_mm, True, kt_sb_d[h * 128:(h + 1) * 128, t0:t0 + 512], t_rb)
                    banksA.release(bi, t_ev)
                if T + 1 < NT:
                    prologue(T + 1)
                lat = []
                for j in range(2):
                    bi, bank, bfree = banksA.get()
                    t_mm = mmg(bank, lambda c, j=j: wc[:, c, j * 128:(j + 1) * 128], lambda c: hTa[sl][:, c, :], 16,
                               [t_h, t_wa, bfree])
                    lat.append((bi, bank, t_mm))
                b1, bk1, f1 = banksA.get()
                t_m1 = mmg(bk1[0:64, :], lambda c: wc[:, c, 256:320], lambda c: hTa[sl][:, c, :], 16, [t_h, t_wa, f1])
                b2, bk2, f2 = banksA.get()
                t_m2 = mmg(bk2[0:64, :], lambda c: wc[:, c, 320:384], lambda c: hTa[sl][:, c, :], 16, [t_h, t_wa, f2])
                bs2, SS2, fs2 = banksA.get()
                t_o2 = None
                for j in range(2):
                    bi, bank, t_mm = lat[j]
                    t_f = B.op("dve", I("tensor_tensor", out=ckvf[:, j, :], in0=bank, in1=rstdb_a, op=ALU.mult),
                               waits=[t_mm, t_rb, now("pe")])
                    banksA.release(bi, t_f)
                    t_s2 = B.op("act", I("activation", out=sq2[j], in_=ckvf[:, j, :], func=AF.Square), waits=[t_f, now("pe")])
                    t_o2 = B.op("pe", I("matmul", SS2, lhsT=ones, rhs=sq2[j], start=(j == 0), stop=(j == 1)),
                                waits=[t_s2, fs2 if j == 0 else None])
                t_ka, t_kb = rsqrt_chain(rstdkv, SS2, 1.0 / 256.0, [t_o2, now("dve")])
                banksA.release(bs2, t_ka)
                t_n = None
                for j in range(2):
                    t_n = B.op("dve", I("scalar_tensor_tensor", out=ckvn_a[:, j, :], in0=ckvf[:, j, :],
                                        scalar=csts[:, 4 + j:5 + j], in1=rstdkv, op0=ALU.mult, op1=ALU.mult),
                               waits=[t_kb, now("pe"), t_cst])
                tr_1 = B.op("dve", I("tensor_tensor", out=rt1a, in0=bk1[0:64, :], in1=cos_a, op=ALU.mult),
                            waits=[t_m1, ts7])
                banksA.release(b1, tr_1)
                tr_2 = B.op("dve", I("tensor_tensor", out=rt2a, in0=bk2[0:64, :], in1=sin_a, op=ALU.mult),
                            waits=[t_m2, ts4])
                banksA.release(b2, tr_2)
                tr_3 = B.op("dve", I("tensor_tensor", out=rt1a, in0=rt1a, in1=rt2a, op=ALU.add), waits=[tr_1, tr_2])
                tr_4 = B.op("dve", I("tensor_tensor", out=krot_a, in0=rt1a, in1=rstdb_a[0:64, :], op=ALU.mult),
                            waits=[tr_3, t_rb, krotA_free[0]])
                krotA_free[0] = B.op("sp", I("dma_start", out=kr_d[:, t0:t0 + 512], in_=krot_a), waits=[tr_4],
                                     sem=S_kra, amt=16)
                for tb in range(4):
                    gb = 4 * T + tb
                    for half in range(2):
                        bi, bank, bfree = banksA.get()
                        t_mm = mmg(bank, lambda c, tb=tb: hTa[sl][:, c, tb * 128:(tb + 1) * 128],
                                   lambda c, half=half: wvv[:, c, half * 512:(half + 1) * 512], 16, [t_h, t_wa, bfree])
                        dview = v_sb_d[half * 512:(half + 1) * 512, gb * 128:(gb + 1) * 128].rearrange(
                            "(h s) d -> s h d", s=128)
                        t_ev = vst_out(bank, t_mm, rcol[:, 4 + tb:5 + tb], dview, t_rc)
                        banksA.release(bi, t_ev)
                hTa_free[sl] = t_mm
                for h in range(8):
                    bi, bank, bfree = banksA.get()
                    t_mm = mmg(bank, lambda c, h=h: wkn_a[:, c, h * 128:(h + 1) * 128], lambda c: ckvn_a[:, c, :], 2,
                               [t_n, t_wa, bfree])
                    t_ev = kst_out(bank, t_mm, False, kt_ml_d[h * 128:(h + 1) * 128, t0:t0 + 512], None)
                    banksA.release(bi, t_ev)
                for tb in range(4):
                    gb = 4 * T + tb
                    for half in range(2):
                        bi, bank, bfree = banksA.get()
                        t_mm = mmg(bank, lambda c, tb=tb: ckvn_a[:, c, tb * 128:(tb + 1) * 128],
                                   lambda c, half=half: wkvv_a[:, c, half * 512:(half + 1) * 512], 2, [t_n, t_wa, bfree])
                        dview = v_ml_d[half * 512:(half + 1) * 512, gb * 128:(gb + 1) * 128].rearrange(
                            "(h s) d -> s h d", s=128)
                        t_ev = vst_out(bank, t_mm, None, dview, None)
                        banksA.release(bi, t_ev)
            phA_all = all_now() + [(S_ka[0], S_ka[0][1]), (S_ka[1], S_ka[1][1]), (S_va[0], S_va[0][1]),
                                   (S_va[1], S_va[1][1]), (S_kra, S_kra[1])]
            t_ag_sb = None
            t_ag_mla = None
            A.reset(m_persist)
            maybe_stop(-1)

            hT = A.alloc([16, TL], BF16)
            cosT = A.alloc([TL], F32, parts=64)
            sinS = A.alloc([TL], F32, parts=64)
            cosTq = A.alloc([TL], F32, parts=64)
            sinSq = A.alloc([TL], F32, parts=64)
            m_ph1 = A.mark()
            wg = [A.alloc([8192], BF16) for i in range(2)]
            cqn = A.alloc([4, TL], BF16)
            ckvn = A.alloc([2, TL], BF16)
            kst = [A.alloc([TL], BF16) for i in range(2)]
            vst = [A.alloc([512], BF16) for i in range(2)]
            sq = [A.alloc([512], BF16) for i in range(2)]
            rstdb = A.alloc([512], F32)
            rt1 = A.alloc([512], F32, parts=64)
            rt2 = A.alloc([512], F32, parts=64)
            krot = A.alloc([TL], BF16, parts=64)
            A.reset(m_ph1)

            posi = A.alloc([TL], I32, parts=64)
            posf = A.alloc([TL], F32, parts=64)
            ang = A.alloc([TL], F32, parts=64)
            rr_y = A.alloc([TL], F32, parts=64)
            rr_k = A.alloc([TL], F32, parts=64)
            xt = [A.alloc([D], F32) for i in range(2)]
            xn = [A.alloc([D], BF16) for i in range(2)]
            junk = A.alloc([D], BF16)
            pnwb = A.alloc([D], F32)

            S_p = B.new_sem("s_pos")
            B.op("sp", I("dma_start", out=posi, in_=posb_d[:, :]), waits=phA_all, sem=S_p, amt=16)
            B.op("sp", I("dma_start", out=pnwb, in_=pnwb_d[:, :]), waits=phA_all, sem=S_p, amt=16)
            t_pnw = (S_p, 32)
            t = B.op("dve", I("tensor_copy", out=posf, in_=posi), waits=[t_pnw])
            invf = csts[0:64, 6:7]
            sgn = csts[0:64, 7:8]
            PI = math.pi
            INV2PI = 1.0 / (2.0 * PI)
            C1 = 6.28125
            C2 = 2.0 * PI - C1
            t1 = B.op("dve", I("tensor_scalar", out=ang, in0=posf, scalar1=invf, scalar2=None, op0=ALU.mult),
                      waits=[t, t_cst])
            ty = B.op("dve", I("tensor_scalar", out=rr_y, in0=ang, scalar1=INV2PI, scalar2=0.5,
                                                        op0=ALU.mult, op1=ALU.add), waits=[t1])
            tk = B.op("dve", I("tensor_copy", out=posi, in_=rr_y), waits=[ty])
            tkf = B.op("dve", I("tensor_copy", out=rr_k, in_=posi), waits=[tk])
            tg = B.op("dve", I("tensor_tensor", out=rr_y, in0=rr_k, in1=rr_y, op=ALU.is_gt), waits=[tkf])
            tm = B.op("dve", I("tensor_tensor", out=rr_k, in0=rr_k, in1=rr_y, op=ALU.subtract), waits=[tg])
            tr1 = B.op("dve", I("scalar_tensor_tensor", out=ang, in0=rr_k, scalar=-C1, in1=ang,
                                                                op0=ALU.mult, op1=ALU.add), waits=[tm])
            tr2 = B.op("dve", I("scalar_tensor_tensor", out=ang, in0=rr_k, scalar=-C2, in1=ang,
                                                                op0=ALU.mult, op1=ALU.add), waits=[tr1])
            tc1 = B.op("dve", I("tensor_scalar", out=ang, in0=ang, scalar1=PI, scalar2=-PI,
                                                         op0=ALU.min, op1=ALU.max), waits=[tr2])
            t3 = B.op("act", I("activation", out=sinS, in_=ang, func=AF.Sin), waits=[tc1])
            t4 = B.op("dve", I("tensor_scalar", out=sinS, in0=sinS, scalar1=sgn, scalar2=None, op0=ALU.mult),
                      waits=[t3])
            t_sinq = B.op("dve", I("tensor_scalar", out=sinSq, in0=sinS, scalar1=SC_MLA, scalar2=None, op0=ALU.mult),
                          waits=[t4])
            t5 = B.op("dve", I("tensor_scalar", out=rr_k, in0=ang, scalar1=0.5 * PI, scalar2=None, op0=ALU.add),
                      waits=[t3, tm])
            t5b = B.op("dve", I("tensor_single_scalar", out=rr_y, in_=rr_k, scalar=PI, op=ALU.is_gt), waits=[t5])
            t6 = B.op("dve", I("scalar_tensor_tensor", out=rr_k, in0=rr_y, scalar=-2.0 * PI, in1=rr_k,
                                                               op0=ALU.mult, op1=ALU.add), waits=[t5b])
            t6b = B.op("dve", I("tensor_scalar", out=rr_k, in0=rr_k, scalar1=PI, scalar2=-PI,
                                                         op0=ALU.min, op1=ALU.max), waits=[t6])
            t7 = B.op("act", I("activation", out=cosT, in_=rr_k, func=AF.Sin), waits=[t6b])
            t_cosq = B.op("dve", I("tensor_scalar", out=cosTq, in0=cosT, scalar1=SC_MLA, scalar2=None, op0=ALU.mult),
                          waits=[t7])
            t_tabs = [t4, t_sinq, t7, t_cosq]

            S_x = [B.new_sem("s_x%d" % i) for i in range(2)]
            xn_free = [None, None]
            xt_free = [None, None]
            tpb = [PS[0].bitcast(BF16), PS[1].bitcast(BF16)]
            tp_free = [None, None]
            tpi = 0
            hT_toks = []
            for b in range(NB):
                s = b % 2
                t_x = B.op("sp", I("dma_start", out=xt[s], in_=x_d[b * 128:(b + 1) * 128, :]),
                           waits=[xt_free[s]] + phA_all, sem=S_x[s], amt=16)
                t_sq = B.op("act", I("activation", out=junk, in_=xt[s], func=AF.Square,
                                                                   accum_out=small[:, b:b + 1]), waits=[t_x, t_z])
                t_r1, t_r2 = rsqrt_chain(small[:, 16 + b:17 + b], small[:, b:b + 1], 1.0 / D, [t_sq])
                t_xn = B.op("dve", I("scalar_tensor_tensor",
                    out=xn[s], in0=xt[s], scalar=small[:, 16 + b:17 + b], in1=pnwb,
                    op0=ALU.mult, op1=ALU.mult), waits=[t_r2, t_x, t_pnw, xn_free[s]])
                xt_free[s] = t_xn
                for g in range(4):
                    ti = tpi % 2
                    tpi += 1
                    for j in range(4):
                        c = 4 * g + j
                        t_tp = B.op("pe", I("transpose",
                            out=tpb[ti][:, j * 128:(j + 1) * 128], in_=xn[s][:, c * 128:(c + 1) * 128], identity=ident),
                            waits=[t_xn, tp_free[ti], t_cbf] if j == 0 else [], inc=(j == 3))
                    src = tpb[ti][:, 0:512].rearrange("p (j t) -> p j t", j=4)
                    dst = hT[:, 4 * g:4 * g + 4, b * 128:(b + 1) * 128]
                    if g % 2 == 0:
                        t_ev = B.op("act", I("copy", out=dst, in_=src), waits=[t_tp])
                    else:
                        t_ev = B.op("dve", I("tensor_copy", out=dst, in_=src), waits=[t_tp])
                    tp_free[ti] = t_ev
                    hT_toks.append(t_ev)
                xn_free[s] = t_tp
            ph0_done = hT_toks[-8:] + t_tabs
            if stop_after <= 0:
                dump("d_hT", hT.rearrange("p c t -> p (c t)"), ph0_done)
                for i_, tb_ in enumerate((cosT, sinS, cosTq, sinSq)):
                    dump("d_tab", tb_, ph0_done) if False else dbg_toks.append(B.op(
                        "sp", I("dma_start", out=dbg["d_tab"][:, i_ * TL:(i_ + 1) * TL], in_=tb_),
                        waits=ph0_done, sem=S_dbg, amt=16))
            maybe_stop(0)

            banks = Banks(PS)
            banks.free[0] = tp_free[0]
            banks.free[1] = tp_free[1]
            S_w = [B.new_sem("s_w%d" % i) for i in range(2)]
            wg_free = [None, None]
            wslot = [0]

            def load_w(parts):
                s = wslot[0] % 2
                wslot[0] += 1
                tok = None
                for (src, dst) in parts:
                    tok = B.op("pool", I("dma_start", out=dst, in_=src),
                               waits=[wg_free[s]] + ph0_done, sem=S_w[s], amt=16)
                return s, tok

            def wview(s, off, nch, width):
                return wg[s][:, off:off + nch * width].rearrange("p (c n) -> p c n", c=nch)

            def mm_group(out_ap, lhs_fn, rhs_fn, nch, waits):
                tok = None
                for c in range(nch):
                    tok = B.op("pe", I("matmul", out_ap, lhsT=lhs_fn(c), rhs=rhs_fn(c),
                                                              start=(c == 0), stop=(c == nch - 1)),
                               waits=waits if c == 0 else [], inc=(c == nch - 1))
                return tok

            S_k = [B.new_sem("s_kst%d" % i) for i in range(2)]
            S_v = [B.new_sem("s_vst%d" % i) for i in range(2)]
            kst_free = [None, None]
            vst_free = [None, None]
            kcnt = [0]
            vcnt = [0]
            snd_sb_toks = []
            snd_mla_toks = []

            def win_cols(c0, n):
                return win_d[:, c0:c0 + n].rearrange("(c p) n -> p c n", p=128)

            def proj_k_heads(wv, t_w, s_w, nheads, head0, dst, toks, nch, rhs_tile, rdy):
                for hh in range(nheads):
                    h = head0 + hh
                    ks = kcnt[0] % 2
                    kcnt[0] += 1
                    evs = []
                    for tt in range(2):
                        bi, bank, bfree = banks.get()
                        t_mm = mm_group(bank, lambda c, hh=hh: wv[:, c, hh * 128:(hh + 1) * 128],
                                        lambda c, tt=tt: rhs_tile[:, c, tt * 512:(tt + 1) * 512], nch,
                                        [t_w, bfree] + rdy)
                        t_ev = B.op("act", I("copy",
                            out=kst[ks][:, tt * 512:(tt + 1) * 512], in_=bank), waits=[t_mm, kst_free[ks]])
                        banks.release(bi, t_ev)
                        evs.append(t_ev)
                    wg_free[s_w] = t_mm
                    r0 = h * 128
                    t_d = B.op("sp", I("dma_start", out=dst[r0:r0 + 128, :], in_=kst[ks]),
                               waits=evs, sem=S_k[ks], amt=16)
                    kst_free[ks] = t_d
                    toks.append(t_d)

            def proj_v_tok(wv, t_w, s_w, ch, nch, lhs_tile, dst, row0, toks, rdy):
                for b in range(NB):
                    bi, bank, bfree = banks.get()
                    t_mm = mm_group(bank, lambda c, b=b: lhs_tile[:, c, b * 128:(b + 1) * 128],
                                    lambda c: wv[:, c, :], nch, [t_w, bfree] + rdy)
                    vs = vcnt[0] % 2
                    vcnt[0] += 1
                    t_ev = B.op("dve", I("tensor_copy", out=vst[vs], in_=bank),
                                waits=[t_mm, vst_free[vs]])
                    banks.release(bi, t_ev)
                    r0 = row0 + ch * 512
                    dview = dst[r0:r0 + 512, b * 128:(b + 1) * 128].rearrange("(h s) d -> s h d", s=128)
                    t_d = B.op("sp", I("dma_start",
                        out=dview, in_=vst[vs].rearrange("s (h d) -> s h d", h=4)),
                        waits=[t_ev], sem=S_v[vs], amt=16)
                    vst_free[vs] = t_d
                    toks.append(t_d)
                wg_free[s_w] = t_mm

            sqfree = [None, None]
            rstd_free = [None]
            rope_free = [None]

            def latent_norm(wv, t_w, ncb, nfeat, nw_col0, dst, extra_fn=None):
                last = None
                for tt in range(2):
                    lb = []
                    for j in range(ncb):
                        bi, bank, bfree = banks.get()
                        t_mm = mm_group(bank, lambda c, j=j: wv[:, c, j * 128:(j + 1) * 128],
                                        lambda c, tt=tt: hT[:, c, tt * 512:(tt + 1) * 512], 16,
                                        [t_w, bfree] + ph0_done)
                        lb.append((bi, bank, t_mm))
                    ex = extra_fn(tt) if extra_fn is not None else None
                    bs, ssb, ssfree = banks.get()
                    t_o = None
                    for j in range(ncb):
                        bi, bank, t_mm = lb[j]
                        t_s = B.op("act", I("activation", out=sq[j % 2], in_=bank, func=AF.Square),
                                   waits=[t_mm, sqfree[j % 2]])
                        t_o = B.op("pe", I("matmul", ssb, lhsT=ones, rhs=sq[j % 2],
                                                                  start=(j == 0), stop=(j == ncb - 1)),
                                   waits=[t_s, ssfree if j == 0 else None, t_cbf])
                        sqfree[j % 2] = t_o
                    t_a, t_b = rsqrt_chain(rstdb, ssb, 1.0 / nfeat, [t_o, rstd_free[0]])
                    banks.release(bs, t_a)
                    for j in range(ncb):
                        bi, bank, t_mm = lb[j]
                        t_n = B.op("dve", I("scalar_tensor_tensor",
                            out=dst[:, j, tt * 512:(tt + 1) * 512], in0=bank, scalar=csts[:, nw_col0 + j:nw_col0 + j + 1],
                            in1=rstdb, op0=ALU.mult, op1=ALU.mult), waits=[t_b, t_mm, t_cst])
                        banks.release(bi, t_n)
                        last = t_n
                    rstd_free[0] = last
                    if ex is not None:
                        ex()
                return last

            def rope(pa, pb, ta, tb, ct, sn, dst_ap):
                t1 = B.op("dve", I("tensor_tensor", out=rt1, in0=pa, in1=ct, op=ALU.mult),
                          waits=[ta, rope_free[0]] + t_tabs)
                t2 = B.op("dve", I("tensor_tensor", out=rt2, in0=pb, in1=sn, op=ALU.mult),
                          waits=[tb] + t_tabs)
                t3 = B.op("dve", I("tensor_tensor", out=dst_ap, in0=rt1, in1=rt2, op=ALU.add),
                          waits=[t1, t2])
                rope_free[0] = t3
                return t1, t2, t3

            def proj_fm_keep(col0, dst, chunk0, func, scale):
                toks = []
                for gi in range(2):
                    s_w = wslot[0] % 2
                    s_w, t_w = load_w([(win_cols(col0 + gi * 512, 512), wview(s_w, 0, 16, 512))])
                    wv = wview(s_w, 0, 16, 512)
                    for hh in range(4):
                        ch = chunk0 + gi * 4 + hh
                        for tt in range(2):
                            bi, bank, bfree = banks.get()
                            t_mm = mm_group(bank, lambda c, hh=hh, wv=wv: wv[:, c, hh * 128:(hh + 1) * 128],
                                            lambda c, tt=tt: hT[:, c, tt * 512:(tt + 1) * 512], 16,
                                            [t_w, bfree] + ph0_done)
                            t_ev = B.op("act", I("activation",
                                out=dst[:, ch, tt * 512:(tt + 1) * 512], in_=bank, func=func, scale=scale), waits=[t_mm])
                            banks.release(bi, t_ev)
                            toks.append(t_ev)
                    wg_free[s_w] = t_mm
                return toks

            proj_fm_keep(0, qT_sb, 0, AF.Copy, SC_SB)
            proj_fm_keep(3072, gate, 0, AF.Silu, 1.0)
            proj_fm_keep(4928, gate, 8, AF.Silu, 1.0)

            s_w = wslot[0] % 2
            s_w, t_w = load_w([(win_cols(4096, 512), wview(s_w, 0, 16, 512))])
            t_cqn = latent_norm(wview(s_w, 0, 16, 512), t_w, 4, 512, 0, cqn, None)
            wg_free[s_w] = now("pe")

            s_w = wslot[0] % 2
            s_w, t_w = load_w([
                (wqn_d[:, :].rearrange("(c p) n -> p c n", p=128), wview(s_w, 0, 4, 1024)),
                (wqr_d[:, :].rearrange("(c p) n -> p c n", p=128), wview(s_w, 4096, 4, 512)),
                (wqrs_d[:, :].rearrange("(c p) n -> p c n", p=128), wview(s_w, 6144, 4, 512)),
            ])
            t_w = (S_w[s_w], S_w[s_w][1])
            wqn_v = wview(s_w, 0, 4, 1024)
            wqr_v = wview(s_w, 4096, 4, 512)
            wqrs_v = wview(s_w, 6144, 4, 512)
            for h in range(8):
                for tt in range(2):
                    bi, bank, bfree = banks.get()
                    t_mm = mm_group(bank, lambda c, h=h: wqn_v[:, c, h * 128:(h + 1) * 128],
                                    lambda c, tt=tt: cqn[:, c, tt * 512:(tt + 1) * 512], 4, [t_w, bfree, t_cqn])
                    t_ev = B.op("act", I("activation",
                        out=qN[:, h, tt * 512:(tt + 1) * 512], in_=bank, func=AF.Copy, scale=SC_MLA), waits=[t_mm])
                    banks.release(bi, t_ev)
                    b1, bk1, f1 = banks.get()
                    t_m1 = mm_group(bk1[0:64, :], lambda c, h=h: wqr_v[:, c, h * 64:(h + 1) * 64],
                                    lambda c, tt=tt: cqn[:, c, tt * 512:(tt + 1) * 512], 4, [t_w, f1, t_cqn])
                    b2, bk2, f2 = banks.get()
                    t_m2 = mm_group(bk2[0:64, :], lambda c, h=h: wqrs_v[:, c, h * 64:(h + 1) * 64],
                                    lambda c, tt=tt: cqn[:, c, tt * 512:(tt + 1) * 512], 4, [t_w, f2, t_cqn])
                    t1, t2, t3 = rope(bk1[0:64, :], bk2[0:64, :], t_m1, t_m2, cosTq[:, tt * 512:(tt + 1) * 512],
                                      sinSq[:, tt * 512:(tt + 1) * 512], qR[:, h, tt * 512:(tt + 1) * 512])
                    banks.release(b1, t1)
                    banks.release(b2, t2)
            ph1_all = all_now()

            if debug:
                dump("d_hT", hT.rearrange("p c t -> p (c t)"), ph1_all)
                dump("d_qsb", qT_sb.rearrange("p c t -> p (c t)"), ph1_all)
                dump("d_qn", qN.rearrange("p c t -> p (c t)"), ph1_all)
                dump("d_qr", qR.rearrange("p c t -> p (c t)"), ph1_all)
                dump("d_gate", gate.rearrange("p c t -> p (c t)"), ph1_all)
                dump("d_cqn", cqn.rearrange("p c t -> p (c t)"), ph1_all)
                ph1_all = ph1_all + [(S_dbg, S_dbg[1])]
            maybe_stop(1)

            if True:
                A.reset(m_persist)
                kbuf = [A.alloc([64, 128], BF16) for _ in range(2)]
                vbuf = [A.alloc([64, 128], BF16) for _ in range(2)]
                krbuf = A.alloc([64, 128], BF16, parts=64)
                e_t = [A.alloc([2, 512], F32) for _ in range(2)]
                sp_t = [A.alloc([2, 512], BF16) for _ in range(3)]
                a_t = [A.alloc([2, 512], BF16) for _ in range(3)]
                Rt = A.alloc([512], BF16)
                rec = A.alloc([512], F32)
                otmp = A.alloc([512], F32)
                m_ph2 = A.mark()

                ZP = [psA[:, 1024 * p_:1024 * (p_ + 1)].rearrange("p (b n) -> p b n", b=2) for p_ in range(3)]
                OACC = [PS[6], PS[6]]
                DEN = [PS[7], PS[7]]
                zfree = [None] * 3
                oacc_free = [None, None]
                den_free = [None, None]
                spfree = [[None, None] for _ in range(3)]
                afree = [None] * 3
                S_kv = [B.new_sem("s_kv%d" % i) for i in range(2)]
                kv_free = [None, None]
                S_krb = B.new_sem("s_krb")
                t_krb = B.op("sp", I("dma_start", out=krbuf, in_=kr_d.rearrange("f (g s) -> f g s", s=128)),
                             waits=ph1_all + phA_all, sem=S_krb, amt=16)

                def load_head(hi):
                    s = hi % 2
                    h = hi % 8
                    if hi < 8:
                        ksrc = kt_sb_d[h * 128:(h + 1) * 128, :]
                        vsrc = v_sb_d[h * 128:(h + 1) * 128, :]
                    else:
                        ksrc = kt_ml_d[h * 128:(h + 1) * 128, :]
                        vsrc = v_ml_d[h * 128:(h + 1) * 128, :]
                    B.op("sp", I("dma_start", out=kbuf[s], in_=ksrc.rearrange("p (g s) -> p g s", s=128)),
                         waits=[kv_free[s]] + ph1_all + phA_all, sem=S_kv[s], amt=16)
                    B.op("sp", I("dma_start", out=vbuf[s], in_=vsrc.rearrange("p (g s) -> p g s", s=128)),
                         waits=[kv_free[s]] + ph1_all + phA_all, sem=S_kv[s], amt=16)
                    return (S_kv[s], S_kv[s][1])

                cnt = {"z": 0, "e": 0, "sp": 0, "a": 0, "st": 0}
                mixed_toks = []
                kv_tok = {}
                kv_tok[0] = load_head(0)
                for hi in range(16):
                    s = hi % 2
                    h = hi % 8
                    is_sb = hi < 8
                    if hi + 1 < 16:
                        kv_tok[hi + 1] = load_head(hi + 1)
                    t_kv = kv_tok[hi]
                    for u in range(2):
                        blocks = [(g, rp) for g in range(4 * u + 3, -1, -1) for rp in range(7, -1, -1)]
                        n = len(blocks)
                        oi = 0
                        cnt["st"] += 1
                        oacc = OACC[oi]
                        den = DEN[oi]
                        info = [None] * n
                        t_Rz = None
                        if is_sb:
                            t_Rz = B.op("dve", I("memset", Rt, 0.0), waits=[now("pe")] + ph1_all)
                        tR_prev = [t_Rz]

                        npair = n // 2

                        def geo(P):
                            g, rp = blocks[2 * P]
                            c0 = 128 * max(0, g - 4 * u)
                            return g, c0, (g >= 4 * u)

                        def stA(P):
                            g, c0, diag = geo(P)
                            zi = cnt["z"] % 3
                            cnt["z"] += 1
                            d = {"zi": zi}
                            info[P] = d
                            qc = slice(512 * u + c0, 512 * u + 512)
                            t = None
                            for b_ in range(2):
                                g_, rp = blocks[2 * P + b_]
                                zb = ZP[zi][:, b_, :]
                                kblk = kbuf[s][:, 8 * g + rp, :]
                                w = ([zfree[zi], t_kv] + ph1_all) if b_ == 0 else []
                                if is_sb:
                                    t = B.op("pe", I("matmul", zb[:, c0:512], lhsT=kblk, rhs=qT_sb[:, h, qc],
                                                     start=True, stop=True), waits=w, inc=not diag)
                                    if diag:
                                        t = B.op("pe", I("matmul", zb[:, c0:c0 + 128], lhsT=ident,
                                                         rhs=msk_sb[:, rp * 128:(rp + 1) * 128],
                                                         start=False, stop=False, skip_group_check=True))
                                else:
                                    B.op("pe", I("matmul", zb[:, c0:512], lhsT=kblk, rhs=qN[:, h, qc],
                                                 start=True, stop=False), waits=w, inc=False)
                                    t = B.op("pe", I("matmul", zb[:, c0:512], lhsT=krbuf[:, 8 * g + rp, :],
                                                     rhs=qR[:, h, qc], start=False, stop=True),
                                             waits=[t_krb])
                                    if diag:
                                        t = B.op("pe", I("matmul", zb[:, c0:c0 + 128], lhsT=ident,
                                                         rhs=msk_mla[:, rp * 128:(rp + 1) * 128],
                                                         start=False, stop=True, skip_group_check=True))
                            d["tz"] = t

                        def stB(P):
                            g, c0, diag = geo(P)
                            d = info[P]
                            zp = ZP[d["zi"]][:, :, c0:512]
                            if is_sb:
                                ei = cnt["e"] % 2
                                cnt["e"] += 1
                                si = cnt["sp"] % 3
                                cnt["sp"] += 1
                                d["si"] = si
                                t_e = B.op("act", I("activation", out=e_t[ei][:, :, c0:512], in_=zp, func=AF.Exp),
                                           waits=[d["tz"]])
                                t_s = B.op("act", I("activation", out=sp_t[si][:, :, c0:512], in_=e_t[ei][:, :, c0:512],
                                                    func=AF.Ln, bias=1.0, scale=1.0),
                                           waits=[t_e] + spfree[si])
                                d["tsp"] = t_s
                            else:
                                ai = cnt["a"] % 3
                                cnt["a"] += 1
                                d["ai"] = ai
                                t_a = B.op("act", I("activation", out=a_t[ai][:, :, c0:512], in_=zp, func=AF.Exp),
                                           waits=[d["tz"], afree[ai]])
                                d["ta"] = t_a
                                zfree[d["zi"]] = t_a

                        def stC(P):
                            if not is_sb:
                                return
                            g, c0, diag = geo(P)
                            d = info[P]
                            zi = d["zi"]
                            si = d["si"]
                            first = (P == 0)
                            zb0 = ZP[zi][:, 0, c0:512]
                            zb1 = ZP[zi][:, 1, c0:512]
                            sp0 = sp_t[si][:, 0, c0:512]
                            sp1 = sp_t[si][:, 1, c0:512]
                            t = B.op("pe", I("matmul", zb0, lhsT=negtri, rhs=sp0, start=False, stop=first,
                                             skip_group_check=True), waits=[d["tsp"]], inc=False)
                            if not first:
                                B.op("pe", I("matmul", zb0, lhsT=negones, rhs=Rt[:, c0:512], start=False, stop=True,
                                             skip_group_check=True), waits=[tR_prev[0]], inc=False)
                            B.op("pe", I("matmul", zb1, lhsT=negtri, rhs=sp1, start=False, stop=False,
                                         skip_group_check=True), inc=False)
                            t = B.op("pe", I("matmul", zb1, lhsT=negones, rhs=sp0, start=False, stop=first,
                                             skip_group_check=True), inc=first)
                            if not first:
                                t = B.op("pe", I("matmul", zb1, lhsT=negones, rhs=Rt[:, c0:512], start=False, stop=True,
                                                 skip_group_check=True))
                            d["tc"] = t
                            tR1 = B.op("dve", I("tensor_tensor", out=Rt[:, c0:512], in0=Rt[:, c0:512], in1=sp0, op=ALU.add),
                                       waits=[d["tsp"], t, tR_prev[0]])
                            tR2 = B.op("dve", I("tensor_tensor", out=Rt[:, c0:512], in0=Rt[:, c0:512], in1=sp1, op=ALU.add),
                                       waits=[tR1])
                            tR_prev[0] = tR2
                            spfree[si] = [t, tR2]
                            ai = cnt["a"] % 3
                            cnt["a"] += 1
                            d["ai"] = ai
                            t_a = B.op("act", I("activation", out=a_t[ai][:, :, c0:512], in_=ZP[zi][:, :, c0:512], func=AF.Exp),
                                       waits=[t, afree[ai]])
                            d["ta"] = t_a
                            zfree[zi] = t_a

                        def stF(P):
                            g, c0, diag = geo(P)
                            d = info[P]
                            ai = d["ai"]
                            t = None
                            for b_ in range(2):
                                k = 2 * P + b_
                                g_, rp = blocks[k]
                                vblk = vbuf[s][:, 8 * g + rp, :]
                                w = [d["ta"]] if b_ == 0 else []
                                if k == 0:
                                    w += [oacc_free[oi], den_free[oi]]
                                t = B.op("pe", I("matmul", oacc[:, c0:512], lhsT=vblk, rhs=a_t[ai][:, b_, c0:512],
                                                 start=(k == 0), stop=(k == n - 1), skip_group_check=True),
                                         waits=w, inc=(is_sb and b_ == 1))
                                if not is_sb:
                                    t = B.op("pe", I("matmul", den[:, c0:512], lhsT=ones, rhs=a_t[ai][:, b_, c0:512],
                                                     start=(k == 0), stop=(k == n - 1), skip_group_check=True),
                                             inc=(b_ == 1))
                            afree[ai] = t
                            d["tf"] = t

                        for step in range(npair + 3):
                            if step == 0:
                                stA(0)
                            if step + 1 < npair:
                                stA(step + 1)
                            if step < npair:
                                stB(step)
                            if 0 <= step - 1 < npair:
                                stC(step - 1)
                            if 0 <= step - 2 < npair:
                                stF(step - 2)
                        t_last = info[npair - 1]["tf"]
                        kv_free[s] = t_last
                        ch = h if is_sb else 8 + h
                        gsl = gate[:, ch, 512 * u:512 * u + 512]
                        if is_sb:
                            t_m = B.op("dve", I("tensor_tensor", out=gsl, in0=oacc, in1=gsl, op=ALU.mult),
                                       waits=[t_last] + ph1_all)
                            oacc_free[oi] = t_m
                        else:
                            t_r = B.op("dve", I("reciprocal", out=rec, in_=den), waits=[t_last, now("dve")])
                            den_free[oi] = t_r
                            t_o = B.op("dve", I("tensor_tensor", out=otmp, in0=oacc, in1=rec, op=ALU.mult),
                                       waits=[t_r])
                            oacc_free[oi] = t_o
                            t_m = B.op("dve", I("tensor_tensor", out=gsl, in0=otmp, in1=gsl, op=ALU.mult),
                                       waits=[t_o] + ph1_all)
                        mixed_toks.append(t_m)
                ph2_all = all_now()
                if debug:
                    dbg_toks.append(B.op("sp", I("dma_start", out=dbg["d_mixed"], in_=gate.rearrange("p c t -> p (c t)")),
                                         waits=ph2_all, sem=S_dbg, amt=16))
                    ph2_all = ph2_all + [(S_dbg, S_dbg[1])]

            maybe_stop(2)
            if True:
                A.reset(m_persist)
                wout = A.alloc([16, D], BF16)
                ponwb = A.alloc([D], F32)
                xres = [A.alloc([D], F32) for _ in range(2)]
                ytile = [A.alloc([D], F32) for _ in range(2)]
                junk3 = A.alloc([512], BF16)
                S_wo = B.new_sem("s_wo")
                wo_v = wout_d.rearrange("(c p) n -> p c n", p=128)
                for nq in range(4):
                    B.op("pool", I("dma_start", out=wout[:, :, nq * 512:(nq + 1) * 512],
                                                               in_=wo_v[:, :, nq * 512:(nq + 1) * 512]),
                         waits=ph2_all, sem=S_wo, amt=16)
                S_wo2 = B.new_sem("s_wo2")
                t_pon = B.op("sp", I("dma_start", out=ponwb, in_=ponwb_d[:, :]), waits=ph2_all, sem=S_wo2, amt=16)
                t_wo = (S_wo, 64)
                S_xr = [B.new_sem("s_xr%d" % i) for i in range(2)]
                S_o = [B.new_sem("s_o%d" % i) for i in range(2)]
                xr_free = [None, None]
                yt_free = [None, None]
                yb_free = [None] * 8
                out_toks = []
                for b in range(NB):
                    s = b % 2
                    t_xr = B.op("sp", I("dma_start", out=xres[s], in_=x_d[b * 128:(b + 1) * 128, :]),
                                waits=[xr_free[s]] + ph2_all, sem=S_xr[s], amt=16)
                    t_mms = []
                    for nq in range(4):
                        bi = 4 * s + nq
                        t_mm = mm_group(PS[bi], lambda c, b=b: gate[:, c, b * 128:(b + 1) * 128],
                                        lambda c, nq=nq: wout[:, c, nq * 512:(nq + 1) * 512], 16,
                                        [t_wo, yb_free[bi]] + ph2_all)
                        t_mms.append(t_mm)
                        B.op("act", I("activation",
                            out=junk3, in_=PS[bi], func=AF.Square, accum_out=small2[:, 4 * b + nq:4 * b + nq + 1]),
                            waits=[t_mm, t_z])
                    t_sq = now("act")
                    t_s1 = B.op("dve", I("tensor_reduce", out=small2[:, 32 + b:33 + b], in_=small2[:, 4 * b:4 * b + 4],
                                                                      axis=mybir.AxisListType.X, op=ALU.add), waits=[t_sq])
                    t_s2, t_s3 = rsqrt_chain(small2[:, 48 + b:49 + b], small2[:, 32 + b:33 + b], 1.0 / D, [t_s1])
                    t_y = None
                    for nq in range(4):
                        bi = 4 * s + nq
                        cs = slice(nq * 512, (nq + 1) * 512)
                        t_y1 = B.op("dve", I("scalar_tensor_tensor",
                            out=ytile[s][:, cs], in0=PS[bi], scalar=small2[:, 48 + b:49 + b], in1=ponwb[:, cs],
                            op0=ALU.mult, op1=ALU.mult), waits=[t_s3, t_mms[nq], yt_free[s], t_wo, t_pon])
                        yb_free[bi] = t_y1
                        t_y = B.op("dve", I("tensor_tensor", out=ytile[s][:, cs], in0=ytile[s][:, cs],
                                                                                in1=xres[s][:, cs], op=ALU.add),
                                   waits=[t_y1, t_xr])
                    xr_free[s] = t_y
                    t_o = B.op("sp", I("dma_start", out=out_d[b * 128:(b + 1) * 128, :], in_=ytile[s]),
                               waits=[t_y], sem=S_o[s], amt=16)
                    yt_free[s] = t_o
                    out_toks.append(t_o)


        except _Stop:
            pass

        final_toks = list(dbg_toks) + out_toks[-2:]
        fw = [(t[0][0], t[1]) for t in final_toks]

        with nc.Block() as block:
            @block.tensor
            def _(e):
                B.replay(e, "pe")

            @block.scalar
            def _(e):
                B.replay(e, "act")

            @block.vector
            def _(e):
                B.replay(e, "dve")

            @block.gpsimd
            def _(e):
                B.replay(e, "pool")

            @block.sync
            def _(e):
                B.replay(e, "sp")
                for h_, v_ in fw:
                    e.wait_ge(h_, v_)
    return nc


def _host_inputs(x, positions, pre_norm_w, w_in, q_norm_w, w_q_up, kv_norm_w, w_kv_up, w_out, post_norm_w):
    f32 = np.float32
    x = np.asarray(x, f32)[0]
    pos = np.asarray(positions)[0].astype(np.int32)
    w_in = np.ascontiguousarray(np.asarray(w_in, f32)[0])
    w_q_up = np.asarray(w_q_up, f32)[0]
    w_kv_up = np.asarray(w_kv_up, f32)[0]
    w_out = np.ascontiguousarray(np.asarray(w_out, f32)[0])
    pnw = np.asarray(pre_norm_w, f32)[0]
    ponw = np.asarray(post_norm_w, f32)[0]
    qnw = np.asarray(q_norm_w, f32)[0]
    kvnw = np.asarray(kv_norm_w, f32)[0]

    w_krs = np.ascontiguousarray(np.concatenate([w_in[:, 4896:4928], w_in[:, 4864:4896]], axis=1))
    wq = w_q_up.reshape(512, 8, 192)
    w_qn = np.ascontiguousarray(wq[:, :, 0:128].reshape(512, 1024))
    w_qr = np.ascontiguousarray(wq[:, :, 128:192].reshape(512, 512))
    w_qrs = np.ascontiguousarray(np.concatenate([wq[:, :, 160:192], wq[:, :, 128:160]], axis=2).reshape(512, 512))
    wkv = w_kv_up.reshape(256, 8, 256)
    w_kn = np.ascontiguousarray(wkv[:, :, 0:128].reshape(256, 1024))
    w_kvv = np.ascontiguousarray(wkv[:, :, 128:256].reshape(256, 1024))
    pnw_b = np.ascontiguousarray(np.broadcast_to(pnw[None, :], (128, D)))
    ponw_b = np.ascontiguousarray(np.broadcast_to(ponw[None, :], (128, D)))

    inv_freq = (10000.0 ** (-np.arange(0, 64, 2, dtype=np.float32) / np.float32(64))).astype(f32)
    cst_s = np.zeros((128, 8), f32)
    cst_s[:, 0:4] = qnw.reshape(4, 128).T
    cst_s[:, 4:6] = kvnw.reshape(2, 128).T
    cst_s[0:64, 6] = np.concatenate([inv_freq, inv_freq])
    cst_s[0:64, 7] = np.concatenate([-np.ones(32, f32), np.ones(32, f32)])
    idx = np.arange(128)
    ident = np.eye(128, dtype=f32)
    negtri = -(idx[:, None] >= idx[None, :]).astype(f32)
    negones = -np.ones((128, 128), f32)
    ones = np.ones((128, 128), f32)

    xb = x.reshape(64, 128, D)
    pb = pos.reshape(64, 128)
    xT = np.ascontiguousarray(x.T)
    posa = np.ascontiguousarray(np.broadcast_to(pos[None, :], (64, SEQ))).astype(np.int32)
    pnw_c = np.ascontiguousarray(pnw.reshape(16, 128).T)
    in_maps = []
    for r in range(NCORES):
        msb = np.zeros((128, 8, 128), f32)
        mml = np.zeros((128, 8, 128), f32)
        for rp in range(8):
            if rp > r:
                msb[:, rp, :] = NEG
                mml[:, rp, :] = NEG
            elif rp == r:
                msb[:, rp, :] = np.where(idx[:, None] < idx[None, :], 0.0, NEG)
                mml[:, rp, :] = np.where((idx[:, None] // 64) <= (idx[None, :] // 64), 0.0, NEG)
        cst_bf = np.concatenate([ident, negtri, negones, ones, msb.reshape(128, 1024), mml.reshape(128, 1024)],
                                axis=1).astype(f32)
        in_maps.append({
            "x": np.ascontiguousarray(xb[r::8].reshape(TL, D)),
            "posb": np.ascontiguousarray(np.broadcast_to(pb[r::8].reshape(1, TL), (64, TL))).astype(np.int32),
            "xT": xT, "posa": posa, "pnw_c": pnw_c,
            "w_in": w_in, "w_krs": w_krs, "w_qn": w_qn, "w_qr": w_qr, "w_qrs": w_qrs,
            "w_kn": w_kn, "w_kvv": w_kvv, "w_out": w_out, "pnw_b": pnw_b, "ponw_b": ponw_b,
            "cst_s": cst_s, "cst_bf": np.ascontiguousarray(cst_bf),
        })
    return in_maps


_NC_CACHE = {}


def kernel(x, positions, pre_norm_w, w_in, q_norm_w, w_q_up, kv_norm_w, w_kv_up, w_out, post_norm_w):
    in_maps = _host_inputs(x, positions, pre_norm_w, w_in, q_norm_w, w_q_up, kv_norm_w, w_kv_up, w_out, post_norm_w)
    nc = build_program()
    res = run_bass_kernel_spmd(nc, in_maps, core_ids=list(range(NCORES)))
    out = np.zeros((64, 128, D), np.float32)
    for r in range(NCORES):
        out[r::8] = np.asarray(res.results[r]["out"], np.float32).reshape(8, 128, D)
    return out.reshape(1, SEQ, D)
```

```python
import math
from contextlib import ExitStack

import numpy as np
import concourse.bass as bass
import concourse.mybir as mybir
from concourse.bass_utils import run_bass_kernel_spmd

F32 = mybir.dt.float32
BF16 = mybir.dt.bfloat16
I32 = mybir.dt.int32
AF = mybir.ActivationFunctionType
ALU = mybir.AluOpType

NCORES = 8
D = 2048
SEQ = 8192
TL = 1024
NB = 8
DIN = 5952
EPS = 1e-6
NEG = -30000.0
SC_SB = 1.0 / math.sqrt(128.0)
SC_MLA = 1.0 / math.sqrt(192.0)
SB_ROWS = 2048
MLA_ROWS = 2112

DEBUG = False


class _Stop(Exception):
    pass


class Builder:
    def __init__(self, nc, stack):
        self.nc = nc
        self.stack = stack
        self.q = {k: [] for k in ("pe", "act", "dve", "pool", "sp")}
        self.waited = {k: {} for k in self.q}
        self.prog = {k: self.new_sem("prog_" + k) for k in ("pe", "act", "dve", "pool")}
        self.nsem = 0

    def new_sem(self, name):
        h = self.stack.enter_context(self.nc.semaphore(name))
        return [h, 0]

    def op(self, eng, fn, waits=(), sem=None, amt=1, inc=True):
        ws = []
        mx = {}
        for t in waits:
            if t is None:
                continue
            s, v = t
            if id(s) not in mx or mx[id(s)][1] < v:
                mx[id(s)] = (s, v)
        for key, (s, v) in mx.items():
            if self.waited[eng].get(key, 0) >= v:
                continue
            self.waited[eng][key] = v
            ws.append((s[0], v))
        tok = None
        incspec = None
        if inc:
            s = sem if sem is not None else self.prog[eng]
            s[1] += amt
            tok = (s, s[1])
            incspec = (s[0], amt)
        self.q[eng].append((fn, ws, incspec))
        return tok

    def replay(self, eng_obj, key):
        for fn, ws, incspec in self.q[key]:
            for h, v in ws[:-1]:
                eng_obj.wait_ge(h, v)
            ins = fn(eng_obj)
            if ws:
                ins._wait_ge(ws[-1][0], ws[-1][1])
            if incspec is not None:
                ins.then_inc(incspec[0], incspec[1])


def I(name, *args, **kw):
    return lambda e: getattr(e, name)(*args, **kw)


class Banks:
    def __init__(self, aps):
        self.aps = aps
        self.free = [None] * len(aps)
        self.i = 0

    def get(self):
        i = self.i
        self.i = (self.i + 1) % len(self.aps)
        return i, self.aps[i], self.free[i]

    def release(self, i, tok):
        self.free[i] = tok


class Arena:
    def __init__(self, t, nbytes):
        self.t = t
        self.nbytes = nbytes
        self.off = 0

    def alloc(self, shape, dt, parts=128):
        esz = 2 if dt == BF16 else 4
        n = 1
        for s in shape:
            n *= s
        nb = n * esz
        self.off = (self.off + 63) // 64 * 64
        assert self.off + nb <= self.nbytes, ("SBUF arena overflow", self.off, nb)
        ap = self.t[0:parts, self.off // 2:(self.off + nb) // 2]
        self.off += nb
        if esz == 4:
            ap = ap.bitcast(dt)
        if len(shape) == 2:
            ap = ap.rearrange("p (a b) -> p a b", a=shape[0])
        elif len(shape) == 3:
            ap = ap.rearrange("p (a b c) -> p a b c", a=shape[0], b=shape[1])
        return ap

    def mark(self):
        return self.off

    def reset(self, m):
        self.off = m


def build_program(debug=False, stop_after=9):
    nc = bass.Bass("TRN2", target_bir_lowering=False)

    def din(name, shape, dt=F32):
        return nc.dram_tensor(name, shape, dt, kind="ExternalInput").ap()

    x_d = din("x", [TL, D])
    posb_d = din("posb", [64, TL], I32)
    win_d = din("w_in", [D, DIN])
    wkrs_d = din("w_krs", [D, 64])
    wqn_d = din("w_qn", [512, 1024])
    wqr_d = din("w_qr", [512, 512])
    wqrs_d = din("w_qrs", [512, 512])
    wkn_d = din("w_kn", [256, 1024])
    wkvv_d = din("w_kvv", [256, 1024])
    wout_d = din("w_out", [D, D])
    pnwb_d = din("pnw_b", [128, D])
    ponwb_d = din("ponw_b", [128, D])
    csts_d = din("cst_s", [128, 8])
    cstbf_d = din("cst_bf", [128, 2560])
    out_d = nc.dram_tensor("out", [TL, D], F32, kind="ExternalOutput").ap()

    xT_d = din("xT", [D, SEQ])
    posa_d = din("posa", [64, SEQ], I32)
    pnwc_d = din("pnw_c", [128, 16])
    kt_sb_d = nc.dram_tensor("kt_sb", [1024, SEQ], BF16).ap()
    v_sb_d = nc.dram_tensor("v_sb", [1024, SEQ], BF16).ap()
    kt_ml_d = nc.dram_tensor("kt_ml", [1024, SEQ], BF16).ap()
    v_ml_d = nc.dram_tensor("v_ml", [1024, SEQ], BF16).ap()
    kr_d = nc.dram_tensor("kr", [64, SEQ], BF16).ap()

    dbg = {}
    if debug:
        def dout(name, shape, dt=F32):
            dbg[name] = nc.dram_tensor(name, shape, dt, kind="ExternalOutput").ap()
        dout("d_hT", [128, 16 * TL], BF16)
        dout("d_qsb", [128, 8 * TL], BF16)
        dout("d_qn", [128, 8 * TL], BF16)
        dout("d_qr", [64, 8 * TL], BF16)
        dout("d_gate", [128, 16 * TL], BF16)
        dout("d_tab", [64, 4 * TL], F32)
        dout("d_k0", [128, 8 * TL], BF16)
        dout("d_v0", [128, 8 * TL], BF16)
        dout("d_kr", [64, 8 * TL], BF16)
        dout("d_cqn", [128, 4 * TL], BF16)
        dout("d_ckvn", [128, 2 * TL], BF16)
        dout("d_kn0", [128, 8 * TL], BF16)
        dout("d_vm0", [128, 8 * TL], BF16)
        dout("d_mixed", [128, 16 * TL], BF16)

    ARENA_BYTES = 200 * 1024
    with ExitStack() as st:
        B = Builder(nc, st)
        big = st.enter_context(nc.sbuf_tensor("arena", [128, ARENA_BYTES // 2], BF16))
        A = Arena(big, ARENA_BYTES)

        cbf = A.alloc([2560], BF16)
        ident = cbf[:, 0:128]
        negtri = cbf[:, 128:256]
        negones = cbf[:, 256:384]
        ones = cbf[:, 384:512]
        msk_sb = cbf[:, 512:1536]
        msk_mla = cbf[:, 1536:2560]
        csts = A.alloc([8], F32)
        small = A.alloc([64], F32)
        small2 = A.alloc([64], F32)
        pscr = A.alloc([8], F32)
        pnwc = A.alloc([16], F32)
        m_const = A.mark()
        qT_sb = A.alloc([8, TL], BF16)
        qN = A.alloc([8, TL], BF16)
        qR = A.alloc([8, TL], BF16, parts=64)
        gate = A.alloc([16, TL], BF16)
        m_persist = A.mark()

        psA = st.enter_context(nc.psum_tensor("psA", [128, 4096], F32))
        PS = [psA[:, 512 * i:512 * (i + 1)] for i in range(8)]

        def now(eng):
            return (B.prog[eng], B.prog[eng][1])

        def all_now():
            return [now(k) for k in ("pe", "act", "dve", "pool") if B.prog[k][1] > 0]


        def rsqrt_chain(dst, src_ap, scale, waits):
            ta = B.op("dve", I("tensor_scalar", out=dst, in0=src_ap, scalar1=scale, scalar2=EPS,
                                                        op0=ALU.mult, op1=ALU.add), waits=waits)
            tb = B.op("act", I("sqrt", out=dst, in_=dst), waits=[ta])
            tc = B.op("dve", I("reciprocal", out=dst, in_=dst), waits=[tb])
            return ta, tc

        S_c = B.new_sem("s_cst")
        S_c2 = B.new_sem("s_cst2")
        B.op("pool", I("dma_start", out=cbf, in_=cstbf_d[:, :]), sem=S_c, amt=16)
        B.op("sp", I("dma_start", out=csts, in_=csts_d[:, :]), sem=S_c2, amt=16)
        t_cst = (S_c2, 16)
        t_cbf = (S_c, 16)
        t_z = B.op("dve", I("memset", small, 0.0))
        t_z = B.op("dve", I("memset", small2, 0.0))

        dbg_toks = []
        S_dbg = B.new_sem("s_dbg")

        def dump(name, src_ap, waits):
            dbg_toks.append(B.op("sp", I("dma_start", out=dbg[name], in_=src_ap), waits=waits,
                                 sem=S_dbg, amt=16))

        def maybe_stop(level):
            if stop_after <= level:
                raise _Stop()

        out_toks = []
        try:

            A.reset(m_const)
            wk = A.alloc([16, 1024], BF16)
            wvv = A.alloc([16, 1024], BF16)
            wc = A.alloc([16, 384], BF16)
            wkn_a = A.alloc([2, 1024], BF16)
            wkvv_a = A.alloc([2, 1024], BF16)
            xa = [A.alloc([16, 512], BF16) for _ in range(2)]
            hTa = [A.alloc([16, 512], BF16) for _ in range(2)]
            sqa = A.alloc([8, 512], BF16)
            rstdb_a = A.alloc([512], F32)
            rcol = A.alloc([8], F32)
            ckvf = A.alloc([2, 512], F32)
            sq2 = [A.alloc([512], BF16) for _ in range(2)]
            rstdkv = A.alloc([512], F32)
            ckvn_a = A.alloc([2, 512], BF16)
            kstA = [A.alloc([512], BF16) for _ in range(2)]
            vstA = [A.alloc([512], BF16) for _ in range(2)]
            posi_a = A.alloc([512], I32, parts=64)
            posf_a = A.alloc([512], F32, parts=64)
            ang_a = A.alloc([512], F32, parts=64)
            ry_a = A.alloc([512], F32, parts=64)
            rk_a = A.alloc([512], F32, parts=64)
            cos_a = A.alloc([512], F32, parts=64)
            sin_a = A.alloc([512], F32, parts=64)
            rt1a = A.alloc([512], F32, parts=64)
            rt2a = A.alloc([512], F32, parts=64)
            krot_a = A.alloc([512], BF16, parts=64)

            def wsrc(ap_):
                return ap_.rearrange("(c p) n -> p c n", p=128)

            S_wa = B.new_sem("s_wa")
            S_wa2 = B.new_sem("s_wa2")
            t_pnwc = B.op("sp", I("dma_start", out=pnwc, in_=pnwc_d[:, :]), sem=S_wa2, amt=16)
            B.op("pool", I("dma_start", out=wk, in_=wsrc(win_d[:, 1024:2048])), sem=S_wa, amt=16)
            B.op("pool", I("dma_start", out=wc[:, :, 0:320], in_=wsrc(win_d[:, 4608:4928])), sem=S_wa, amt=16)
            B.op("pool", I("dma_start", out=wc[:, :, 320:384], in_=wsrc(wkrs_d[:, :])), sem=S_wa, amt=16)
            B.op("pool", I("dma_start", out=wvv, in_=wsrc(win_d[:, 2048:3072])), sem=S_wa, amt=16)
            B.op("pool", I("dma_start", out=wkn_a, in_=wsrc(wkn_d[:, :])), sem=S_wa, amt=16)
            B.op("pool", I("dma_start", out=wkvv_a, in_=wsrc(wkvv_d[:, :])), sem=S_wa, amt=16)
            t_wa = (S_wa, 6 * 16)

            banksA = Banks(PS[0:6])
            SSb = PS[6]
            CSb = PS[7]
            ss_free = [None]
            cs_free = [None]
            pro = {}
            S_xa = [B.new_sem("s_xa%d" % i) for i in range(2)]
            S_pa = B.new_sem("s_posa")
            S_ka = [B.new_sem("s_ka%d" % i) for i in range(2)]
            S_va = [B.new_sem("s_va%d" % i) for i in range(2)]
            S_kra = B.new_sem("s_kra")
            xa_free = [[], []]
            hTa_free = [None, None]
            sqa_free = [None]
            posi_free = [None]
            kstA_free = [None, None]
            vstA_free = [None, None]
            krotA_free = [None]
            kA = [0]
            vA = [0]
            PI = math.pi
            INV2PI = 1.0 / (2.0 * PI)
            C1 = 6.28125
            C2 = 2.0 * PI - C1
            invf = csts[0:64, 6:7]
            sgn = csts[0:64, 7:8]
            NT = SEQ // 512

            def mmg(out_ap, lhs_fn, rhs_fn, nch, waits):
                tok = None
                for c in range(nch):
                    tok = B.op("pe", I("matmul", out_ap, lhsT=lhs_fn(c), rhs=rhs_fn(c),
                                       start=(c == 0), stop=(c == nch - 1)),
                               waits=waits if c == 0 else [], inc=(c == nch - 1))
                return tok

            def kst_out(bank, t_mm, mul_rstd, dst_ap, t_rb_):
                ks = kA[0] % 2
                kA[0] += 1
                if mul_rstd:
                    t_ev = B.op("dve", I("tensor_tensor", out=kstA[ks], in0=bank, in1=rstdb_a, op=ALU.mult),
                                waits=[t_mm, t_rb_, kstA_free[ks]])
                else:
                    t_ev = B.op("act", I("copy", out=kstA[ks], in_=bank), waits=[t_mm, kstA_free[ks]])
                t_d = B.op("sp", I("dma_start", out=dst_ap, in_=kstA[ks]), waits=[t_ev], sem=S_ka[ks], amt=16)
                kstA_free[ks] = t_d
                return t_ev

            def vst_out(bank, t_mm, scale_ap, dst_ap, t_rc_):
                vs = vA[0] % 2
                vA[0] += 1
                if scale_ap is not None:
                    t_ev = B.op("act", I("activation", out=vstA[vs], in_=bank, func=AF.Copy, scale=scale_ap),
                                waits=[t_mm, t_rc_, vstA_free[vs]])
                else:
                    t_ev = B.op("dve", I("tensor_copy", out=vstA[vs], in_=bank), waits=[t_mm, vstA_free[vs]])
                t_d = B.op("sp", I("dma_start", out=dst_ap, in_=vstA[vs].rearrange("s (h d) -> s h d", h=4)),
                           waits=[t_ev], sem=S_va[vs], amt=16)
                vstA_free[vs] = t_d
                return t_ev

            def prologue(T):
                sl = T % 2
                t0 = T * 512
                t_xa = B.op("pool", I("dma_start", out=xa[sl], in_=xT_d[:, t0:t0 + 512].rearrange("(c p) t -> p c t", p=128)),
                            waits=xa_free[sl], sem=S_xa[sl], amt=16)
                t_pos = B.op("sp", I("dma_start", out=posi_a, in_=posa_d[:, t0:t0 + 512]), waits=[posi_free[0]],
                             sem=S_pa, amt=16)
                t_sq = None
                t_on = None
                t_cs = None
                for half in range(2):
                    t_sq = B.op("act", I("activation", out=sqa, in_=xa[sl][:, 8 * half:8 * half + 8, :], func=AF.Square),
                                waits=[t_xa, sqa_free[0]])
                    for c in range(8):
                        t_on = B.op("pe", I("matmul", SSb, lhsT=ones, rhs=sqa[:, c, :],
                                            start=(half == 0 and c == 0), stop=(half == 1 and c == 7)),
                                    waits=[t_sq, ss_free[0], t_cbf] if c == 0 else [], inc=(c == 7))
                    for tb in range(4):
                        for c in range(8):
                            t_cs = B.op("pe", I("matmul", CSb[:, 2 * tb + half:2 * tb + half + 1],
                                                lhsT=sqa[:, c, tb * 128:(tb + 1) * 128], rhs=ones[:, 0:1],
                                                start=(c == 0), stop=(c == 7)),
                                        waits=[cs_free[0]] if (c == 0 and tb == 0 and half == 0) else [], inc=(c == 7))
                    sqa_free[0] = t_cs
                t_h = B.op("dve", I("tensor_tensor", out=hTa[sl], in0=xa[sl],
                                    in1=pnwc.unsqueeze(2).to_broadcast([128, 16, 512]), op=ALU.mult),
                           waits=[t_xa, hTa_free[sl], t_wa, t_pnwc])
                xa_free[sl] = [t_h, t_sq]
                pro[T] = (t_pos, t_on, t_cs, t_h)

            prologue(0)
            for T in range(NT):
                sl = T % 2
                t0 = T * 512
                t_pos, t_on, t_cs, t_h = pro[T]
                t_ra, t_rb = rsqrt_chain(rstdb_a, SSb, 1.0 / D, [t_on, now("pe"), now("dve")])
                ss_free[0] = t_ra
                csv = CSb[:, 0:8].rearrange("p (t h) -> p t h", h=2)
                t_c1 = B.op("dve", I("tensor_reduce", out=rcol[:, 0:4], in_=csv, axis=mybir.AxisListType.X, op=ALU.add),
                            waits=[t_cs, now("act")])
                cs_free[0] = t_c1
                t_c2, t_rc = rsqrt_chain(rcol[:, 4:8], rcol[:, 0:4], 1.0 / D, [t_c1])
                tq = B.op("dve", I("tensor_copy", out=posf_a, in_=posi_a), waits=[t_pos, now("act")])
                posi_free[0] = tq
                tq = B.op("dve", I("tensor_scalar", out=ang_a, in0=posf_a, scalar1=invf, scalar2=None, op0=ALU.mult),
                          waits=[tq, t_cst])
                ty = B.op("dve", I("tensor_scalar", out=ry_a, in0=ang_a, scalar1=INV2PI, scalar2=0.5,
                                   op0=ALU.mult, op1=ALU.add), waits=[tq])
                tk = B.op("dve", I("tensor_copy", out=posi_a, in_=ry_a), waits=[ty])
                tkf = B.op("dve", I("tensor_copy", out=rk_a, in_=posi_a), waits=[tk])
                posi_free[0] = tkf
                tg = B.op("dve", I("tensor_tensor", out=ry_a, in0=rk_a, in1=ry_a, op=ALU.is_gt), waits=[tkf])
                tm = B.op("dve", I("tensor_tensor", out=rk_a, in0=rk_a, in1=ry_a, op=ALU.subtract), waits=[tg])
                tr1 = B.op("dve", I("scalar_tensor_tensor", out=ang_a, in0=rk_a, scalar=-C1, in1=ang_a,
                                    op0=ALU.mult, op1=ALU.add), waits=[tm])
                tr2 = B.op("dve", I("scalar_tensor_tensor", out=ang_a, in0=rk_a, scalar=-C2, in1=ang_a,
                                    op0=ALU.mult, op1=ALU.add), waits=[tr1])
                tc1 = B.op("dve", I("tensor_scalar", out=ang_a, in0=ang_a, scalar1=PI, scalar2=-PI,
                                    op0=ALU.min, op1=ALU.max), waits=[tr2])
                ts3 = B.op("act", I("activation", out=sin_a, in_=ang_a, func=AF.Sin), waits=[tc1, now("dve")])
                ts4 = B.op("dve", I("tensor_scalar", out=sin_a, in0=sin_a, scalar1=sgn, scalar2=None, op0=ALU.mult),
                           waits=[ts3])
                t5 = B.op("dve", I("tensor_scalar", out=rk_a, in0=ang_a, scalar1=0.5 * PI, scalar2=None, op0=ALU.add),
                          waits=[ts3, tm])
                t5b = B.op("dve", I("tensor_single_scalar", out=ry_a, in_=rk_a, scalar=PI, op=ALU.is_gt), waits=[t5])
                t6 = B.op("dve", I("scalar_tensor_tensor", out=rk_a, in0=ry_a, scalar=-2.0 * PI, in1=rk_a,
                                   op0=ALU.mult, op1=ALU.add), waits=[t5b])
                t6b = B.op("dve", I("tensor_scalar", out=rk_a, in0=rk_a, scalar1=PI, scalar2=-PI,
                                    op0=ALU.min, op1=ALU.max), waits=[t6])
                ts7 = B.op("act", I("activation", out=cos_a, in_=rk_a, func=AF.Sin), waits=[t6b])
                t_mm = None
                for h in range(8):
                    bi, bank, bfree = banksA.get()
                    t_mm = mmg(bank, lambda c, h=h: wk[:, c, h * 128:(h + 1) * 128], lambda c: hTa[sl][:, c, :], 16,
                               [t_h, t_wa, bfree])
                    t_ev = kst_out(bank, t_mm, True, kt_sb_d[h * 128:(h + 1) * 128, t0:t0 + 512], t_rb)
                    banksA.release(bi, t_ev)
                if T + 1 < NT:
                    prologue(T + 1)
                lat = []
                for j in range(2):
                    bi, bank, bfree = banksA.get()
                    t_mm = mmg(bank, lambda c, j=j: wc[:, c, j * 128:(j + 1) * 128], lambda c: hTa[sl][:, c, :], 16,
                               [t_h, t_wa, bfree])
                    lat.append((bi, bank, t_mm))
                b1, bk1, f1 = banksA.get()
                t_m1 = mmg(bk1[0:64, :], lambda c: wc[:, c, 256:320], lambda c: hTa[sl][:, c, :], 16, [t_h, t_wa, f1])
                b2, bk2, f2 = banksA.get()
                t_m2 = mmg(bk2[0:64, :], lambda c: wc[:, c, 320:384], lambda c: hTa[sl][:, c, :], 16, [t_h, t_wa, f2])
                bs2, SS2, fs2 = banksA.get()
                t_o2 = None
                for j in range(2):
                    bi, bank, t_mm = lat[j]
                    t_f = B.op("dve", I("tensor_tensor", out=ckvf[:, j, :], in0=bank, in1=rstdb_a, op=ALU.mult),
                               waits=[t_mm, t_rb, now("pe")])
                    banksA.release(bi, t_f)
                    t_s2 = B.op("act", I("activation", out=sq2[j], in_=ckvf[:, j, :], func=AF.Square), waits=[t_f, now("pe")])
                    t_o2 = B.op("pe", I("matmul", SS2, lhsT=ones, rhs=sq2[j], start=(j == 0), stop=(j == 1)),
                                waits=[t_s2, fs2 if j == 0 else None])
                t_ka, t_kb = rsqrt_chain(rstdkv, SS2, 1.0 / 256.0, [t_o2, now("dve")])
                banksA.release(bs2, t_ka)
                t_n = None
                for j in range(2):
                    t_n = B.op("dve", I("scalar_tensor_tensor", out=ckvn_a[:, j, :], in0=ckvf[:, j, :],
                                        scalar=csts[:, 4 + j:5 + j], in1=rstdkv, op0=ALU.mult, op1=ALU.mult),
                               waits=[t_kb, now("pe"), t_cst])
                tr_1 = B.op("dve", I("tensor_tensor", out=rt1a, in0=bk1[0:64, :], in1=cos_a, op=ALU.mult),
                            waits=[t_m1, ts7])
                banksA.release(b1, tr_1)
                tr_2 = B.op("dve", I("tensor_tensor", out=rt2a, in0=bk2[0:64, :], in1=sin_a, op=ALU.mult),
                            waits=[t_m2, ts4])
                banksA.release(b2, tr_2)
                tr_3 = B.op("dve", I("tensor_tensor", out=rt1a, in0=rt1a, in1=rt2a, op=ALU.add), waits=[tr_1, tr_2])
                tr_4 = B.op("dve", I("tensor_tensor", out=krot_a, in0=rt1a, in1=rstdb_a[0:64, :], op=ALU.mult),
                            waits=[tr_3, t_rb, krotA_free[0]])
                krotA_free[0] = B.op("sp", I("dma_start", out=kr_d[:, t0:t0 + 512], in_=krot_a), waits=[tr_4],
                                     sem=S_kra, amt=16)
                for tb in range(4):
                    gb = 4 * T + tb
                    for half in range(2):
                        bi, bank, bfree = banksA.get()
                        t_mm = mmg(bank, lambda c, tb=tb: hTa[sl][:, c, tb * 128:(tb + 1) * 128],
                                   lambda c, half=half: wvv[:, c, half * 512:(half + 1) * 512], 16, [t_h, t_wa, bfree])
                        dview = v_sb_d[half * 512:(half + 1) * 512, gb * 128:(gb + 1) * 128].rearrange(
                            "(h s) d -> s h d", s=128)
                        t_ev = vst_out(bank, t_mm, rcol[:, 4 + tb:5 + tb], dview, t_rc)
                        banksA.release(bi, t_ev)
                hTa_free[sl] = t_mm
                for h in range(8):
                    bi, bank, bfree = banksA.get()
                    t_mm = mmg(bank, lambda c, h=h: wkn_a[:, c, h * 128:(h + 1) * 128], lambda c: ckvn_a[:, c, :], 2,
                               [t_n, t_wa, bfree])
                    t_ev = kst_out(bank, t_mm, False, kt_ml_d[h * 128:(h + 1) * 128, t0:t0 + 512], None)
                    banksA.release(bi, t_ev)
                for tb in range(4):
                    gb = 4 * T + tb
                    for half in range(2):
                        bi, bank, bfree = banksA.get()
                        t_mm = mmg(bank, lambda c, tb=tb: ckvn_a[:, c, tb * 128:(tb + 1) * 128],
                                   lambda c, half=half: wkvv_a[:, c, half * 512:(half + 1) * 512], 2, [t_n, t_wa, bfree])
                        dview = v_ml_d[half * 512:(half + 1) * 512, gb * 128:(gb + 1) * 128].rearrange(
                            "(h s) d -> s h d", s=128)
                        t_ev = vst_out(bank, t_mm, None, dview, None)
                        banksA.release(bi, t_ev)
            phA_all = all_now() + [(S_ka[0], S_ka[0][1]), (S_ka[1], S_ka[1][1]), (S_va[0], S_va[0][1]),
                                   (S_va[1], S_va[1][1]), (S_kra, S_kra[1])]
            t_ag_sb = None
            t_ag_mla = None
            A.reset(m_persist)
            maybe_stop(-1)

            hT = A.alloc([16, TL], BF16)
            cosT = A.alloc([TL], F32, parts=64)
            sinS = A.alloc([TL], F32, parts=64)
            cosTq = A.alloc([TL], F32, parts=64)
            sinSq = A.alloc([TL], F32, parts=64)
            m_ph1 = A.mark()
            wg = [A.alloc([8192], BF16) for i in range(2)]
            cqn = A.alloc([4, TL], BF16)
            ckvn = A.alloc([2, TL], BF16)
            kst = [A.alloc([TL], BF16) for i in range(2)]
            vst = [A.alloc([512], BF16) for i in range(2)]
            sq = [A.alloc([512], BF16) for i in range(2)]
            rstdb = A.alloc([512], F32)
            rt1 = A.alloc([512], F32, parts=64)
            rt2 = A.alloc([512], F32, parts=64)
            krot = A.alloc([TL], BF16, parts=64)
            A.reset(m_ph1)

            posi = A.alloc([TL], I32, parts=64)
            posf = A.alloc([TL], F32, parts=64)
            ang = A.alloc([TL], F32, parts=64)
            rr_y = A.alloc([TL], F32, parts=64)
            rr_k = A.alloc([TL], F32, parts=64)
            xt = [A.alloc([D], F32) for i in range(2)]
            xn = [A.alloc([D], BF16) for i in range(2)]
            junk = A.alloc([D], BF16)
            pnwb = A.alloc([D], F32)

            S_p = B.new_sem("s_pos")
            B.op("sp", I("dma_start", out=posi, in_=posb_d[:, :]), waits=phA_all, sem=S_p, amt=16)
            B.op("sp", I("dma_start", out=pnwb, in_=pnwb_d[:, :]), waits=phA_all, sem=S_p, amt=16)
            t_pnw = (S_p, 32)
            t = B.op("dve", I("tensor_copy", out=posf, in_=posi), waits=[t_pnw])
            invf = csts[0:64, 6:7]
            sgn = csts[0:64, 7:8]
            PI = math.pi
            INV2PI = 1.0 / (2.0 * PI)
            C1 = 6.28125
            C2 = 2.0 * PI - C1
            t1 = B.op("dve", I("tensor_scalar", out=ang, in0=posf, scalar1=invf, scalar2=None, op0=ALU.mult),
                      waits=[t, t_cst])
            ty = B.op("dve", I("tensor_scalar", out=rr_y, in0=ang, scalar1=INV2PI, scalar2=0.5,
                                                        op0=ALU.mult, op1=ALU.add), waits=[t1])
            tk = B.op("dve", I("tensor_copy", out=posi, in_=rr_y), waits=[ty])
            tkf = B.op("dve", I("tensor_copy", out=rr_k, in_=posi), waits=[tk])
            tg = B.op("dve", I("tensor_tensor", out=rr_y, in0=rr_k, in1=rr_y, op=ALU.is_gt), waits=[tkf])
            tm = B.op("dve", I("tensor_tensor", out=rr_k, in0=rr_k, in1=rr_y, op=ALU.subtract), waits=[tg])
            tr1 = B.op("dve", I("scalar_tensor_tensor", out=ang, in0=rr_k, scalar=-C1, in1=ang,
                                                                op0=ALU.mult, op1=ALU.add), waits=[tm])
            tr2 = B.op("dve", I("scalar_tensor_tensor", out=ang, in0=rr_k, scalar=-C2, in1=ang,
                                                                op0=ALU.mult, op1=ALU.add), waits=[tr1])
            tc1 = B.op("dve", I("tensor_scalar", out=ang, in0=ang, scalar1=PI, scalar2=-PI,
                                                         op0=ALU.min, op1=ALU.max), waits=[tr2])
            t3 = B.op("act", I("activation", out=sinS, in_=ang, func=AF.Sin), waits=[tc1])
            t4 = B.op("dve", I("tensor_scalar", out=sinS, in0=sinS, scalar1=sgn, scalar2=None, op0=ALU.mult),
                      waits=[t3])
            t_sinq = B.op("dve", I("tensor_scalar", out=sinSq, in0=sinS, scalar1=SC_MLA, scalar2=None, op0=ALU.mult),
                          waits=[t4])
            t5 = B.op("dve", I("tensor_scalar", out=rr_k, in0=ang, scalar1=0.5 * PI, scalar2=None, op0=ALU.add),
                      waits=[t3, tm])
            t5b = B.op("dve", I("tensor_single_scalar", out=rr_y, in_=rr_k, scalar=PI, op=ALU.is_gt), waits=[t5])
            t6 = B.op("dve", I("scalar_tensor_tensor", out=rr_k, in0=rr_y, scalar=-2.0 * PI, in1=rr_k,
                                                               op0=ALU.mult, op1=ALU.add), waits=[t5b])
            t6b = B.op("dve", I("tensor_scalar", out=rr_k, in0=rr_k, scalar1=PI, scalar2=-PI,
                                                         op0=ALU.min, op1=ALU.max), waits=[t6])
            t7 = B.op("act", I("activation", out=cosT, in_=rr_k, func=AF.Sin), waits=[t6b])
            t_cosq = B.op("dve", I("tensor_scalar", out=cosTq, in0=cosT, scalar1=SC_MLA, scalar2=None, op0=ALU.mult),
                          waits=[t7])
            t_tabs = [t4, t_sinq, t7, t_cosq]

            S_x = [B.new_sem("s_x%d" % i) for i in range(2)]
            xn_free = [None, None]
            xt_free = [None, None]
            tpb = [PS[0].bitcast(BF16), PS[1].bitcast(BF16)]
            tp_free = [None, None]
            tpi = 0
            hT_toks = []
            for b in range(NB):
                s = b % 2
                t_x = B.op("sp", I("dma_start", out=xt[s], in_=x_d[b * 128:(b + 1) * 128, :]),
                           waits=[xt_free[s]] + phA_all, sem=S_x[s], amt=16)
                t_sq = B.op("act", I("activation", out=junk, in_=xt[s], func=AF.Square,
                                                                   accum_out=small[:, b:b + 1]), waits=[t_x, t_z])
                t_r1, t_r2 = rsqrt_chain(small[:, 16 + b:17 + b], small[:, b:b + 1], 1.0 / D, [t_sq])
                t_xn = B.op("dve", I("scalar_tensor_tensor",
                    out=xn[s], in0=xt[s], scalar=small[:, 16 + b:17 + b], in1=pnwb,
                    op0=ALU.mult, op1=ALU.mult), waits=[t_r2, t_x, t_pnw, xn_free[s]])
                xt_free[s] = t_xn
                for g in range(4):
                    ti = tpi % 2
                    tpi += 1
                    for j in range(4):
                        c = 4 * g + j
                        t_tp = B.op("pe", I("transpose",
                            out=tpb[ti][:, j * 128:(j + 1) * 128], in_=xn[s][:, c * 128:(c + 1) * 128], identity=ident),
                            waits=[t_xn, tp_free[ti], t_cbf] if j == 0 else [], inc=(j == 3))
                    src = tpb[ti][:, 0:512].rearrange("p (j t) -> p j t", j=4)
                    dst = hT[:, 4 * g:4 * g + 4, b * 128:(b + 1) * 128]
                    if g % 2 == 0:
                        t_ev = B.op("act", I("copy", out=dst, in_=src), waits=[t_tp])
                    else:
                        t_ev = B.op("dve", I("tensor_copy", out=dst, in_=src), waits=[t_tp])
                    tp_free[ti] = t_ev
                    hT_toks.append(t_ev)
                xn_free[s] = t_tp
            ph0_done = hT_toks[-8:] + t_tabs
            if stop_after <= 0:
                dump("d_hT", hT.rearrange("p c t -> p (c t)"), ph0_done)
                for i_, tb_ in enumerate((cosT, sinS, cosTq, sinSq)):
                    dump("d_tab", tb_, ph0_done) if False else dbg_toks.append(B.op(
                        "sp", I("dma_start", out=dbg["d_tab"][:, i_ * TL:(i_ + 1) * TL], in_=tb_),
                        waits=ph0_done, sem=S_dbg, amt=16))
            maybe_stop(0)

            banks = Banks(PS)
            banks.free[0] = tp_free[0]
            banks.free[1] = tp_free[1]
            S_w = [B.new_sem("s_w%d" % i) for i in range(2)]
            wg_free = [None, None]
            wslot = [0]

            def load_w(parts):
                s = wslot[0] % 2
                wslot[0] += 1
                tok = None
                for (src, dst) in parts:
                    tok = B.op("pool", I("dma_start", out=dst, in_=src),
                               waits=[wg_free[s]] + ph0_done, sem=S_w[s], amt=16)
                return s, tok

            def wview(s, off, nch, width):
                return wg[s][:, off:off + nch * width].rearrange("p (c n) -> p c n", c=nch)

            def mm_group(out_ap, lhs_fn, rhs_fn, nch, waits):
                tok = None
                for c in range(nch):
                    tok = B.op("pe", I("matmul", out_ap, lhsT=lhs_fn(c), rhs=rhs_fn(c),
                                                              start=(c == 0), stop=(c == nch - 1)),
                               waits=waits if c == 0 else [], inc=(c == nch - 1))
                return tok

            S_k = [B.new_sem("s_kst%d" % i) for i in range(2)]
            S_v = [B.new_sem("s_vst%d" % i) for i in range(2)]
            kst_free = [None, None]
            vst_free = [None, None]
            kcnt = [0]
            vcnt = [0]
            snd_sb_toks = []
            snd_mla_toks = []

            def win_cols(c0, n):
                return win_d[:, c0:c0 + n].rearrange("(c p) n -> p c n", p=128)

            def proj_k_heads(wv, t_w, s_w, nheads, head0, dst, toks, nch, rhs_tile, rdy):
                for hh in range(nheads):
                    h = head0 + hh
                    ks = kcnt[0] % 2
                    kcnt[0] += 1
                    evs = []
                    for tt in range(2):
                        bi, bank, bfree = banks.get()
                        t_mm = mm_group(bank, lambda c, hh=hh: wv[:, c, hh * 128:(hh + 1) * 128],
                                        lambda c, tt=tt: rhs_tile[:, c, tt * 512:(tt + 1) * 512], nch,
                                        [t_w, bfree] + rdy)
                        t_ev = B.op("act", I("copy",
                            out=kst[ks][:, tt * 512:(tt + 1) * 512], in_=bank), waits=[t_mm, kst_free[ks]])
                        banks.release(bi, t_ev)
                        evs.append(t_ev)
                    wg_free[s_w] = t_mm
                    r0 = h * 128
                    t_d = B.op("sp", I("dma_start", out=dst[r0:r0 + 128, :], in_=kst[ks]),
                               waits=evs, sem=S_k[ks], amt=16)
                    kst_free[ks] = t_d
                    toks.append(t_d)

            def proj_v_tok(wv, t_w, s_w, ch, nch, lhs_tile, dst, row0, toks, rdy):
                for b in range(NB):
                    bi, bank, bfree = banks.get()
                    t_mm = mm_group(bank, lambda c, b=b: lhs_tile[:, c, b * 128:(b + 1) * 128],
                                    lambda c: wv[:, c, :], nch, [t_w, bfree] + rdy)
                    vs = vcnt[0] % 2
                    vcnt[0] += 1
                    t_ev = B.op("dve", I("tensor_copy", out=vst[vs], in_=bank),
                                waits=[t_mm, vst_free[vs]])
                    banks.release(bi, t_ev)
                    r0 = row0 + ch * 512
                    dview = dst[r0:r0 + 512, b * 128:(b + 1) * 128].rearrange("(h s) d -> s h d", s=128)
                    t_d = B.op("sp", I("dma_start",
                        out=dview, in_=vst[vs].rearrange("s (h d) -> s h d", h=4)),
                        waits=[t_ev], sem=S_v[vs], amt=16)
                    vst_free[vs] = t_d
                    toks.append(t_d)
                wg_free[s_w] = t_mm

            sqfree = [None, None]
            rstd_free = [None]
            rope_free = [None]

            def latent_norm(wv, t_w, ncb, nfeat, nw_col0, dst, extra_fn=None):
                last = None
                for tt in range(2):
                    lb = []
                    for j in range(ncb):
                        bi, bank, bfree = banks.get()
                        t_mm = mm_group(bank, lambda c, j=j: wv[:, c, j * 128:(j + 1) * 128],
                                        lambda c, tt=tt: hT[:, c, tt * 512:(tt + 1) * 512], 16,
                                        [t_w, bfree] + ph0_done)
                        lb.append((bi, bank, t_mm))
                    ex = extra_fn(tt) if extra_fn is not None else None
                    bs, ssb, ssfree = banks.get()
                    t_o = None
                    for j in range(ncb):
                        bi, bank, t_mm = lb[j]
                        t_s = B.op("act", I("activation", out=sq[j % 2], in_=bank, func=AF.Square),
                                   waits=[t_mm, sqfree[j % 2]])
                        t_o = B.op("pe", I("matmul", ssb, lhsT=ones, rhs=sq[j % 2],
                                                                  start=(j == 0), stop=(j == ncb - 1)),
                                   waits=[t_s, ssfree if j == 0 else None, t_cbf])
                        sqfree[j % 2] = t_o
                    t_a, t_b = rsqrt_chain(rstdb, ssb, 1.0 / nfeat, [t_o, rstd_free[0]])
                    banks.release(bs, t_a)
                    for j in range(ncb):
                        bi, bank, t_mm = lb[j]
                        t_n = B.op("dve", I("scalar_tensor_tensor",
                            out=dst[:, j, tt * 512:(tt + 1) * 512], in0=bank, scalar=csts[:, nw_col0 + j:nw_col0 + j + 1],
                            in1=rstdb, op0=ALU.mult, op1=ALU.mult), waits=[t_b, t_mm, t_cst])
                        banks.release(bi, t_n)
                        last = t_n
                    rstd_free[0] = last
                    if ex is not None:
                        ex()
                return last

            def rope(pa, pb, ta, tb, ct, sn, dst_ap):
                t1 = B.op("dve", I("tensor_tensor", out=rt1, in0=pa, in1=ct, op=ALU.mult),
                          waits=[ta, rope_free[0]] + t_tabs)
                t2 = B.op("dve", I("tensor_tensor", out=rt2, in0=pb, in1=sn, op=ALU.mult),
                          waits=[tb] + t_tabs)
                t3 = B.op("dve", I("tensor_tensor", out=dst_ap, in0=rt1, in1=rt2, op=ALU.add),
                          waits=[t1, t2])
                rope_free[0] = t3
                return t1, t2, t3

            def proj_fm_keep(col0, dst, chunk0, func, scale):
                toks = []
                for gi in range(2):
                    s_w = wslot[0] % 2
                    s_w, t_w = load_w([(win_cols(col0 + gi * 512, 512), wview(s_w, 0, 16, 512))])
                    wv = wview(s_w, 0, 16, 512)
                    for hh in range(4):
                        ch = chunk0 + gi * 4 + hh
                        for tt in range(2):
                            bi, bank, bfree = banks.get()
                            t_mm = mm_group(bank, lambda c, hh=hh, wv=wv: wv[:, c, hh * 128:(hh + 1) * 128],
                                            lambda c, tt=tt: hT[:, c, tt * 512:(tt + 1) * 512], 16,
                                            [t_w, bfree] + ph0_done)
                            t_ev = B.op("act", I("activation",
                                out=dst[:, ch, tt * 512:(tt + 1) * 512], in_=bank, func=func, scale=scale), waits=[t_mm])
                            banks.release(bi, t_ev)
                            toks.append(t_ev)
                    wg_free[s_w] = t_mm
                return toks

            proj_fm_keep(0, qT_sb, 0, AF.Copy, SC_SB)
            proj_fm_keep(3072, gate, 0, AF.Silu, 1.0)
            proj_fm_keep(4928, gate, 8, AF.Silu, 1.0)

            s_w = wslot[0] % 2
            s_w, t_w = load_w([(win_cols(4096, 512), wview(s_w, 0, 16, 512))])
            t_cqn = latent_norm(wview(s_w, 0, 16, 512), t_w, 4, 512, 0, cqn, None)
            wg_free[s_w] = now("pe")

            s_w = wslot[0] % 2
            s_w, t_w = load_w([
                (wqn_d[:, :].rearrange("(c p) n -> p c n", p=128), wview(s_w, 0, 4, 1024)),
                (wqr_d[:, :].rearrange("(c p) n -> p c n", p=128), wview(s_w, 4096, 4, 512)),
                (wqrs_d[:, :].rearrange("(c p) n -> p c n", p=128), wview(s_w, 6144, 4, 512)),
            ])
            t_w = (S_w[s_w], S_w[s_w][1])
            wqn_v = wview(s_w, 0, 4, 1024)
            wqr_v = wview(s_w, 4096, 4, 512)
            wqrs_v = wview(s_w, 6144, 4, 512)
            for h in range(8):
                for tt in range(2):
                    bi, bank, bfree = banks.get()
                    t_mm = mm_group(bank, lambda c, h=h: wqn_v[:, c, h * 128:(h + 1) * 128],
                                    lambda c, tt=tt: cqn[:, c, tt * 512:(tt + 1) * 512], 4, [t_w, bfree, t_cqn])
                    t_ev = B.op("act", I("activation",
                        out=qN[:, h, tt * 512:(tt + 1) * 512], in_=bank, func=AF.Copy, scale=SC_MLA), waits=[t_mm])
                    banks.release(bi, t_ev)
                    b1, bk1, f1 = banks.get()
                    t_m1 = mm_group(bk1[0:64, :], lambda c, h=h: wqr_v[:, c, h * 64:(h + 1) * 64],
                                    lambda c, tt=tt: cqn[:, c, tt * 512:(tt + 1) * 512], 4, [t_w, f1, t_cqn])
                    b2, bk2, f2 = banks.get()
                    t_m2 = mm_group(bk2[0:64, :], lambda c, h=h: wqrs_v[:, c, h * 64:(h + 1) * 64],
                                    lambda c, tt=tt: cqn[:, c, tt * 512:(tt + 1) * 512], 4, [t_w, f2, t_cqn])
                    t1, t2, t3 = rope(bk1[0:64, :], bk2[0:64, :], t_m1, t_m2, cosTq[:, tt * 512:(tt + 1) * 512],
                                      sinSq[:, tt * 512:(tt + 1) * 512], qR[:, h, tt * 512:(tt + 1) * 512])
                    banks.release(b1, t1)
                    banks.release(b2, t2)
            ph1_all = all_now()

            if debug:
                dump("d_hT", hT.rearrange("p c t -> p (c t)"), ph1_all)
                dump("d_qsb", qT_sb.rearrange("p c t -> p (c t)"), ph1_all)
                dump("d_qn", qN.rearrange("p c t -> p (c t)"), ph1_all)
                dump("d_qr", qR.rearrange("p c t -> p (c t)"), ph1_all)
                dump("d_gate", gate.rearrange("p c t -> p (c t)"), ph1_all)
                dump("d_cqn", cqn.rearrange("p c t -> p (c t)"), ph1_all)
                ph1_all = ph1_all + [(S_dbg, S_dbg[1])]
            maybe_stop(1)

            if True:
                A.reset(m_persist)
                kbuf = [A.alloc([64, 128], BF16) for _ in range(2)]
                vbuf = [A.alloc([64, 128], BF16) for _ in range(2)]
                krbuf = A.alloc([64, 128], BF16, parts=64)
                e_t = [A.alloc([2, 512], F32) for _ in range(2)]
                sp_t = [A.alloc([2, 512], BF16) for _ in range(3)]
                a_t = [A.alloc([2, 512], BF16) for _ in range(3)]
                Rt = A.alloc([512], BF16)
                rec = A.alloc([512], F32)
                otmp = A.alloc([512], F32)
                pp_t = [A.alloc([512], BF16) for _ in range(3)]
                m_ph2 = A.mark()

                ZP = [psA[:, 1024 * p_:1024 * (p_ + 1)].rearrange("p (b n) -> p b n", b=2) for p_ in range(3)]
                OACC = [PS[6], PS[6]]
                DEN = [PS[7], PS[7]]
                zfree = [None] * 3
                oacc_free = [None, None]
                den_free = [None, None]
                spfree = [[None, None] for _ in range(3)]
                afree = [None] * 3
                adve = [None] * 3
                S_kv = [B.new_sem("s_kv%d" % i) for i in range(2)]
                kv_free = [None, None]
                S_krb = B.new_sem("s_krb")
                t_krb = B.op("sp", I("dma_start", out=krbuf, in_=kr_d.rearrange("f (g s) -> f g s", s=128)),
                             waits=ph1_all + phA_all, sem=S_krb, amt=16)

                def load_head(hi):
                    s = hi % 2
                    h = hi % 8
                    if hi < 8:
                        ksrc = kt_sb_d[h * 128:(h + 1) * 128, :]
                        vsrc = v_sb_d[h * 128:(h + 1) * 128, :]
                    else:
                        ksrc = kt_ml_d[h * 128:(h + 1) * 128, :]
                        vsrc = v_ml_d[h * 128:(h + 1) * 128, :]
                    B.op("sp", I("dma_start", out=kbuf[s], in_=ksrc.rearrange("p (g s) -> p g s", s=128)),
                         waits=[kv_free[s]] + ph1_all + phA_all, sem=S_kv[s], amt=16)
                    B.op("sp", I("dma_start", out=vbuf[s], in_=vsrc.rearrange("p (g s) -> p g s", s=128)),
                         waits=[kv_free[s]] + ph1_all + phA_all, sem=S_kv[s], amt=16)
                    return (S_kv[s], S_kv[s][1])

                cnt = {"z": 0, "e": 0, "sp": 0, "a": 0, "st": 0}
                mixed_toks = []
                kv_tok = {}
                kv_tok[0] = load_head(0)
                for hi in range(16):
                    s = hi % 2
                    h = hi % 8
                    is_sb = hi < 8
                    if hi + 1 < 16:
                        kv_tok[hi + 1] = load_head(hi + 1)
                    t_kv = kv_tok[hi]
                    for u in range(2):
                        blocks = [(g, rp) for g in range(4 * u + 3, -1, -1) for rp in range(7, -1, -1)]
                        n = len(blocks)
                        oi = 0
                        cnt["st"] += 1
                        oacc = OACC[oi]
                        den = DEN[oi]
                        info = [None] * n
                        t_Rz = None
                        if is_sb:
                            t_Rz = B.op("dve", I("memset", Rt, 0.0), waits=[now("pe")] + ph1_all)
                        tR_prev = [t_Rz]

                        npair = n // 2

                        def geo(P):
                            g, rp = blocks[2 * P]
                            c0 = 128 * max(0, g - 4 * u)
                            return g, c0, (g >= 4 * u)

                        def stA(P):
                            g, c0, diag = geo(P)
                            zi = cnt["z"] % 3
                            cnt["z"] += 1
                            d = {"zi": zi}
                            info[P] = d
                            qc = slice(512 * u + c0, 512 * u + 512)
                            t = None
                            for b_ in range(2):
                                g_, rp = blocks[2 * P + b_]
                                zb = ZP[zi][:, b_, :]
                                kblk = kbuf[s][:, 8 * g + rp, :]
                                w = ([zfree[zi], t_kv] + ph1_all) if b_ == 0 else []
                                if is_sb:
                                    t = B.op("pe", I("matmul", zb[:, c0:512], lhsT=kblk, rhs=qT_sb[:, h, qc],
                                                     start=True, stop=True), waits=w, inc=not diag)
                                    if diag:
                                        t = B.op("pe", I("matmul", zb[:, c0:c0 + 128], lhsT=ident,
                                                         rhs=msk_sb[:, rp * 128:(rp + 1) * 128],
                                                         start=False, stop=False, skip_group_check=True))
                                else:
                                    B.op("pe", I("matmul", zb[:, c0:512], lhsT=kblk, rhs=qN[:, h, qc],
                                                 start=True, stop=False), waits=w, inc=False)
                                    t = B.op("pe", I("matmul", zb[:, c0:512], lhsT=krbuf[:, 8 * g + rp, :],
                                                     rhs=qR[:, h, qc], start=False, stop=True),
                                             waits=[t_krb])
                                    if diag:
                                        t = B.op("pe", I("matmul", zb[:, c0:c0 + 128], lhsT=ident,
                                                         rhs=msk_mla[:, rp * 128:(rp + 1) * 128],
                                                         start=False, stop=True, skip_group_check=True))
                            d["tz"] = t

                        def stB(P):
                            g, c0, diag = geo(P)
                            d = info[P]
                            zp = ZP[d["zi"]][:, :, c0:512]
                            if is_sb:
                                ei = cnt["e"] % 2
                                cnt["e"] += 1
                                si = cnt["sp"] % 3
                                cnt["sp"] += 1
                                d["si"] = si
                                t_e = B.op("act", I("activation", out=e_t[ei][:, :, c0:512], in_=zp, func=AF.Exp),
                                           waits=[d["tz"]])
                                t_s = B.op("act", I("activation", out=sp_t[si][:, :, c0:512], in_=e_t[ei][:, :, c0:512],
                                                    func=AF.Ln, bias=1.0, scale=1.0),
                                           waits=[t_e] + spfree[si])
                                d["tsp"] = t_s
                            else:
                                ai = cnt["a"] % 3
                                cnt["a"] += 1
                                d["ai"] = ai
                                t_a = B.op("act", I("activation", out=a_t[ai][:, :, c0:512], in_=zp, func=AF.Exp),
                                           waits=[d["tz"], afree[ai], adve[ai]])
                                d["ta"] = t_a
                                zfree[d["zi"]] = t_a
                                t_pp = B.op("dve", I("tensor_tensor", out=pp_t[ai][:, c0:512], in0=a_t[ai][:, 0, c0:512],
                                                     in1=a_t[ai][:, 1, c0:512], op=ALU.add), waits=[t_a, afree[ai]])
                                adve[ai] = t_pp
                                d["tpp"] = t_pp

                        def stC(P):
                            if not is_sb:
                                return
                            g, c0, diag = geo(P)
                            d = info[P]
                            zi = d["zi"]
                            si = d["si"]
                            first = (P == 0)
                            zb0 = ZP[zi][:, 0, c0:512]
                            zb1 = ZP[zi][:, 1, c0:512]
                            sp0 = sp_t[si][:, 0, c0:512]
                            sp1 = sp_t[si][:, 1, c0:512]
                            t = B.op("pe", I("matmul", zb0, lhsT=negtri, rhs=sp0, start=False, stop=first,
                                             skip_group_check=True), waits=[d["tsp"]], inc=False)
                            if not first:
                                B.op("pe", I("matmul", zb0, lhsT=negones, rhs=Rt[:, c0:512], start=False, stop=True,
                                             skip_group_check=True), waits=[tR_prev[0]], inc=False)
                            B.op("pe", I("matmul", zb1, lhsT=negtri, rhs=sp1, start=False, stop=False,
                                         skip_group_check=True), inc=False)
                            t = B.op("pe", I("matmul", zb1, lhsT=negones, rhs=sp0, start=False, stop=first,
                                             skip_group_check=True), inc=first)
                            if not first:
                                t = B.op("pe", I("matmul", zb1, lhsT=negones, rhs=Rt[:, c0:512], start=False, stop=True,
                                                 skip_group_check=True))
                            d["tc"] = t
                            tR1 = B.op("dve", I("tensor_tensor", out=Rt[:, c0:512], in0=Rt[:, c0:512], in1=sp0, op=ALU.add),
                                       waits=[d["tsp"], t, tR_prev[0]])
                            tR2 = B.op("dve", I("tensor_tensor", out=Rt[:, c0:512], in0=Rt[:, c0:512], in1=sp1, op=ALU.add),
                                       waits=[tR1])
                            tR_prev[0] = tR2
                            spfree[si] = [t, tR2]
                            ai = cnt["a"] % 3
                            cnt["a"] += 1
                            d["ai"] = ai
                            t_a = B.op("act", I("activation", out=a_t[ai][:, :, c0:512], in_=ZP[zi][:, :, c0:512], func=AF.Exp),
                                       waits=[t, afree[ai]])
                            d["ta"] = t_a
                            zfree[zi] = t_a

                        def stF(P):
                            g, c0, diag = geo(P)
                            d = info[P]
                            ai = d["ai"]
                            t = None
                            for b_ in range(2):
                                k = 2 * P + b_
                                g_, rp = blocks[k]
                                vblk = vbuf[s][:, 8 * g + rp, :]
                                w = [d["ta"]] if b_ == 0 else []
                                if k == 0:
                                    w += [oacc_free[oi], den_free[oi]]
                                t = B.op("pe", I("matmul", oacc[:, c0:512], lhsT=vblk, rhs=a_t[ai][:, b_, c0:512],
                                                 start=(k == 0), stop=(k == n - 1), skip_group_check=True),
                                         waits=w, inc=(is_sb and b_ == 1))
                                if (not is_sb) and b_ == 1:
                                    t = B.op("pe", I("matmul", den[:, c0:512], lhsT=ones, rhs=pp_t[ai][:, c0:512],
                                                     start=(k == 1), stop=(k == n - 1), skip_group_check=True),
                                             waits=[d["tpp"]])
                            afree[ai] = t
                            d["tf"] = t

                        for step in range(npair + 3):
                            if step == 0:
                                stA(0)
                            if step + 1 < npair:
                                stA(step + 1)
                            if step < npair:
                                stB(step)
                            if 0 <= step - 1 < npair:
                                stC(step - 1)
                            if 0 <= step - 2 < npair:
                                stF(step - 2)
                        t_last = info[npair - 1]["tf"]
                        kv_free[s] = t_last
                        ch = h if is_sb else 8 + h
                        gsl = gate[:, ch, 512 * u:512 * u + 512]
                        if is_sb:
                            t_m = B.op("dve", I("tensor_tensor", out=gsl, in0=oacc, in1=gsl, op=ALU.mult),
                                       waits=[t_last] + ph1_all)
                            oacc_free[oi] = t_m
                        else:
                            t_r = B.op("dve", I("reciprocal", out=rec, in_=den), waits=[t_last, now("dve")])
                            den_free[oi] = t_r
                            t_o = B.op("dve", I("tensor_tensor", out=otmp, in0=oacc, in1=rec, op=ALU.mult),
                                       waits=[t_r])
                            oacc_free[oi] = t_o
                            t_m = B.op("dve", I("tensor_tensor", out=gsl, in0=otmp, in1=gsl, op=ALU.mult),
                                       waits=[t_o] + ph1_all)
                        mixed_toks.append(t_m)
                ph2_all = all_now()
                if debug:
                    dbg_toks.append(B.op("sp", I("dma_start", out=dbg["d_mixed"], in_=gate.rearrange("p c t -> p (c t)")),
                                         waits=ph2_all, sem=S_dbg, amt=16))
                    ph2_all = ph2_all + [(S_dbg, S_dbg[1])]

            maybe_stop(2)
            if True:
                A.reset(m_persist)
                wout = A.alloc([16, D], BF16)
                ponwb = A.alloc([D], F32)
                xres = [A.alloc([D], F32) for _ in range(2)]
                ytile = [A.alloc([D], F32) for _ in range(2)]
                junk3 = A.alloc([512], BF16)
                S_wo = B.new_sem("s_wo")
                wo_v = wout_d.rearrange("(c p) n -> p c n", p=128)
                t_wog = []
                for q_ in range(4):
                    t_wog.append(B.op("pool", I("dma_start", out=wout[:, 4 * q_:4 * q_ + 4, :],
                                                  in_=wo_v[:, 4 * q_:4 * q_ + 4, :]),
                                      waits=ph2_all, sem=S_wo, amt=16))
                S_wo2 = B.new_sem("s_wo2")
                t_pon = B.op("sp", I("dma_start", out=ponwb, in_=ponwb_d[:, :]), waits=ph2_all, sem=S_wo2, amt=16)
                t_wo = (S_wo, 64)
                S_xr = [B.new_sem("s_xr%d" % i) for i in range(2)]
                S_o = [B.new_sem("s_o%d" % i) for i in range(2)]
                xr_free = [None, None]
                yt_free = [None, None]
                yb_free = [None] * 8
                out_toks = []
                for b in range(NB):
                    s = b % 2
                    t_xr = B.op("sp", I("dma_start", out=xres[s], in_=x_d[b * 128:(b + 1) * 128, :]),
                                waits=[xr_free[s]] + ph2_all, sem=S_xr[s], amt=16)
                    t_mms = []
                    for nq in range(4):
                        bi = 4 * s + nq
                        t_mm = None
                        for c in range(16):
                            t_mm = B.op("pe", I("matmul", PS[bi], lhsT=gate[:, c, b * 128:(b + 1) * 128],
                                                rhs=wout[:, c, nq * 512:(nq + 1) * 512], start=(c == 0), stop=(c == 15)),
                                        waits=([yb_free[bi]] + ph2_all if c == 0 else []) + [t_wog[c // 4]],
                                        inc=(c == 15))
                        t_mms.append(t_mm)
                        B.op("act", I("activation",
                            out=junk3, in_=PS[bi], func=AF.Square, accum_out=small2[:, 4 * b + nq:4 * b + nq + 1]),
                            waits=[t_mm, t_z])
                    t_sq = now("act")
                    t_s1 = B.op("dve", I("tensor_reduce", out=small2[:, 32 + b:33 + b], in_=small2[:, 4 * b:4 * b + 4],
                                                                      axis=mybir.AxisListType.X, op=ALU.add), waits=[t_sq])
                    t_s2, t_s3 = rsqrt_chain(small2[:, 48 + b:49 + b], small2[:, 32 + b:33 + b], 1.0 / D, [t_s1])
                    t_y = None
                    for nq in range(4):
                        bi = 4 * s + nq
                        cs = slice(nq * 512, (nq + 1) * 512)
                        t_y1 = B.op("dve", I("scalar_tensor_tensor",
                            out=ytile[s][:, cs], in0=PS[bi], scalar=small2[:, 48 + b:49 + b], in1=ponwb[:, cs],
                            op0=ALU.mult, op1=ALU.mult), waits=[t_s3, t_mms[nq], yt_free[s], t_wo, t_pon])
                        yb_free[bi] = t_y1
                        t_y = B.op("dve", I("tensor_tensor", out=ytile[s][:, cs], in0=ytile[s][:, cs],
                                                                                in1=xres[s][:, cs], op=ALU.add),
                                   waits=[t_y1, t_xr])
                    xr_free[s] = t_y
                    t_o = B.op("sp", I("dma_start", out=out_d[b * 128:(b + 1) * 128, :], in_=ytile[s]),
                               waits=[t_y], sem=S_o[s], amt=16)
                    yt_free[s] = t_o
                    out_toks.append(t_o)


        except _Stop:
            pass

        final_toks = list(dbg_toks) + out_toks[-2:]
        fw = [(t[0][0], t[1]) for t in final_toks]

        with nc.Block() as block:
            @block.tensor
            def _(e):
                B.replay(e, "pe")

            @block.scalar
            def _(e):
                B.replay(e, "act")

            @block.vector
            def _(e):
                B.replay(e, "dve")

            @block.gpsimd
            def _(e):
                B.replay(e, "pool")

            @block.sync
            def _(e):
                B.replay(e, "sp")
                for h_, v_ in fw:
                    e.wait_ge(h_, v_)
    return nc


def _host_inputs(x, positions, pre_norm_w, w_in, q_norm_w, w_q_up, kv_norm_w, w_kv_up, w_out, post_norm_w):
    f32 = np.float32
    x = np.asarray(x, f32)[0]
    pos = np.asarray(positions)[0].astype(np.int32)
    w_in = np.ascontiguousarray(np.asarray(w_in, f32)[0])
    w_q_up = np.asarray(w_q_up, f32)[0]
    w_kv_up = np.asarray(w_kv_up, f32)[0]
    w_out = np.ascontiguousarray(np.asarray(w_out, f32)[0])
    pnw = np.asarray(pre_norm_w, f32)[0]
    ponw = np.asarray(post_norm_w, f32)[0]
    qnw = np.asarray(q_norm_w, f32)[0]
    kvnw = np.asarray(kv_norm_w, f32)[0]

    w_krs = np.ascontiguousarray(np.concatenate([w_in[:, 4896:4928], w_in[:, 4864:4896]], axis=1))
    wq = w_q_up.reshape(512, 8, 192)
    w_qn = np.ascontiguousarray(wq[:, :, 0:128].reshape(512, 1024))
    w_qr = np.ascontiguousarray(wq[:, :, 128:192].reshape(512, 512))
    w_qrs = np.ascontiguousarray(np.concatenate([wq[:, :, 160:192], wq[:, :, 128:160]], axis=2).reshape(512, 512))
    wkv = w_kv_up.reshape(256, 8, 256)
    w_kn = np.ascontiguousarray(wkv[:, :, 0:128].reshape(256, 1024))
    w_kvv = np.ascontiguousarray(wkv[:, :, 128:256].reshape(256, 1024))
    pnw_b = np.ascontiguousarray(np.broadcast_to(pnw[None, :], (128, D)))
    ponw_b = np.ascontiguousarray(np.broadcast_to(ponw[None, :], (128, D)))

    inv_freq = (10000.0 ** (-np.arange(0, 64, 2, dtype=np.float32) / np.float32(64))).astype(f32)
    cst_s = np.zeros((128, 8), f32)
    cst_s[:, 0:4] = qnw.reshape(4, 128).T
    cst_s[:, 4:6] = kvnw.reshape(2, 128).T
    cst_s[0:64, 6] = np.concatenate([inv_freq, inv_freq])
    cst_s[0:64, 7] = np.concatenate([-np.ones(32, f32), np.ones(32, f32)])
    idx = np.arange(128)
    ident = np.eye(128, dtype=f32)
    negtri = -(idx[:, None] >= idx[None, :]).astype(f32)
    negones = -np.ones((128, 128), f32)
    ones = np.ones((128, 128), f32)

    xb = x.reshape(64, 128, D)
    pb = pos.reshape(64, 128)
    xT = np.ascontiguousarray(x.T)
    posa = np.ascontiguousarray(np.broadcast_to(pos[None, :], (64, SEQ))).astype(np.int32)
    pnw_c = np.ascontiguousarray(pnw.reshape(16, 128).T)
    in_maps = []
    for r in range(NCORES):
        msb = np.zeros((128, 8, 128), f32)
        mml = np.zeros((128, 8, 128), f32)
        for rp in range(8):
            if rp > r:
                msb[:, rp, :] = NEG
                mml[:, rp, :] = NEG
            elif rp == r:
                msb[:, rp, :] = np.where(idx[:, None] < idx[None, :], 0.0, NEG)
                mml[:, rp, :] = np.where((idx[:, None] // 64) <= (idx[None, :] // 64), 0.0, NEG)
        cst_bf = np.concatenate([ident, negtri, negones, ones, msb.reshape(128, 1024), mml.reshape(128, 1024)],
                                axis=1).astype(f32)
        in_maps.append({
            "x": np.ascontiguousarray(xb[r::8].reshape(TL, D)),
            "posb": np.ascontiguousarray(np.broadcast_to(pb[r::8].reshape(1, TL), (64, TL))).astype(np.int32),
            "xT": xT, "posa": posa, "pnw_c": pnw_c,
            "w_in": w_in, "w_krs": w_krs, "w_qn": w_qn, "w_qr": w_qr, "w_qrs": w_qrs,
            "w_kn": w_kn, "w_kvv": w_kvv, "w_out": w_out, "pnw_b": pnw_b, "ponw_b": ponw_b,
            "cst_s": cst_s, "cst_bf": np.ascontiguousarray(cst_bf),
        })
    return in_maps


_NC_CACHE = {}


def kernel(x, positions, pre_norm_w, w_in, q_norm_w, w_q_up, kv_norm_w, w_kv_up, w_out, post_norm_w):
    in_maps = _host_inputs(x, positions, pre_norm_w, w_in, q_norm_w, w_q_up, kv_norm_w, w_kv_up, w_out, post_norm_w)
    nc = build_program()
    res = run_bass_kernel_spmd(nc, in_maps, core_ids=list(range(NCORES)))
    out = np.zeros((64, 128, D), np.float32)
    for r in range(NCORES):
        out[r::8] = np.asarray(res.results[r]["out"], np.float32).reshape(8, 128, D)
    return out.reshape(1, SEQ, D)
```
